# Optimizing a Trainium2 kernel written in Bass

```python
import math
import jax, jax.numpy as jnp
from jax import lax
import numpy as np

D_MODEL = 4096
BATCH = 4
SEQ = 4096
DEPTH = 2

SSM_INNER = D_MODEL // 2
SSM_HEAD_DIM = 64
SSM_HEADS = SSM_INNER // SSM_HEAD_DIM
SSM_GROUPS = 4
SSM_STATE = 128
SSM_CONV = 4
SSM_CHUNK = 128
SSM_XBC = SSM_INNER + 2 * SSM_GROUPS * SSM_STATE
FOX_HEAD_DIM = 128
FOX_HEADS = (D_MODEL // 4) // FOX_HEAD_DIM
FOX_WIDTH = FOX_HEADS * FOX_HEAD_DIM
FOX_BLOCK = 128
GDN_HEAD_DIM = 128
GDN_HEADS = (D_MODEL // 4) // GDN_HEAD_DIM
GDN_WIDTH = GDN_HEADS * GDN_HEAD_DIM
GDN_CONV = 4
GDN_CHUNK = 64
D_FF = ((8 * D_MODEL // 3) + 255) // 256 * 256
FFN_CONV = 3
NORM_EPS = 1e-6
N_IN = (SSM_INNER + SSM_XBC + SSM_HEADS + 3 * FOX_WIDTH + FOX_HEADS
        + 3 * GDN_WIDTH + 2 * GDN_HEADS + GDN_WIDTH + 3 * D_MODEL)

kernel_name = 'hybrid_ssd_fox_gdn_block'


def _split_points():
    sizes = [SSM_INNER, SSM_XBC, SSM_HEADS, 3 * FOX_WIDTH, FOX_HEADS, 3 * GDN_WIDTH,
             GDN_HEADS, GDN_HEADS, GDN_WIDTH, D_MODEL, D_MODEL]
    return [int(s) for s in np.cumsum(sizes)]


def rmsnorm(x, w, groups=1):
    xf = x.astype(jnp.float32)
    shp = xf.shape
    xg = xf.reshape(shp[:-1] + (groups, shp[-1] // groups))
    xg = xg * lax.rsqrt(jnp.mean(xg * xg, axis=-1, keepdims=True) + NORM_EPS)
    return (xg.reshape(shp) * w).astype(x.dtype)


def l2norm(t):
    return t * lax.rsqrt(jnp.sum(t * t, axis=-1, keepdims=True) + 1e-6)


def causal_dwconv(u, w, bias=None):
    k_width = w.shape[0]
    seq = u.shape[1]
    up = jnp.pad(u, ((0, 0), (k_width - 1, 0), (0, 0)))
    y = up[:, 0:seq] * w[0]
    for j in range(1, k_width):
        y = y + up[:, j:j + seq] * w[j]
    if bias is not None:
        y = y + bias
    return y


def ssd_chunked(x, dt, a, bm, cm):
    bsz, seq, n_heads, p = x.shape
    g, n = bm.shape[2], bm.shape[3]
    hpg = n_heads // g
    nc, cl = seq // SSM_CHUNK, SSM_CHUNK
    xs = (x * dt[..., None]).reshape(bsz, nc, cl, g, hpg, p)
    a_cs = jnp.cumsum((dt * a).reshape(bsz, nc, cl, g, hpg), axis=2)
    bc = bm.reshape(bsz, nc, cl, g, n)
    cc = cm.reshape(bsz, nc, cl, g, n)
    causal = jnp.tril(jnp.ones((cl, cl), dtype=bool))[None, None, :, :, None, None]
    seg = a_cs[:, :, :, None] - a_cs[:, :, None, :]
    decay = jnp.exp(jnp.where(causal, seg, -jnp.inf))
    cb = jnp.einsum('bclgn,bcsgn->bclsg', cc, bc)
    y_diag = jnp.einsum('bclsg,bclsgi,bcsgip->bclgip', cb, decay, xs)
    decay_to_end = jnp.exp(a_cs[:, :, -1:] - a_cs)
    states = jnp.einsum('bclgn,bclgi,bclgip->bcgipn', bc, decay_to_end, xs)
    chunk_decay = jnp.exp(a_cs[:, :, -1])

    def step(h, inp):
        st, cd = inp
        return h * cd[..., None, None] + st, h

    h0 = jnp.zeros((bsz, g, hpg, p, n), x.dtype)
    _, h_in = lax.scan(step, h0, (jnp.moveaxis(states, 1, 0), jnp.moveaxis(chunk_decay, 1, 0)))
    h_in = jnp.moveaxis(h_in, 0, 1)
    y_off = jnp.einsum('bclgn,bcgipn,bclgi->bclgip', cc, h_in, jnp.exp(a_cs))
    return (y_diag + y_off).reshape(bsz, seq, n_heads, p)


def ssd_mixer(z, xbc, dt_raw, conv_w, conv_b, dt_bias, a_log, d_skip, norm_w):
    bsz, seq, _ = z.shape
    xbc = jax.nn.silu(causal_dwconv(xbc, conv_w, conv_b))
    xs, bm, cm = jnp.split(xbc, [SSM_INNER, SSM_INNER + SSM_GROUPS * SSM_STATE], axis=-1)
    xs = xs.reshape(bsz, seq, SSM_HEADS, SSM_HEAD_DIM).astype(jnp.float32)
    bm = bm.reshape(bsz, seq, SSM_GROUPS, SSM_STATE).astype(jnp.float32)
    cm = cm.reshape(bsz, seq, SSM_GROUPS, SSM_STATE).astype(jnp.float32)
    dt = jax.nn.softplus(dt_raw.astype(jnp.float32) + dt_bias)
    a = -jnp.exp(a_log.astype(jnp.float32))
    y = ssd_chunked(xs, dt, a, bm, cm) + xs * d_skip[:, None]
    y = y.reshape(bsz, seq, SSM_INNER) * jax.nn.silu(z.astype(jnp.float32))
    y = rmsnorm(y, norm_w, groups=SSM_GROUPS)
    return y.astype(z.dtype)


def forgetting_attention(q, k, v, log_f):
    bsz, seq, h, d = q.shape
    nb = seq // FOX_BLOCK
    cum_f = jnp.cumsum(log_f, axis=1)
    key_pos = jnp.arange(seq)
    k_bias = jnp.moveaxis(cum_f, 1, 2)[:, :, None, :]
    q_blocks = jnp.moveaxis(q.reshape(bsz, nb, FOX_BLOCK, h, d), 1, 0)
    f_blocks = jnp.moveaxis(cum_f.reshape(bsz, nb, FOX_BLOCK, h), 1, 0)
    vf = v.astype(jnp.float32)
    scale = d ** -0.5

    def attend(args):
        q_blk, f_blk, blk = args
        s = jnp.einsum('bqhd,bkhd->bhqk', q_blk, k).astype(jnp.float32) * scale
        s = s + jnp.moveaxis(f_blk, 1, 2)[..., None] - k_bias
        q_pos = blk * FOX_BLOCK + jnp.arange(FOX_BLOCK)
        s = jnp.where(key_pos[None, :] <= q_pos[:, None], s, -jnp.inf)
        p = jax.nn.softmax(s, axis=-1)
        return jnp.einsum('bhqk,bkhd->bqhd', p, vf)

    out = lax.map(attend, (q_blocks, f_blocks, jnp.arange(nb)))
    return jnp.moveaxis(out, 0, 1).reshape(bsz, seq, h * d)


def fox_mixer(qkv, f_raw, f_bias):
    bsz, seq, _ = qkv.shape
    q, k, v = [t.reshape(bsz, seq, FOX_HEADS, FOX_HEAD_DIM) for t in jnp.split(qkv, 3, axis=-1)]
    log_f = jax.nn.log_sigmoid(f_raw.astype(jnp.float32) + f_bias)
    return forgetting_attention(q, k, v, log_f).astype(qkv.dtype)


def gated_delta_chunked(q, k, v, g, beta):
    bsz, seq, h, dk = q.shape
    dv = v.shape[-1]
    cl = GDN_CHUNK
    nc = seq // cl

    def to_chunks(t):
        return jnp.moveaxis(t.reshape((bsz, nc, cl, h) + t.shape[3:]), 3, 1)

    q, k, v, g, beta = [to_chunks(t) for t in (q, k, v, g, beta)]
    gc = jnp.cumsum(g, axis=-1)
    lower = jnp.tril(jnp.ones((cl, cl), dtype=bool))
    strict = jnp.tril(jnp.ones((cl, cl), dtype=bool), -1)
    decay = jnp.exp(jnp.where(lower, gc[..., :, None] - gc[..., None, :], -jnp.inf))
    kb = k * beta[..., None]
    a_mat = jnp.where(strict, jnp.einsum('bhcld,bhcsd->bhcls', kb, k) * decay, 0.0)
    rhs = jnp.concatenate([v * beta[..., None], kb * jnp.exp(gc)[..., None]], axis=-1)
    sol = lax.linalg.triangular_solve(a_mat + jnp.eye(cl, dtype=a_mat.dtype), rhs,
                                      left_side=True, lower=True, unit_diagonal=True)
    u, w = sol[..., :dv], sol[..., dv:]
    attn = jnp.einsum('bhcld,bhcsd->bhcls', q, k) * decay
    q_dec = q * jnp.exp(gc)[..., None]
    k_dec = k * jnp.exp(gc[..., -1:] - gc)[..., None]
    chunk_decay = jnp.exp(gc[..., -1])

    def step(state, inp):
        q_c, k_c, u_c, w_c, attn_c, cd = inp
        v_new = u_c - jnp.einsum('bhld,bhde->bhle', w_c, state)
        o = jnp.einsum('bhld,bhde->bhle', q_c, state) + jnp.einsum('bhls,bhse->bhle', attn_c, v_new)
        state = state * cd[..., None, None] + jnp.einsum('bhld,bhle->bhde', k_c, v_new)
        return state, o

    xs = tuple(jnp.moveaxis(t, 2, 0) for t in (q_dec, k_dec, u, w, attn, chunk_decay))
    s0 = jnp.zeros((bsz, h, dk, dv), q.dtype)
    _, o = lax.scan(step, s0, xs)
    o = jnp.moveaxis(o, 0, 2)
    return jnp.transpose(o, (0, 2, 3, 1, 4)).reshape(bsz, seq, h, dv)


def gdn_mixer(qkv, a_raw, b_raw, z, conv_w, dt_bias, a_log, norm_w):
    bsz, seq, _ = qkv.shape
    qkv = jax.nn.silu(causal_dwconv(qkv, conv_w))
    q, k, v = [t.reshape(bsz, seq, GDN_HEADS, GDN_HEAD_DIM).astype(jnp.float32)
               for t in jnp.split(qkv, 3, axis=-1)]
    q = l2norm(q) * GDN_HEAD_DIM ** -0.5
    k = l2norm(k)
    beta = jax.nn.sigmoid(b_raw.astype(jnp.float32))
    g = -jnp.exp(a_log.astype(jnp.float32)) * jax.nn.softplus(a_raw.astype(jnp.float32) + dt_bias)
    o = gated_delta_chunked(q, k, v, g, beta)
    o = rmsnorm(o, norm_w) * jax.nn.silu(z.astype(jnp.float32).reshape(bsz, seq, GDN_HEADS, GDN_HEAD_DIM))
    return o.reshape(bsz, seq, GDN_WIDTH).astype(qkv.dtype)


def conv_ffn(h, w_up, conv_w, conv_b, w_down):
    u = causal_dwconv(h @ w_up, conv_w, conv_b)
    gate, val = jnp.split(u, 2, axis=-1)
    return (jax.nn.silu(gate) * val) @ w_down


def setup_inputs(seed: int = 0) -> dict:
    key = jax.random.key(seed)
    ks = jax.random.split(key, 24)
    f32 = jnp.float32

    def nrm(k, shape, scale):
        return jax.random.normal(k, shape, f32) * scale

    def gain(k, shape):
        return 1.0 + 0.01 * jax.random.normal(k, shape, f32)

    def dt_bias_init(k, shape):
        dt = jnp.exp(jax.random.uniform(k, shape, f32, math.log(1e-3), math.log(1e-1)))
        return dt + jnp.log(-jnp.expm1(-dt))

    def a_log_init(k, shape):
        return jnp.log(jax.random.uniform(k, shape, f32, 1.0, 16.0))

    return {
        'x': jax.random.normal(ks[0], (BATCH, SEQ, D_MODEL), f32),
        'norm_mix': gain(ks[1], (DEPTH, D_MODEL)),
        'w_in': nrm(ks[2], (DEPTH, D_MODEL, N_IN), D_MODEL ** -0.5),
        'ssm_conv_w': nrm(ks[3], (DEPTH, SSM_CONV, SSM_XBC), SSM_CONV ** -0.5),
        'ssm_conv_b': nrm(ks[4], (DEPTH, SSM_XBC), 0.01),
        'ssm_dt_bias': dt_bias_init(ks[5], (DEPTH, SSM_HEADS)),
        'ssm_a_log': a_log_init(ks[6], (DEPTH, SSM_HEADS)),
        'ssm_d': gain(ks[7], (DEPTH, SSM_HEADS)),
        'ssm_norm': gain(ks[8], (DEPTH, SSM_INNER)),
        'fox_f_bias': jax.random.uniform(ks[9], (DEPTH, FOX_HEADS), f32, 1.0, 4.0),
        'gdn_conv_w': nrm(ks[10], (DEPTH, GDN_CONV, 3 * GDN_WIDTH), GDN_CONV ** -0.5),
        'gdn_dt_bias': dt_bias_init(ks[11], (DEPTH, GDN_HEADS)),
        'gdn_a_log': a_log_init(ks[12], (DEPTH, GDN_HEADS)),
        'gdn_norm': gain(ks[13], (DEPTH, GDN_HEAD_DIM)),
        'w_br_ssm': nrm(ks[14], (DEPTH, SSM_INNER, D_MODEL), SSM_INNER ** -0.5),
        'w_br_fox': nrm(ks[15], (DEPTH, FOX_WIDTH, D_MODEL), FOX_WIDTH ** -0.5),
        'w_br_gdn': nrm(ks[16], (DEPTH, GDN_WIDTH, D_MODEL), GDN_WIDTH ** -0.5),
        'w_out': nrm(ks[17], (DEPTH, D_MODEL, D_MODEL), D_MODEL ** -0.5),
        'norm_ffn': gain(ks[18], (DEPTH, D_MODEL)),
        'w_up': nrm(ks[19], (DEPTH, D_MODEL, 2 * D_FF), D_MODEL ** -0.5),
        'ffn_conv_w': nrm(ks[20], (DEPTH, FFN_CONV, 2 * D_FF), FFN_CONV ** -0.5),
        'ffn_conv_b': nrm(ks[21], (DEPTH, 2 * D_FF), 0.01),
        'w_down': nrm(ks[22], (DEPTH, D_FF, D_MODEL), D_FF ** -0.5),
        'norm_final': gain(ks[23], (D_MODEL,)),
    }


def reference(x, norm_mix, w_in, ssm_conv_w, ssm_conv_b, ssm_dt_bias, ssm_a_log, ssm_d,
              ssm_norm, fox_f_bias, gdn_conv_w, gdn_dt_bias, gdn_a_log, gdn_norm,
              w_br_ssm, w_br_fox, w_br_gdn, w_out, norm_ffn, w_up, ffn_conv_w, ffn_conv_b,
              w_down, norm_final):
    points = _split_points()
    for layer in range(DEPTH):
        h = rmsnorm(x, norm_mix[layer])
        proj = h @ w_in[layer]
        (ssm_z, ssm_xbc, ssm_dt, fox_qkv, fox_f, gdn_qkv, gdn_a, gdn_b, gdn_z,
         gate_ssm, gate_fox, gate_gdn) = jnp.split(proj, points, axis=-1)
        y_ssm = ssd_mixer(ssm_z, ssm_xbc, ssm_dt, ssm_conv_w[layer], ssm_conv_b[layer],
                          ssm_dt_bias[layer], ssm_a_log[layer], ssm_d[layer], ssm_norm[layer])
        y_fox = fox_mixer(fox_qkv, fox_f, fox_f_bias[layer])
        y_gdn = gdn_mixer(gdn_qkv, gdn_a, gdn_b, gdn_z, gdn_conv_w[layer], gdn_dt_bias[layer],
                          gdn_a_log[layer], gdn_norm[layer])
        merged = (jax.nn.sigmoid(gate_ssm) * (y_ssm @ w_br_ssm[layer])
                  + jax.nn.sigmoid(gate_fox) * (y_fox @ w_br_fox[layer])
                  + jax.nn.sigmoid(gate_gdn) * (y_gdn @ w_br_gdn[layer]))
        x = x + merged @ w_out[layer]
        h = rmsnorm(x, norm_ffn[layer])
        x = x + conv_ffn(h, w_up[layer], ffn_conv_w[layer], ffn_conv_b[layer], w_down[layer])
    return rmsnorm(x, norm_final)
```

```python
import contextlib
import numpy as np
import ml_dtypes
import concourse.bass as bass
import concourse.mybir as mybir
from concourse.bass_utils import run_bass_kernel_spmd

F32 = mybir.dt.float32
BF16 = mybir.dt.bfloat16
AF = mybir.ActivationFunctionType
ALU = mybir.AluOpType
AX = mybir.AxisListType

NCORES = 8
ENGS = ("pe", "act", "dve", "pool", "sp")
NDSEM = {"sp": 12, "pool": 6, "act": 4}
SAME_ENG_SYNC = True


class Buf:
    __slots__ = ("w", "r")

    def __init__(self):
        self.w = None
        self.r = {}


class _Rec:
    def __init__(self):
        self.call = None

    def __getattr__(self, name):
        def f(*a, **k):
            self.call = (name, a, k)
        return f


class Prog:
    def __init__(self, nc):
        self.nc = nc
        self.streams = {e: [] for e in ENGS}
        self.cnt = {}
        self.semh = {}
        self.seen = {e: {} for e in ENGS}
        self.rr = {q: 0 for q in NDSEM}
        for e in ENGS:
            self._mksem(("e", e))
        for q, n in NDSEM.items():
            for i in range(n):
                self._mksem(("d", q, i))
        self._mksem(("cc",))

    def _mksem(self, key):
        self.semh[key] = self.nc.alloc_semaphore(name="s_" + "_".join(str(k) for k in key))
        self.cnt[key] = 0

    def _deps(self, reads, writes):
        deps = {}

        def add(tok):
            if tok is not None and deps.get(tok[0], 0) < tok[1]:
                deps[tok[0]] = tok[1]
        for b in reads:
            add(b.w)
        for b in writes:
            add(b.w)
            for k, v in b.r.items():
                add((k, v))
        return deps

    def _waits(self, eng, deps):
        waits = []
        own = ("e", eng)
        for k, v in deps.items():
            if k == own:
                if eng == "pe" or not SAME_ENG_SYNC or v > self.cnt[own]:
                    continue
            if self.seen[eng].get(k, 0) >= v:
                continue
            self.seen[eng][k] = v
            waits.append((k, v))
        return waits

    def _commit(self, tok, reads, writes):
        for b in reads:
            if b.r.get(tok[0], 0) < tok[1]:
                b.r[tok[0]] = tok[1]
        for b in writes:
            b.w = tok
            b.r = {}

    def op(self, eng, fn, reads=(), writes=(), sig=True, extra=()):
        rec = _Rec()
        fn(rec)
        name, a, k = rec.call
        fn = lambda e, name=name, a=a, k=k: getattr(e, name)(*a, **k)
        deps = self._deps(reads, writes)
        for tok in extra:
            if tok is not None and deps.get(tok[0], 0) < tok[1]:
                deps[tok[0]] = tok[1]
        waits = self._waits(eng, deps)
        key = ("e", eng)
        if sig:
            self.cnt[key] += 1
            tok = (key, self.cnt[key])
            self.streams[eng].append((waits, fn, (key, 1)))
        else:
            tok = (key, self.cnt[key] + 1)
            self.streams[eng].append((waits, fn, None))
        self._commit(tok, reads, writes)
        return tok

    def dma(self, q, out, in_, reads=(), writes=(), **kw):
        k = self.rr[q] % NDSEM[q]
        self.rr[q] += 1
        key = ("d", q, k)
        deps = self._deps(reads, writes)
        if self.cnt[key] > 0:
            deps[key] = self.cnt[key]
        waits = self._waits(q, deps)
        self.cnt[key] += 16
        tok = (key, self.cnt[key])
        self.streams[q].append((waits, lambda e: e.dma_start(out=out, in_=in_, **kw), (key, 16)))
        self._commit(tok, reads, writes)
        return tok

    def collective(self, kind, op, groups, in_ap, out_ap, reads=(), writes=()):
        key = ("cc",)
        deps = self._deps(reads, writes)
        waits = self._waits("pool", deps)
        self.cnt[key] += 1
        tok = (key, self.cnt[key])
        self.streams["pool"].append((waits, lambda e: e.collective_compute(
            kind, op, replica_groups=groups, ins=[in_ap], outs=[out_ap]), (key, 1)))
        self._commit(tok, reads, writes)
        return tok

    def barrier(self, cc=False):
        allk = {k: v for k, v in self.cnt.items() if v > 0 and (cc or k != ("cc",))}
        for e in ENGS:
            waits = self._waits(e, dict(allk))
            if waits:
                self.streams[e].append((waits, None, None))

    def replay(self, block):
        def run(stream):
            def body(e):
                for waits, fn, sig in stream:
                    for k, v in waits:
                        e.wait_ge(self.semh[k], v)
                    if fn is None:
                        continue
                    ins = fn(e)
                    if sig is not None:
                        ins.then_inc(self.semh[sig[0]], sig[1])
            return body
        block.tensor(run(self.streams["pe"]))
        block.scalar(run(self.streams["act"]))
        block.vector(run(self.streams["dve"]))
        block.gpsimd(run(self.streams["pool"]))
        block.sync(run(self.streams["sp"]))


class Arena:
    def __init__(self, t, ncols):
        self.t = t
        self.n = ncols
        self.o = 0

    def reset(self):
        self.o = 0

    def f32(self, cols, parts=128):
        assert self.o + cols <= self.n, (self.o, cols, self.n)
        ap = self.t[0:parts, self.o:self.o + cols]
        self.o += cols
        return ap

    def bf16(self, cols, parts=128):
        c32 = (cols + 1) // 2
        return self.f32(c32, parts).bitcast(BF16)[:, 0:cols]


G4 = [[0, 1, 2, 3], [4, 5, 6, 7]]
GX = [[0, 4], [1, 5], [2, 6], [3, 7]]


class CT:
    def __init__(self, nc, name, R, T, dtype, TC):
        TC = min(TC, T)
        assert T % TC == 0
        self.R, self.T, self.TC, self.NCH = R, T, TC, T // TC
        nparts = 1
        while R * T * 2 / nparts > 200e6:
            nparts *= 2
        assert self.NCH % nparts == 0
        self.cpp = self.NCH // nparts
        self.ts = [nc.dram_tensor(name if nparts == 1 else "%s_p%d" % (name, i), [self.cpp * R, TC], dtype)
                   for i in range(nparts)]
        self.t = self.ts[0]
        self.name = name
        self.bufs = [Buf() for _ in range(self.NCH)]

    def rows(self, ch, r0, n):
        t, c = self.ts[ch // self.cpp], ch % self.cpp
        return t[c * self.R + r0:c * self.R + r0 + n, :]

    def chunk(self, ch):
        return self.rows(ch, 0, self.R)

    def span(self, r0, rows, t0, tl):
        assert t0 % self.TC == 0 and tl % self.TC == 0
        ch0, n = t0 // self.TC, tl // self.TC
        t, c = self.ts[ch0 // self.cpp], ch0 % self.cpp
        assert c + n <= self.cpp
        v = t.ap().rearrange("(c r) t -> r c t", r=self.R)
        return v[r0:r0 + rows, c:c + n, :]

    def sview(self, ap2d):
        return ap2d.rearrange("p (c t) -> p c t", t=self.TC)


def tc_for(R):
    return 512 if R <= 512 else 128


RANK_ORDER = [0, 4, 1, 5, 2, 6, 3, 7]


def allgather8(P, *triples):
    LA = 3
    n = triples[0][0].NCH
    assert all(t[0].NCH == n for t in triples)

    def s1(ch):
        for loc, mid, full in triples:
            P.collective("AllGather", ALU.bypass, GX, loc.chunk(ch), mid.chunk(ch), reads=[loc.bufs[ch]], writes=[mid.bufs[ch]])

    def s2(ch):
        for loc, mid, full in triples:
            P.collective("AllGather", ALU.bypass, G4, mid.chunk(ch), full.chunk(ch), reads=[mid.bufs[ch]], writes=[full.bufs[ch]])
    for ch in range(min(LA, n)):
        s1(ch)
    for ch in range(n):
        if ch + LA < n:
            s1(ch + LA)
        s2(ch)


_DB = {}


def dbuf_of(t):
    if t.name not in _DB:
        _DB[t.name] = Buf()
    return _DB[t.name]


def allreduce8(P, src, mid, dst):
    P.collective("AllReduce", ALU.add, G4, src.ap().opt(), mid.ap().opt(), reads=[dbuf_of(src)], writes=[dbuf_of(mid)])
    P.collective("AllReduce", ALU.add, GX, mid.ap().opt(), dst.ap().opt(), reads=[dbuf_of(mid)], writes=[dbuf_of(dst)])


def cdiv(a, b):
    return (a + b - 1) // b


D = 4096
NIN = 24632
DFF = 11008
FFL = DFF // NCORES
EPS = 1e-6
PZ, PX, PB, PC = 0, 256, 512, 640
FQ, FK, FV = 768, 896, 1024
GQ, GK, GV, GZ = 1152, 1280, 1408, 1536
GA_SSM, GA_FOX, GA_GDN = 1664, 2176, 2688
PS = 3200
NPROJ = 3208
NPROJ_PAD = 3328
PV_NMIX, PV_SCW, PV_SCB, PV_SNORM, PV_SD, PV_GCW, PV_GNORM, PV_NFFN = 0, 4, 20, 24, 26, 28, 40, 41
PV_FCW, PV_FCB, PV_SBIAS, PV_SALOG, PV_SSGN = 45, 111, 133, 134, 135
NPV = 136
NUPW = 2816


class Cfg:
    def __init__(self, NB=4, SEQ=4096, DEPTH=2, stop_after=None, taps=(), lite=False):
        self.NB, self.SEQ, self.DEPTH = NB, SEQ, DEPTH
        self.lite = lite
        self.T = NB * SEQ
        self.stop_after = stop_after
        self.taps = tuple(taps)


def make_env(nc, st, ncols=45056):
    env = {}
    t = st.enter_context(nc.sbuf_tensor("arena", [128, ncols], F32))
    env["ar"] = Arena(t, ncols)
    env["psum"] = [st.enter_context(nc.psum_tensor("ps%d" % i, [128, 512], F32)) for i in range(8)]
    env["psb"] = [Buf() for _ in range(8)]
    return env


def make_consts(nc, st, P, env):
    c = {}
    cb = Buf()
    t = st.enter_context(nc.sbuf_tensor("consts", [128, 1024], F32))
    ones, ident, tri, tris = t[:, 0:128], t[:, 128:256], t[:, 256:384], t[:, 384:512]
    sel4 = t[0:8, 512:640]
    bfv = t[:, 640:1024].bitcast(BF16)
    ones_bf, tri_bf, ident_bf = bfv[:, 0:128], bfv[:, 128:256], bfv[:, 256:384]
    P.op("pool", lambda e: e.memset(ones, 1.0), writes=[cb])
    P.op("pool", lambda e: e.affine_select(ident, ones, pattern=[[-1, 128]], compare_op=ALU.is_equal,
                                           fill=0.0, base=0, channel_multiplier=1), reads=[cb], writes=[cb])
    P.op("pool", lambda e: e.affine_select(tri, ones, pattern=[[1, 128]], compare_op=ALU.is_ge,
                                           fill=0.0, base=0, channel_multiplier=-1), reads=[cb], writes=[cb])
    P.op("pool", lambda e: e.affine_select(tris, ones, pattern=[[1, 128]], compare_op=ALU.is_ge,
                                           fill=0.0, base=-1, channel_multiplier=-1), reads=[cb], writes=[cb])
    P.op("pool", lambda e: e.affine_select(sel4, ones[0:8, :], pattern=[[0, 128]], compare_op=ALU.is_equal,
                                           fill=0.0, base=-4, channel_multiplier=1), reads=[cb], writes=[cb])
    P.op("pool", lambda e: e.tensor_copy(ones_bf, ones), reads=[cb], writes=[cb])
    P.op("pool", lambda e: e.tensor_copy(tri_bf, tri), reads=[cb], writes=[cb])
    P.op("pool", lambda e: e.tensor_copy(ident_bf, ident), reads=[cb], writes=[cb])
    t2 = st.enter_context(nc.sbuf_tensor("consts2", [128, 128], F32))
    lows = t2[:, 0:128]
    P.op("pool", lambda e: e.affine_select(lows, ones, pattern=[[-1, 128]], compare_op=ALU.is_gt,
                                           fill=0.0, base=0, channel_multiplier=1), reads=[cb], writes=[cb])
    c.update(ones=ones, ident=ident, tri=tri, tris=tris, sel4=sel4, ones_bf=ones_bf, tri_bf=tri_bf,
             ident_bf=ident_bf, lows=lows, buf=cb)
    return c


def gemm(P, env, segs, N, T, epi, setup=None, NPW=512, TT=512, KG=8, pre=None):
    ar, psum, psb = env["ar"], env["psum"], env["psb"]
    ar.reset()
    nseg = len(segs)
    KCs = [K // 128 for (_, _, K) in segs]
    assert all(K % 128 == 0 for (_, _, K) in segs)
    CPP = NPW // 128
    assert nseg * CPP <= 8
    dbl = nseg * CPP <= 4
    NPASS = cdiv(N, NPW)
    wsbs = [ar.bf16(kc * NPW).rearrange("p (k n) -> p k n", n=NPW) for kc in KCs]
    wb = Buf()
    NAB = 3
    abufs = [(ar.bf16(KG * TT).rearrange("p (k t) -> p k t", t=TT), Buf()) for _ in range(NAB)]
    if setup is not None:
        setup(ar)
    ai = 0
    ti = 0
    for ps in range(NPASS):
        n0 = ps * NPW
        npw = min(NPW, N - n0)
        for s, (W, A, K) in enumerate(segs):
            for k0 in range(0, KCs[s], KG):
                kk = min(KG, KCs[s] - k0)
                P.dma("pool", wsbs[s][:, k0:k0 + kk, 0:npw],
                      W[k0 * 128:(k0 + kk) * 128, n0:n0 + npw].rearrange("(k p) n -> p k n", p=128),
                      writes=[wb])
        if ps == 0 and pre is not None:
            pre()
        nchunks = cdiv(npw, 128)
        for t0 in range(0, T, TT):
            tl = min(TT, T - t0)
            half = (ti % 2) * 4 if dbl else 0
            ti += 1
            for s, (W, A, K) in enumerate(segs):
                KC = KCs[s]
                for k0 in range(0, KC, KG):
                    kk = min(KG, KC - k0)
                    at, ab = abufs[ai % NAB]
                    ai += 1
                    for c0 in range(0, tl, A.TC):
                        ch = (t0 + c0) // A.TC
                        P.dma("sp", at[:, 0:kk, c0:c0 + A.TC],
                              A.rows(ch, k0 * 128, kk * 128).rearrange("(k p) t -> p k t", p=128),
                              reads=[A.bufs[ch]], writes=[ab])
                    for j in range(nchunks):
                        rows = min(128, npw - j * 128)
                        bk = half + s * CPP + j
                        for k in range(kk):
                            kc = k0 + k
                            P.op("pe", (lambda e, o=psum[bk][0:rows, 0:tl],
                                        l=wsbs[s][:, kc, j * 128:j * 128 + rows],
                                        r=at[:, k, 0:tl], st_=(kc == 0), sp_=(kc == KC - 1):
                                        e.matmul(o, lhsT=l, rhs=r, start=st_, stop=sp_)),
                                 reads=[wb, ab], writes=[psb[bk]], sig=(k == kk - 1))
            chunks = []
            for s in range(nseg):
                cs = []
                for j in range(nchunks):
                    rows = min(128, npw - j * 128)
                    bk = half + s * CPP + j
                    cs.append((n0 + j * 128, rows, psb[bk], psum[bk][0:rows, 0:tl]))
                chunks.append(cs)
            epi(n0, chunks, t0, tl)


def phase_norm(P, env, C, cfg, src, wcols, ssq_loc, ssq_mid, ssq_tot, h_loc=None, h_mid=None, hfull=None, out_f32=None):
    ar, psum, psb = env["ar"], env["psum"], env["psb"]
    T = cfg.T
    TT = 512
    ar.reset()
    xts = [(ar.f32(4 * TT), Buf()) for _ in range(2)]
    sqs = [(ar.f32(4 * TT), Buf()) for _ in range(2)]
    rows = [(ar.f32(TT, 1), Buf()) for _ in range(2)]
    bcs = [(ar.f32(TT), Buf()) for _ in range(2)]
    hts = [((ar.bf16(4 * TT) if out_f32 is None else ar.f32(4 * TT)), Buf()) for _ in range(2)]
    dbuf = Buf()
    for i, t0 in enumerate(range(0, T, TT)):
        tl = min(TT, T - t0)
        xt, xb = xts[i % 2]
        x3 = xt.rearrange("p (c t) -> p c t", t=TT)
        P.dma("sp", x3[:, :, 0:tl], src[:, t0:t0 + tl].rearrange("(c p) t -> p c t", p=128), writes=[xb])
        sq, sqb = sqs[i % 2]
        s3 = sq.rearrange("p (c t) -> p c t", t=TT)
        P.op("act", lambda e, o=s3[:, :, 0:tl], a=x3[:, :, 0:tl]: e.activation(o, a, AF.Square),
             reads=[xb], writes=[sqb])
        pb = psb[i % 2]
        for c in range(4):
            P.op("pe", lambda e, o=psum[i % 2][:, 0:tl], r=s3[:, c, 0:tl], c=c:
                 e.matmul(o, lhsT=C["ones"], rhs=r, start=(c == 0), stop=(c == 3)),
                 reads=[sqb, C["buf"]], writes=[pb], sig=(c == 3))
        rw, rwb = rows[i % 2]
        P.op("dve", lambda e, o=rw[0:1, 0:tl], a=psum[i % 2][0:1, 0:tl]: e.tensor_copy(o, a),
             reads=[pb], writes=[rwb])
        P.dma("sp", ssq_loc[0:1, t0:t0 + tl], rw[0:1, 0:tl], reads=[rwb], writes=[dbuf, dbuf_of(ssq_loc)])
    P.barrier()
    allreduce8(P, ssq_loc, ssq_mid, ssq_tot)
    for i, t0 in enumerate(range(0, T, TT)):
        tl = min(TT, T - t0)
        xt, xb = xts[i % 2]
        x3 = xt.rearrange("p (c t) -> p c t", t=TT)
        P.dma("sp", x3[:, :, 0:tl], src[:, t0:t0 + tl].rearrange("(c p) t -> p c t", p=128), writes=[xb])
        bc, bcb = bcs[i % 2]
        P.dma("sp", bc[:, 0:tl], ssq_tot[0:1, t0:t0 + tl].partition_broadcast(128), reads=[dbuf_of(ssq_tot)], writes=[bcb])
        P.op("dve", lambda e, a=bc[:, 0:tl]: e.tensor_scalar(a, a, 1.0 / D, EPS, ALU.mult, ALU.add),
             reads=[bcb], writes=[bcb])
        P.op("act", lambda e, a=bc[:, 0:tl]: e.activation(a, a, AF.Sqrt), reads=[bcb], writes=[bcb])
        P.op("dve", lambda e, a=bc[:, 0:tl]: e.reciprocal(a, a), reads=[bcb], writes=[bcb])
        ht, hb = hts[i % 2]
        h3 = ht.rearrange("p (c t) -> p c t", t=TT)
        for c in range(4):
            P.op("dve", lambda e, o=h3[:, c, 0:tl], a=x3[:, c, 0:tl], w=wcols[:, c:c + 1], b=bc[:, 0:tl]:
                 e.scalar_tensor_tensor(o, a, w, b, ALU.mult, ALU.mult), reads=[xb, bcb], writes=[hb])
        if out_f32 is not None:
            P.dma("sp", out_f32[:, t0:t0 + tl].rearrange("(c p) t -> p c t", p=128), h3[:, :, 0:tl],
                  reads=[hb], writes=[dbuf])
        else:
            for c in range(4):
                P.dma("sp", h_loc.span(c * 128, 128, t0, tl), h_loc.sview(h3[:, c, 0:tl]), reads=[hb], writes=[dbuf])
    P.barrier()


def copy_epilogue(P, dst, nbuf=4):
    st = {}

    def setup(ar):
        st["bufs"] = [(ar.f32(512), Buf()) for _ in range(nbuf)]
        st["i"] = 0
        st["d"] = Buf()

    def epi(n0, chunks, t0, tl):
        for (r0, rows, pb, pap) in chunks[0]:
            ot, otb = st["bufs"][st["i"] % nbuf]
            eng = "act" if st["i"] % 2 else "dve"
            st["i"] += 1
            if eng == "act":
                P.op("act", lambda e, o=ot[0:rows, 0:tl], a=pap: e.activation(o, a, AF.Copy),
                     reads=[pb], writes=[otb])
            else:
                P.op("dve", lambda e, o=ot[0:rows, 0:tl], a=pap: e.tensor_copy(o, a), reads=[pb], writes=[otb])
            P.dma("sp", dst[r0:r0 + rows, t0:t0 + tl], ot[0:rows, 0:tl], reads=[otb], writes=[st["d"]])
    return setup, epi


def phase_scal(P, env, C, cfg, proj, pv, scal):
    ar = env["ar"]
    ar.reset()
    L = cfg.SEQ
    mul = ar.f32(1, 8)
    mb = Buf()
    P.op("act", lambda e: e.activation(mul, pv[0:8, PV_SALOG:PV_SALOG + 1], AF.Exp), writes=[mb])
    P.op("dve", lambda e: e.tensor_scalar(mul, mul, -1.0, None, ALU.mult), reads=[mb], writes=[mb])
    ones8 = ar.f32(L, 8)
    ob = Buf()
    P.op("pool", lambda e: e.memset(ones8, 1.0), writes=[ob])
    raw, t, a, sp, ov, sg, cu = [ar.f32(L, 8) for _ in range(7)]
    bs = [Buf() for _ in range(7)]
    rb, tb, ab, spb, ovb, sgb, cub = bs
    dbuf = Buf()
    for b in range(cfg.NB):
        sl = slice(b * L, (b + 1) * L)
        P.dma("sp", raw, proj[PS:PS + 8, sl], writes=[rb])
        P.op("dve", lambda e: e.tensor_scalar(t, raw, pv[0:8, PV_SBIAS:PV_SBIAS + 1],
                                              pv[0:8, PV_SSGN:PV_SSGN + 1], ALU.add, ALU.mult),
             reads=[rb], writes=[tb])
        P.op("act", lambda e: e.activation(a, t, AF.Abs), reads=[tb], writes=[ab])
        P.op("act", lambda e: e.activation(a, a, AF.Exp, scale=-1.0), reads=[ab], writes=[ab])
        P.op("act", lambda e: e.activation(a, a, AF.Ln, bias=1.0), reads=[ab], writes=[ab])
        P.op("dve", lambda e: e.scalar_tensor_tensor(sp, t, 0.0, a, ALU.max, ALU.add),
             reads=[tb, ab], writes=[spb])
        P.op("dve", lambda e: e.tensor_scalar(ov, sp, mul, None, ALU.mult), reads=[spb, mb], writes=[ovb])
        P.op("act", lambda e: e.activation(sg, raw, AF.Sigmoid), reads=[rb], writes=[sgb])
        P.op("dve", lambda e: e.tensor_tensor_scan(cu, ones8, ov, 0.0, ALU.mult, ALU.add),
             reads=[ovb, ob], writes=[cub])
        for i, (src, sb) in enumerate(((sp, spb), (ov, ovb), (sg, sgb), (cu, cub))):
            P.dma("sp", scal[8 * i:8 * i + 8, sl], src, reads=[sb], writes=[dbuf])
    P.barrier()


def phase_fox(P, env, C, cfg, proj, scal, y_fox):
    ar, psum, psb = env["ar"], env["psum"], env["psb"]
    ar.reset()
    L = cfg.SEQ
    NBK = L // 128
    qkv = ar.f32(3 * L)
    qkvb = Buf()
    q3 = qkv.rearrange("p (c t) -> p c t", t=L)
    qs = ar.bf16(L)
    kb_ = ar.bf16(L)
    qsb, kbb = Buf(), Buf()
    vt = ar.bf16(L)
    vtb = Buf()
    cum = ar.f32(L, 8)
    cumb = Buf()
    cumT = ar.f32(NBK * 8)
    cumTb = Buf()
    c0 = ar.f32(NBK)
    c0b = Buf()
    bias = ar.f32(NBK * NBK)
    biasb = Buf()
    ybuf = ar.bf16(L)
    yb = Buf()
    NE = 4
    es = [(ar.bf16(128), Buf()) for _ in range(NE)]
    rden = [(ar.f32(128), Buf()) for _ in range(2)]
    dbuf = Buf()
    scale = 128.0 ** -0.5
    cb = C["buf"]
    ei = 0
    for b in range(cfg.NB):
        sl = slice(b * L, (b + 1) * L)
        P.dma("sp", q3, proj[FQ:FQ + 384, sl].rearrange("(c p) t -> p c t", p=128), writes=[qkvb])
        P.dma("sp", cum, scal[24:32, sl], writes=[cumb])
        P.op("act", lambda e: e.activation(qs, q3[:, 0, :], AF.Copy, scale=scale), reads=[qkvb], writes=[qsb])
        P.op("dve", lambda e: e.tensor_copy(kb_, q3[:, 1, :]), reads=[qkvb], writes=[kbb])
        for g0 in range(0, NBK, 4):
            bk = (g0 // 4) % 4
            gn = min(4, NBK - g0)
            for g in range(gn):
                blk = g0 + g
                P.op("pe", lambda e, o=psum[bk][:, g * 128:(g + 1) * 128], a=q3[:, 2, blk * 128:(blk + 1) * 128]:
                     e.transpose(o, a, C["ident"]), reads=[qkvb, cb], writes=[psb[bk]], sig=(g == gn - 1))
            eng = "act" if (g0 // 4) % 2 else "dve"
            if eng == "act":
                P.op("act", lambda e, o=vt[:, g0 * 128:(g0 + gn) * 128], a=psum[bk][:, 0:gn * 128]:
                     e.activation(o, a, AF.Copy), reads=[psb[bk]], writes=[vtb])
            else:
                P.op("dve", lambda e, o=vt[:, g0 * 128:(g0 + gn) * 128], a=psum[bk][:, 0:gn * 128]:
                     e.tensor_copy(o, a), reads=[psb[bk]], writes=[vtb])
        for blk in range(NBK):
            P.op("pe", lambda e, o=psum[4][:, blk * 8:(blk + 1) * 8], a=cum[0:8, blk * 128:(blk + 1) * 128]:
                 e.transpose(o, a, C["ident"][0:8, 0:8]), reads=[cumb, cb], writes=[psb[4]], sig=(blk == NBK - 1))
        P.op("dve", lambda e: e.tensor_copy(cumT, psum[4][:, 0:NBK * 8]), reads=[psb[4]], writes=[cumTb])
        P.op("pe", lambda e: e.matmul(psum[5][:, 0:NBK], lhsT=C["sel4"],
                                      rhs=cum.rearrange("p (b s) -> p b s", s=128)[:, :, 64],
                                      start=True, stop=True), reads=[cumb, cb], writes=[psb[5]])
        P.op("dve", lambda e: e.tensor_copy(c0, psum[5][:, 0:NBK]), reads=[psb[5]], writes=[c0b])
        cumT3 = cumT.rearrange("p (b s) -> p b s", s=8)
        for j in range(NBK):
            P.op("dve", lambda e, o=bias[:, j * NBK:(j + 1) * NBK], s1=c0[:, j:j + 1]:
                 e.tensor_scalar(o, cumT3[:, :, 4], s1, -1.0, ALU.subtract, ALU.mult),
                 reads=[cumTb, c0b], writes=[biasb])
        for j in range(NBK):
            ob_, db_ = 4 + (j % 2), 6 + (j % 2)
            for i in range(j + 1):
                sbk = ei % 4
                et, eb = es[ei % NE]
                ei += 1
                P.op("pe", lambda e, o=psum[sbk][:, 0:128], l=kb_[:, i * 128:(i + 1) * 128],
                     r=qs[:, j * 128:(j + 1) * 128]: e.matmul(o, lhsT=l, rhs=r, start=True, stop=True),
                     reads=[kbb, qsb], writes=[psb[sbk]])
                P.op("act", lambda e, o=et, a=psum[sbk][:, 0:128], bi=bias[:, j * NBK + i:j * NBK + i + 1]:
                     e.activation(o, a, AF.Exp, bias=bi), reads=[psb[sbk], biasb], writes=[eb])
                if i == j:
                    P.op("pool", lambda e, o=et: e.tensor_tensor(o, o, C["tri_bf"], ALU.mult),
                         reads=[eb, cb], writes=[eb])
                P.op("pe", lambda e, o=psum[ob_][:, 0:128], l=vt[:, i * 128:(i + 1) * 128], r=et, i=i, j=j:
                     e.matmul(o, lhsT=l, rhs=r, start=(i == 0), stop=(i == j)),
                     reads=[vtb, eb], writes=[psb[ob_]], sig=False)
                P.op("pe", lambda e, o=psum[db_][:, 0:128], r=et, i=i, j=j:
                     e.matmul(o, lhsT=C["ones_bf"], rhs=r, start=(i == 0), stop=(i == j)),
                     reads=[cb, eb], writes=[psb[db_]])
            rd, rdb = rden[j % 2]
            P.op("dve", lambda e, o=rd, a=psum[db_][:, 0:128]: e.reciprocal(o, a), reads=[psb[db_]], writes=[rdb])
            P.op("dve", lambda e, o=ybuf[:, j * 128:(j + 1) * 128], a=psum[ob_][:, 0:128], r_=rd:
                 e.tensor_tensor(o, a, r_, ALU.mult), reads=[psb[ob_], rdb], writes=[yb])
        P.dma("sp", y_fox.span(0, 128, b * L, L), y_fox.sview(ybuf), reads=[yb], writes=[dbuf])
    P.barrier()


def conv_silu(P, out3, raw3, nch, SEG, KW, pv, wcol0, bcol0, rawb, outb, out_dt_bf_from=None, outbf3=None, outbfb=None):
    for c in range(nch):
        w = lambda j, c=c: pv[:, wcol0 + c * KW + j:wcol0 + c * KW + j + 1]
        if bcol0 is not None:
            P.op("dve", lambda e, c=c: e.tensor_scalar(out3[:, c, :], raw3[:, c, KW - 1:KW - 1 + SEG], w(KW - 1),
                                                       pv[:, bcol0 + c:bcol0 + c + 1], ALU.mult, ALU.add),
                 reads=[rawb], writes=[outb])
        else:
            P.op("dve", lambda e, c=c: e.tensor_scalar(out3[:, c, :], raw3[:, c, KW - 1:KW - 1 + SEG], w(KW - 1),
                                                       None, ALU.mult), reads=[rawb], writes=[outb])
        for j in range(KW - 1):
            P.op("dve", lambda e, c=c, j=j: e.scalar_tensor_tensor(out3[:, c, :], raw3[:, c, j:j + SEG], w(j),
                                                                   out3[:, c, :], ALU.mult, ALU.add),
                 reads=[rawb, outb], writes=[outb])
        P.op("act", lambda e, c=c: e.activation(out3[:, c, :], out3[:, c, :], AF.Silu), reads=[outb], writes=[outb])


def load_with_halo(P, raw3, rawb, src_rows, t0, SEG, HALO, seq_start):
    if seq_start:
        P.op("pool", lambda e: e.memset(raw3[:, :, 0:HALO], 0.0), writes=[rawb])
        P.dma("sp", raw3[:, :, HALO:HALO + SEG], src_rows[:, t0:t0 + SEG].rearrange("(c p) t -> p c t", p=128),
              writes=[rawb])
    else:
        P.dma("sp", raw3[:, :, 0:HALO + SEG],
              src_rows[:, t0 - HALO:t0 + SEG].rearrange("(c p) t -> p c t", p=128), writes=[rawb])


def phase_ssd(P, env, C, cfg, proj, scal, pv, ypre, ssq_loc, ssq_tot, y_ssd, cc=True):
    ar, psum, psb = env["ar"], env["psum"], env["psb"]
    ar.reset()
    L = cfg.SEQ
    SEG = min(1024, L)
    NCH = SEG // 128
    cb = C["buf"]
    raw = ar.f32(4 * (SEG + 3))
    raw3 = raw.rearrange("p (c t) -> p c t", t=SEG + 3)
    rawb = Buf()
    xc = ar.f32(4 * SEG)
    xc3 = xc.rearrange("p (c t) -> p c t", t=SEG)
    xcb = Buf()
    bcbf = ar.bf16(2 * SEG)
    bc3 = bcbf.rearrange("p (c t) -> p c t", t=SEG)
    bcb = Buf()
    zs = ar.f32(2 * SEG)
    zs3 = zs.rearrange("p (c t) -> p c t", t=SEG)
    zsb = Buf()
    scs = ar.f32(2 * SEG, 8)
    scs3 = scs.rearrange("p (c t) -> p c t", t=SEG)
    scsb = Buf()
    hst = ar.f32(256)
    hst3 = hst.rearrange("p (h d) -> p h d", d=64)
    hstb = Buf()
    hbf = ar.bf16(256)
    hbfb = Buf()
    yseg = ar.f32(2 * SEG)
    yseg3 = yseg.rearrange("p (c t) -> p c t", t=SEG)
    ysegb = Buf()
    rowseg = ar.f32(SEG, 1)
    rowb = Buf()
    dts = ar.f32(16)
    dtsb = Buf()
    acs = ar.f32(4)
    acsb = Buf()
    lbc = ar.f32(512)
    lbcb = Buf()
    dec = ar.f32(512)
    dec3 = dec.rearrange("p (h t) -> p h t", t=128)
    decb = Buf()
    cbm = ar.f32(128)
    cbmb = Buf()
    mt = ar.bf16(512)
    mt3 = mt.rearrange("p (h t) -> p h t", t=128)
    mtb = Buf()
    ebc = ar.f32(512)
    ebc3 = ebc.rearrange("p (h t) -> p h t", t=128)
    ebcb = Buf()
    ce = ar.bf16(512)
    ce3 = ce.rearrange("p (h t) -> p h t", t=128)
    ceb = Buf()
    xdt = ar.bf16(256)
    xdt3 = xdt.rearrange("p (h d) -> p h d", d=64)
    xdtb = Buf()
    btok = ar.bf16(128)
    btokb = Buf()
    dd = ar.f32(8)
    ddb = Buf()
    xdec = ar.bf16(256)
    xdec3 = xdec.rearrange("p (h d) -> p h d", d=64)
    xdecb = Buf()
    sq = ar.f32(256)
    sqb = Buf()
    dbuf = Buf()
    ident8 = C["ident"][0:8, 0:8]
    for b in range(cfg.NB):
        P.op("pool", lambda e: e.memset(hst, 0.0), writes=[hstb])
        P.op("pool", lambda e: e.memset(hbf, 0.0), writes=[hbfb])
        for s0 in range(0, L, SEG):
            t0 = b * L + s0
            load_with_halo(P, raw3, rawb, proj[PX:PX + 512, :], t0, SEG, 3, s0 == 0)
            P.dma("sp", zs3, proj[PZ:PZ + 256, t0:t0 + SEG].rearrange("(c p) t -> p c t", p=128), writes=[zsb])
            P.dma("sp", scs3[:, 0, :], scal[0:8, t0:t0 + SEG], writes=[scsb])
            P.dma("sp", scs3[:, 1, :], scal[8:16, t0:t0 + SEG], writes=[scsb])
            conv_silu(P, xc3, raw3, 4, SEG, 4, pv, PV_SCW, PV_SCB, rawb, xcb)
            P.op("pool", lambda e: e.tensor_copy(bc3, xc3[:, 2:4, :]), reads=[xcb], writes=[bcb])
            P.op("act", lambda e: e.activation(zs, zs, AF.Silu), reads=[zsb], writes=[zsb])
            for ch in range(NCH):
                o = ch * 128
                osl = slice(o, o + 128)
                P.op("pe", lambda e: e.transpose(psum[0][:, 0:8], scs3[:, 0, osl], ident8),
                     reads=[scsb, cb], writes=[psb[0]], sig=False)
                P.op("pe", lambda e: e.transpose(psum[0][:, 8:16], scs3[:, 1, osl], ident8),
                     reads=[scsb, cb], writes=[psb[0]])
                P.op("dve", lambda e: e.tensor_copy(dts, psum[0][:, 0:16]), reads=[psb[0]], writes=[dtsb])
                P.op("pe", lambda e: e.matmul(psum[0][:, 16:20], lhsT=C["tri"], rhs=dts[:, 8:12], start=True, stop=True),
                     reads=[dtsb, cb], writes=[psb[0]])
                P.op("dve", lambda e: e.tensor_copy(acs, psum[0][:, 16:20]), reads=[psb[0]], writes=[acsb])
                for h in range(4):
                    P.op("act", lambda e, h=h: e.activation(lbc[:, h * 128:(h + 1) * 128], C["ones"], AF.Copy,
                                                            scale=dts[:, 8 + h:9 + h]),
                         reads=[dtsb, cb], writes=[lbcb])
                for h in range(4):
                    P.op("pe", lambda e, h=h: e.matmul(psum[2][:, h * 128:(h + 1) * 128],
                                                       lhsT=lbc[:, h * 128:(h + 1) * 128], rhs=C["tri"],
                                                       start=True, stop=True),
                         reads=[lbcb, cb], writes=[psb[2]], sig=(h == 3))
                for h in range(4):
                    P.op("dve", lambda e, h=h: e.tensor_scalar(dec[:, h * 128:(h + 1) * 128],
                                                               psum[2][:, h * 128:(h + 1) * 128],
                                                               acs[:, h:h + 1], 0.0, ALU.subtract, ALU.min),
                         reads=[psb[2], acsb], writes=[decb])
                P.op("act", lambda e: e.activation(dec, dec, AF.Exp), reads=[decb], writes=[decb])
                P.op("pe", lambda e: e.matmul(psum[3][:, 0:128], lhsT=bc3[:, 0, osl], rhs=bc3[:, 1, osl],
                                              start=True, stop=True), reads=[bcb], writes=[psb[3]])
                P.op("dve", lambda e: e.tensor_tensor(cbm, psum[3][:, 0:128], C["tri"], ALU.mult),
                     reads=[psb[3], cb], writes=[cbmb])
                P.op("dve", lambda e: e.tensor_tensor(mt3, dec3, cbm.unsqueeze(1).to_broadcast([128, 4, 128]), ALU.mult),
                     reads=[decb, cbmb], writes=[mtb])
                P.op("act", lambda e: e.activation(ebc, psum[2][:, 0:512], AF.Exp), reads=[psb[2]], writes=[ebcb])
                P.op("pool", lambda e: e.tensor_tensor(ce3, ebc3, bc3[:, 1, osl].unsqueeze(1).to_broadcast([128, 4, 128]),
                                                       ALU.mult), reads=[ebcb, bcb], writes=[ceb])
                for k in range(3):
                    P.op("pe", lambda e, k=k: e.transpose(psum[1][:, k * 128:(k + 1) * 128], xc3[:, k, osl], C["ident"]),
                         reads=[xcb, cb], writes=[psb[1]], sig=(k == 2))
                P.op("dve", lambda e: e.tensor_tensor(xdt3, psum[1][:, 0:256].rearrange("p (h d) -> p h d", d=64),
                                                      dts[:, 0:4].unsqueeze(2).to_broadcast([128, 4, 64]), ALU.mult),
                     reads=[psb[1], dtsb], writes=[xdtb])
                P.op("act", lambda e: e.activation(btok, psum[1][:, 256:384], AF.Copy), reads=[psb[1]], writes=[btokb])
                ab3 = psum[2][:, 0:512].rearrange("p (h t) -> p h t", t=128)
                P.op("dve", lambda e: e.tensor_tensor(dd[:, 0:4], ab3[:, :, 127], acs, ALU.subtract),
                     reads=[psb[2], acsb], writes=[ddb])
                P.op("dve", lambda e: e.tensor_copy(dd[:, 4:8], ab3[:, :, 127]), reads=[psb[2]], writes=[ddb])
                P.op("act", lambda e: e.activation(dd, dd, AF.Exp), reads=[ddb], writes=[ddb])
                P.op("pool", lambda e: e.tensor_tensor(xdec3, xdt3, dd[:, 0:4].unsqueeze(2).to_broadcast([128, 4, 64]),
                                                       ALU.mult), reads=[xdtb, ddb], writes=[xdecb])
                for h in range(4):
                    po = psum[4][(h % 2) * 64:(h % 2) * 64 + 64, (h // 2) * 128:(h // 2 + 1) * 128]
                    P.op("pe", lambda e, h=h, po=po: e.matmul(po, lhsT=xdt[:, h * 64:(h + 1) * 64], rhs=mt3[:, h, :],
                                                              start=True, stop=False),
                         reads=[xdtb, mtb], writes=[psb[4]], sig=False)
                    P.op("pe", lambda e, h=h, po=po: e.matmul(po, lhsT=hbf[:, h * 64:(h + 1) * 64], rhs=ce3[:, h, :],
                                                              start=False, stop=True),
                         reads=[hbfb, ceb], writes=[psb[4]], sig=(h == 3))
                P.op("pe", lambda e: e.matmul(psum[5][:, 0:256], lhsT=btok, rhs=xdec, start=True, stop=True),
                     reads=[btokb, xdecb], writes=[psb[5]])
                P.op("dve", lambda e: e.tensor_tensor(hst3, hst3, dd[:, 4:8].unsqueeze(2).to_broadcast([128, 4, 64]),
                                                      ALU.mult), reads=[ddb, hstb], writes=[hstb])
                P.op("dve", lambda e: e.tensor_tensor(hst, hst, psum[5][:, 0:256], ALU.add),
                     reads=[psb[5], hstb], writes=[hstb])
                P.op("act", lambda e: e.activation(hbf, hst, AF.Copy), reads=[hstb], writes=[hbfb])
                for k in range(2):
                    P.op("dve", lambda e, k=k: e.scalar_tensor_tensor(yseg3[:, k, osl], xc3[:, k, osl],
                                                                      pv[:, PV_SD + k:PV_SD + k + 1],
                                                                      psum[4][:, k * 128:(k + 1) * 128], ALU.mult, ALU.add),
                         reads=[xcb, psb[4]], writes=[ysegb])
                    P.op("pool", lambda e, k=k: e.tensor_tensor(yseg3[:, k, osl], yseg3[:, k, osl], zs3[:, k, osl], ALU.mult),
                         reads=[ysegb, zsb], writes=[ysegb])
                    P.op("act", lambda e, k=k: e.activation(sq[:, k * 128:(k + 1) * 128], yseg3[:, k, osl], AF.Square),
                         reads=[ysegb], writes=[sqb])
                for k in range(2):
                    P.op("pe", lambda e, k=k: e.matmul(psum[6][:, 0:128], lhsT=C["ones"], rhs=sq[:, k * 128:(k + 1) * 128],
                                                       start=(k == 0), stop=(k == 1)),
                         reads=[sqb, cb], writes=[psb[6]], sig=(k == 1))
                P.op("dve", lambda e: e.tensor_copy(rowseg[0:1, osl], psum[6][0:1, 0:128]), reads=[psb[6]], writes=[rowb])
            P.dma("sp", ypre[0:256, t0:t0 + SEG].rearrange("(c p) t -> p c t", p=128), yseg3, reads=[ysegb], writes=[dbuf])
            P.dma("sp", ssq_loc[0:1, t0:t0 + SEG], rowseg[0:1, :], reads=[rowb], writes=[dbuf, dbuf_of(ssq_loc)])
    P.barrier()
    if cc:
        P.collective("AllReduce", ALU.add, [[2 * i, 2 * i + 1] for i in range(NCORES // 2)],
                     ssq_loc.ap().opt(), ssq_tot.ap().opt(), reads=[dbuf_of(ssq_loc)], writes=[dbuf_of(ssq_tot)])
    else:
        P.dma("sp", ssq_tot.ap(), ssq_loc.ap(), reads=[dbuf_of(ssq_loc)], writes=[dbuf_of(ssq_tot)])
    ar.reset()
    TT = 512
    yts = [(ar.f32(2 * TT), Buf()) for _ in range(2)]
    bcs = [(ar.f32(TT), Buf()) for _ in range(2)]
    ots = [(ar.bf16(2 * TT), Buf()) for _ in range(2)]
    for i, t0 in enumerate(range(0, cfg.T, TT)):
        yt, ytb = yts[i % 2]
        y3 = yt.rearrange("p (c t) -> p c t", t=TT)
        P.dma("sp", y3, ypre[0:256, t0:t0 + TT].rearrange("(c p) t -> p c t", p=128), writes=[ytb])
        bc, bcb_ = bcs[i % 2]
        P.dma("sp", bc, ssq_tot[0:1, t0:t0 + TT].partition_broadcast(128), reads=[dbuf_of(ssq_tot)], writes=[bcb_])
        P.op("dve", lambda e, a=bc: e.tensor_scalar(a, a, 1.0 / 512.0, EPS, ALU.mult, ALU.add), reads=[bcb_], writes=[bcb_])
        P.op("act", lambda e, a=bc: e.activation(a, a, AF.Sqrt), reads=[bcb_], writes=[bcb_])
        P.op("dve", lambda e, a=bc: e.reciprocal(a, a), reads=[bcb_], writes=[bcb_])
        ot, otb = ots[i % 2]
        o3 = ot.rearrange("p (c t) -> p c t", t=TT)
        for k in range(2):
            P.op("dve", lambda e, k=k, o3=o3, y3=y3, bc=bc: e.scalar_tensor_tensor(
                o3[:, k, :], y3[:, k, :], pv[:, PV_SNORM + k:PV_SNORM + k + 1], bc, ALU.mult, ALU.mult),
                reads=[ytb, bcb_], writes=[otb])
        for k in range(2):
            P.dma("sp", y_ssd.span(k * 128, 128, t0, TT), y_ssd.sview(o3[:, k, :]), reads=[otb], writes=[dbuf])
    P.barrier()


def phase_gdn(P, env, C, cfg, proj, scal, pv, y_gdn):
    ar, psum, psb = env["ar"], env["psum"], env["psb"]
    ar.reset()
    L = cfg.SEQ
    SEG = min(1024, L)
    NCH = SEG // 64
    cb = C["buf"]
    ident, ones, tri, lows = C["ident"], C["ones"], C["tri"], C["lows"]
    i64 = ident[0:64, 0:64]
    raw = ar.f32(3 * (SEG + 3))
    raw3 = raw.rearrange("p (c t) -> p c t", t=SEG + 3)
    rawb = Buf()
    qkv = ar.f32(3 * SEG)
    qkv3 = qkv.rearrange("p (c t) -> p c t", t=SEG)
    qkvb = Buf()
    zs = ar.f32(SEG)
    zsb = Buf()
    scs = ar.f32(2 * SEG, 8)
    scs3 = scs.rearrange("p (c t) -> p c t", t=SEG)
    scsb = Buf()
    sq = ar.f32(512)
    sqb = Buf()
    rs = ar.f32(512)
    rsb = Buf()
    oseg = ar.f32(SEG)
    osegb = Buf()
    yo = ar.bf16(SEG)
    yob = Buf()
    S = ar.f32(128)
    Sb = Buf()
    gb = ar.f32(24, 64)
    gbb = Buf()
    gl = ar.f32(128, 64)
    glb = Buf()
    gct = ar.f32(2, 64)
    gctb = Buf()
    d1 = ar.f32(64, 64)
    d1b = Buf()
    d2 = ar.f32(64, 64)
    d2b = Buf()
    ebc = ar.f32(64)
    ebcb = Buf()
    dl = ar.f32(1, 64)
    dlb = Buf()
    t1 = ar.f32(64, 64)
    t1b = Buf()
    nnt = [(ar.f32(128, 64), Buf()) for _ in range(2)]
    rts = [(ar.f32(64, 64), Buf()) for _ in range(2)]
    t2 = ar.f32(64, 64)
    t2b = Buf()
    attnT = ar.f32(64, 64)
    attnb = Buf()
    kg = ar.f32(64)
    kgb = Buf()
    qd = ar.f32(64)
    qdb = Buf()
    kdec = ar.f32(128, 64)
    kdecb = Buf()
    vb = ar.f32(128, 64)
    vbb = Buf()
    X = ar.f32(128, 64)
    Xb = Buf()
    vn = ar.f32(128, 64)
    vnb = Buf()
    dbuf = Buf()
    ident8 = ident[0:8, 0:8]
    scale = 128.0 ** -0.5
    for b in range(cfg.NB):
        P.op("pool", lambda e: e.memset(S, 0.0), writes=[Sb])
        for s0 in range(0, L, SEG):
            t0 = b * L + s0
            load_with_halo(P, raw3, rawb, proj[GQ:GQ + 384, :], t0, SEG, 3, s0 == 0)
            P.dma("sp", zs, proj[GZ:GZ + 128, t0:t0 + SEG], writes=[zsb])
            P.dma("sp", scs3[:, 0, :], scal[8:16, t0:t0 + SEG], writes=[scsb])
            P.dma("sp", scs3[:, 1, :], scal[16:24, t0:t0 + SEG], writes=[scsb])
            conv_silu(P, qkv3, raw3, 3, SEG, 4, pv, PV_GCW, None, rawb, qkvb)
            P.op("act", lambda e: e.activation(zs, zs, AF.Silu), reads=[zsb], writes=[zsb])
            for c in range(2):
                for u0 in range(0, SEG, 512):
                    ul = min(512, SEG - u0)
                    xs_ = qkv3[:, c, u0:u0 + ul]
                    P.op("act", lambda e: e.activation(sq[:, 0:ul], xs_, AF.Square), reads=[qkvb], writes=[sqb])
                    P.op("pe", lambda e: e.matmul(psum[7][:, 0:ul], lhsT=ones, rhs=sq[:, 0:ul], start=True, stop=True),
                         reads=[sqb, cb], writes=[psb[7]])
                    P.op("dve", lambda e: e.tensor_scalar(rs[:, 0:ul], psum[7][:, 0:ul], 1.0, 1e-6, ALU.mult, ALU.add),
                         reads=[psb[7]], writes=[rsb])
                    P.op("act", lambda e: e.activation(rs[:, 0:ul], rs[:, 0:ul], AF.Sqrt), reads=[rsb], writes=[rsb])
                    P.op("dve", lambda e: e.reciprocal(rs[:, 0:ul], rs[:, 0:ul]), reads=[rsb], writes=[rsb])
                    P.op("dve", lambda e: e.scalar_tensor_tensor(xs_, xs_, (scale if c == 0 else 1.0), rs[:, 0:ul],
                                                                 ALU.mult, ALU.mult), reads=[rsb, qkvb], writes=[qkvb])
            for ch in range(NCH):
                o = ch * 64
                osl = slice(o, o + 64)
                qc, kc, vc = qkv3[:, 0, osl], qkv3[:, 1, osl], qkv3[:, 2, osl]
                P.op("pe", lambda e: e.transpose(psum[0][0:64, 0:8], scs3[:, 0, osl], ident8),
                     reads=[scsb, cb], writes=[psb[0]], sig=False)
                P.op("pe", lambda e: e.transpose(psum[0][0:64, 8:16], scs3[:, 1, osl], ident8),
                     reads=[scsb, cb], writes=[psb[0]])
                P.op("dve", lambda e: e.tensor_copy(gb[:, 0:16], psum[0][0:64, 0:16]), reads=[psb[0]], writes=[gbb])
                P.op("dve", lambda e: e.tensor_scalar(gb[:, 16:17], gb[:, 14:15], -1.0, None, ALU.mult),
                     reads=[gbb], writes=[gbb])
                g_, beta, nbeta = gb[:, 5:6], gb[:, 14:15], gb[:, 16:17]
                P.op("act", lambda e: e.activation(gl, ones[0:64, :], AF.Copy, scale=g_), reads=[gbb, cb], writes=[glb])
                P.op("pe", lambda e: e.matmul(psum[0][0:64, 16:18], lhsT=tri[0:64, 0:64], rhs=gb[:, 4:6],
                                              start=True, stop=True), reads=[gbb, cb], writes=[psb[0]], sig=False)
                P.op("pe", lambda e: e.matmul(psum[0][:, 32:96], lhsT=gl, rhs=tri[0:64, 0:64], start=True, stop=True),
                     reads=[glb, cb], writes=[psb[0]])
                gbc = psum[0][:, 32:96]
                P.op("dve", lambda e: e.tensor_copy(gct, psum[0][0:64, 16:18]), reads=[psb[0]], writes=[gctb])
                gc = gct[:, 1:2]
                P.op("dve", lambda e: e.tensor_scalar(d1, gbc[0:64, :], gc, 0.0, ALU.subtract, ALU.max),
                     reads=[psb[0], gctb], writes=[d1b])
                P.op("act", lambda e: e.activation(d1, d1, AF.Exp, scale=-1.0), reads=[d1b], writes=[d1b])
                P.op("dve", lambda e: e.tensor_scalar(d2, gbc[0:64, :], gc, 0.0, ALU.subtract, ALU.min),
                     reads=[psb[0], gctb], writes=[d2b])
                P.op("act", lambda e: e.activation(d2, d2, AF.Exp), reads=[d2b], writes=[d2b])
                P.op("act", lambda e: e.activation(ebc, gbc, AF.Exp), reads=[psb[0]], writes=[ebcb])
                P.op("dve", lambda e: e.tensor_tensor(dl, gbc[0:64, 63:64], gc, ALU.subtract),
                     reads=[psb[0], gctb], writes=[dlb])
                P.op("act", lambda e: e.activation(dl, dl, AF.Exp), reads=[dlb], writes=[dlb])
                P.op("pe", lambda e: e.matmul(psum[1][0:64, 0:64], lhsT=kc, rhs=kc, start=True, stop=True),
                     reads=[qkvb], writes=[psb[1]], sig=False)
                P.op("pe", lambda e: e.matmul(psum[1][0:64, 64:128], lhsT=kc, rhs=qc, start=True, stop=True),
                     reads=[qkvb], writes=[psb[1]])
                P.op("dve", lambda e: e.tensor_tensor(t1, psum[1][0:64, 0:64], d1, ALU.mult),
                     reads=[psb[1], d1b], writes=[t1b])
                nn0, nn0b = nnt[0]
                P.op("dve", lambda e: e.scalar_tensor_tensor(nn0[:, 0:64], t1, nbeta, lows[0:64, 0:64], ALU.mult, ALU.mult),
                     reads=[t1b, gbb, cb], writes=[nn0b])
                P.op("dve", lambda e: e.tensor_tensor(t2, psum[1][0:64, 64:128], d2, ALU.mult),
                     reads=[psb[1], d2b], writes=[t2b])
                P.op("pool", lambda e: e.tensor_tensor(attnT, t2, tri[0:64, 0:64], ALU.mult),
                     reads=[t2b, cb], writes=[attnb])
                P.op("pe", lambda e: e.transpose(psum[1][0:64, 128:192], nn0[:, 0:64], i64),
                     reads=[nn0b, cb], writes=[psb[1]])
                P.op("act", lambda e: e.activation(nn0[:, 64:128], psum[1][0:64, 128:192], AF.Copy),
                     reads=[psb[1]], writes=[nn0b])
                rt0, rt0b = rts[0]
                P.op("dve", lambda e: e.tensor_tensor(rt0, nn0[:, 64:128], i64, ALU.add), reads=[nn0b, cb], writes=[rt0b])
                for k in range(1, 6):
                    pn, pnb = nnt[(k - 1) % 2]
                    cn, cnb = nnt[k % 2]
                    pr, prb = rts[(k - 1) % 2]
                    cr, crb = rts[k % 2]
                    last = (k == 5)
                    P.op("pe", lambda e: e.matmul(psum[2][0:64, 0:64], lhsT=pn[:, 64:128], rhs=pn[:, 0:64],
                                                  start=True, stop=True), reads=[pnb], writes=[psb[2]], sig=last)
                    if not last:
                        P.op("pe", lambda e: e.matmul(psum[2][0:64, 64:128], lhsT=pn[:, 0:64], rhs=pn[:, 64:128],
                                                      start=True, stop=True), reads=[pnb], writes=[psb[2]])
                    w_ = 64 if last else 128
                    P.op("act", lambda e: e.activation(cn[:, 0:w_], psum[2][0:64, 0:w_], AF.Copy),
                         reads=[psb[2]], writes=[cnb])
                    P.op("pe", lambda e: e.matmul(psum[3][0:64, 0:64], lhsT=cn[:, 0:64], rhs=pr, start=True, stop=True),
                         reads=[cnb, prb], writes=[psb[3]])
                    P.op("dve", lambda e: e.tensor_tensor(cr, pr, psum[3][0:64, 0:64], ALU.add),
                         reads=[psb[3], prb], writes=[crb])
                rT, rTb = rts[5 % 2]
                P.op("dve", lambda e: e.tensor_tensor(kg, kc, ebc, ALU.mult), reads=[qkvb, ebcb], writes=[kgb])
                P.op("pool", lambda e: e.tensor_tensor(qd, qc, ebc, ALU.mult), reads=[qkvb, ebcb], writes=[qdb])
                P.op("pe", lambda e: e.transpose(psum[4][0:64, 0:128], kc, ident), reads=[qkvb, cb], writes=[psb[4]], sig=False)
                P.op("pe", lambda e: e.transpose(psum[4][0:64, 128:256], vc, ident), reads=[qkvb, cb], writes=[psb[4]])
                P.op("act", lambda e: e.activation(kdec, psum[4][0:64, 0:128], AF.Copy, scale=dl),
                     reads=[psb[4], dlb], writes=[kdecb])
                P.op("act", lambda e: e.activation(vb, psum[4][0:64, 128:256], AF.Copy, scale=beta),
                     reads=[psb[4], gbb], writes=[vbb])
                P.op("pe", lambda e: e.matmul(psum[5][0:64, 0:128], lhsT=kg, rhs=S, start=True, stop=True),
                     reads=[kgb, Sb], writes=[psb[5]])
                P.op("dve", lambda e: e.scalar_tensor_tensor(X, psum[5][0:64, 0:128], nbeta, vb, ALU.mult, ALU.add),
                     reads=[psb[5], gbb, vbb], writes=[Xb])
                P.op("pe", lambda e: e.matmul(psum[5][0:64, 128:256], lhsT=rT, rhs=X, start=True, stop=True),
                     reads=[rTb, Xb], writes=[psb[5]])
                P.op("act", lambda e: e.activation(vn, psum[5][0:64, 128:256], AF.Copy), reads=[psb[5]], writes=[vnb])
                P.op("pe", lambda e: e.matmul(psum[6][:, 0:64], lhsT=S, rhs=qd, start=True, stop=False),
                     reads=[Sb, qdb], writes=[psb[6]], sig=False)
                P.op("pe", lambda e: e.matmul(psum[6][:, 0:64], lhsT=vn, rhs=attnT, start=False, stop=True),
                     reads=[vnb, attnb], writes=[psb[6]])
                P.op("act", lambda e: e.activation(oseg[:, osl], psum[6][:, 0:64], AF.Copy), reads=[psb[6]], writes=[osegb])
                P.op("pe", lambda e: e.matmul(psum[7][:, 0:128], lhsT=kdec, rhs=vn, start=True, stop=True),
                     reads=[kdecb, vnb], writes=[psb[7]])
                P.op("dve", lambda e: e.scalar_tensor_tensor(S, S, ebc[:, 63:64], psum[7][:, 0:128], ALU.mult, ALU.add),
                     reads=[psb[7], ebcb, Sb], writes=[Sb])
            for u0 in range(0, SEG, 512):
                ul = min(512, SEG - u0)
                usl = slice(u0, u0 + ul)
                P.op("act", lambda e: e.activation(sq[:, 0:ul], oseg[:, usl], AF.Square), reads=[osegb], writes=[sqb])
                P.op("pe", lambda e: e.matmul(psum[7][:, 0:ul], lhsT=ones, rhs=sq[:, 0:ul], start=True, stop=True),
                     reads=[sqb, cb], writes=[psb[7]])
                P.op("dve", lambda e: e.tensor_scalar(rs[:, 0:ul], psum[7][:, 0:ul], 1.0 / 128.0, EPS, ALU.mult, ALU.add),
                     reads=[psb[7]], writes=[rsb])
                P.op("act", lambda e: e.activation(rs[:, 0:ul], rs[:, 0:ul], AF.Sqrt), reads=[rsb], writes=[rsb])
                P.op("dve", lambda e: e.reciprocal(rs[:, 0:ul], rs[:, 0:ul]), reads=[rsb], writes=[rsb])
                P.op("dve", lambda e: e.scalar_tensor_tensor(oseg[:, usl], oseg[:, usl], pv[:, PV_GNORM:PV_GNORM + 1],
                                                             rs[:, 0:ul], ALU.mult, ALU.mult),
                     reads=[rsb, osegb], writes=[osegb])
                P.op("pool", lambda e: e.tensor_tensor(yo[:, usl], oseg[:, usl], zs[:, usl], ALU.mult),
                     reads=[osegb, zsb], writes=[yob])
            P.dma("sp", y_gdn.span(0, 128, t0, SEG), y_gdn.sview(yo), reads=[yob], writes=[dbuf])
    P.barrier()


def phase_branch(P, env, cfg, w_br, yfs, proj, m_loc, pre=None):
    st = {}
    segs = [(w_br[0:2048, :], yfs[0], 2048), (w_br[2048:3072, :], yfs[1], 1024), (w_br[3072:4096, :], yfs[2], 1024)]
    gate_rows = (GA_SSM, GA_FOX, GA_GDN)

    def setup(ar):
        st["g"] = [(ar.f32(3 * 512), Buf()) for _ in range(2)]
        st["acc"] = [(ar.f32(512), Buf()) for _ in range(2)]
        st["o"] = [(ar.bf16(512), Buf()) for _ in range(2)]
        st["i"] = 0
        st["d"] = Buf()

    def epi(n0, chunks, t0, tl):
        for j in range(len(chunks[0])):
            i = st["i"]
            st["i"] += 1
            r0, rows, _, _ = chunks[0][j]
            g, gb = st["g"][i % 2]
            g3 = g.rearrange("p (s t) -> p s t", t=512)
            for s in range(3):
                P.dma("sp", g3[:, s, 0:tl], proj[gate_rows[s] + r0:gate_rows[s] + r0 + 128, t0:t0 + tl], writes=[gb])
            P.op("act", lambda e: e.activation(g3[:, :, 0:tl], g3[:, :, 0:tl], AF.Sigmoid), reads=[gb], writes=[gb])
            acc, accb = st["acc"][i % 2]
            P.op("dve", lambda e: e.tensor_tensor(acc[:, 0:tl], chunks[0][j][3], g3[:, 0, 0:tl], ALU.mult),
                 reads=[chunks[0][j][2], gb], writes=[accb])
            P.op("dve", lambda e: e.tensor_tensor(g3[:, 1, 0:tl], chunks[1][j][3], g3[:, 1, 0:tl], ALU.mult),
                 reads=[chunks[1][j][2], gb], writes=[gb])
            P.op("dve", lambda e: e.tensor_tensor(g3[:, 2, 0:tl], chunks[2][j][3], g3[:, 2, 0:tl], ALU.mult),
                 reads=[chunks[2][j][2], gb], writes=[gb])
            P.op("pool", lambda e: e.tensor_tensor(acc[:, 0:tl], acc[:, 0:tl], g3[:, 1, 0:tl], ALU.add),
                 reads=[gb, accb], writes=[accb])
            o, ob = st["o"][i % 2]
            P.op("pool", lambda e: e.tensor_tensor(o[:, 0:tl], acc[:, 0:tl], g3[:, 2, 0:tl], ALU.add),
                 reads=[gb, accb], writes=[ob])
            P.dma("sp", m_loc.span(r0, 128, t0, tl), m_loc.sview(o[:, 0:tl]), reads=[ob], writes=[st["d"]])
    gemm(P, env, segs, 512, cfg.T, epi, setup=setup, NPW=256, pre=pre)
    P.barrier()


def phase_resid(P, env, cfg, W, K, A, xsrc, xdst, pre=None):
    st = {}

    def setup(ar):
        st["x"] = [(ar.f32(512), Buf()) for _ in range(4)]
        st["i"] = 0
        st["d"] = Buf()

    def epi(n0, chunks, t0, tl):
        for (r0, rows, pb, pap) in chunks[0]:
            i = st["i"]
            st["i"] += 1
            x, xb = st["x"][i % 4]
            P.dma("sp", x[:, 0:tl], xsrc[r0:r0 + 128, t0:t0 + tl], writes=[xb])
            P.op("dve", lambda e: e.tensor_tensor(x[:, 0:tl], x[:, 0:tl], pap, ALU.add), reads=[pb, xb], writes=[xb])
            P.dma("sp", xdst[r0:r0 + 128, t0:t0 + tl], x[:, 0:tl], reads=[xb], writes=[st["d"]])
    gemm(P, env, [(W, A, K)], 512, cfg.T, epi, setup=setup, pre=pre)
    P.barrier()


def phase_ffn_up(P, env, cfg, W, hfull, pv, act_loc, pre=None):
    st = {}
    L = cfg.SEQ

    def setup(ar):
        st["u"] = [(ar.f32(514), Buf()) for _ in range(4)]
        st["carry"] = [(ar.f32(2), Buf()) for _ in range(4)]
        st["y"] = [(ar.f32(512), Buf()) for _ in range(4)]
        st["o"] = [(ar.bf16(512), Buf()) for _ in range(2)]
        st["d"] = Buf()
        st["oi"] = 0

    def epi(n0, chunks, t0, tl):
        cs = chunks[0]
        nc_ = len(cs)
        half = nc_ // 2
        p = n0 // 512
        ys = []
        for j, (r0, rows, pb, pap) in enumerate(cs):
            is_val = j >= half
            cidx = 2 * p + (j - half if is_val else j)
            pidx = (11 if is_val else 0) + cidx
            u, ub = st["u"][j]
            cr, crb = st["carry"][j]
            if t0 % L == 0:
                P.op("pool", lambda e: e.memset(u[:, 0:2], 0.0), writes=[ub])
            else:
                P.op("pool", lambda e: e.tensor_copy(u[:, 0:2], cr), reads=[crb], writes=[ub])
            P.op("act", lambda e: e.activation(u[:, 2:2 + tl], pap, AF.Copy), reads=[pb], writes=[ub])
            P.op("pool", lambda e: e.tensor_copy(cr, u[:, tl:tl + 2]), reads=[ub], writes=[crb])
            y, yb = st["y"][j]
            w = lambda k: pv[:, PV_FCW + pidx * 3 + k:PV_FCW + pidx * 3 + k + 1]
            P.op("dve", lambda e: e.tensor_scalar(y[:, 0:tl], u[:, 2:2 + tl], w(2), pv[:, PV_FCB + pidx:PV_FCB + pidx + 1],
                                                  ALU.mult, ALU.add), reads=[ub], writes=[yb])
            P.op("dve", lambda e: e.scalar_tensor_tensor(y[:, 0:tl], u[:, 1:1 + tl], w(1), y[:, 0:tl], ALU.mult, ALU.add),
                 reads=[ub, yb], writes=[yb])
            P.op("dve", lambda e: e.scalar_tensor_tensor(y[:, 0:tl], u[:, 0:tl], w(0), y[:, 0:tl], ALU.mult, ALU.add),
                 reads=[ub, yb], writes=[yb])
            ys.append((y, yb, cidx))
        for j in range(half):
            gy, gyb, cidx = ys[j]
            vy, vyb, _ = ys[half + j]
            P.op("act", lambda e: e.activation(gy[:, 0:tl], gy[:, 0:tl], AF.Silu), reads=[gyb], writes=[gyb])
            o, ob = st["o"][st["oi"] % 2]
            st["oi"] += 1
            P.op("pool", lambda e: e.tensor_tensor(o[:, 0:tl], gy[:, 0:tl], vy[:, 0:tl], ALU.mult),
                 reads=[gyb, vyb], writes=[ob])
            rows = min(128, FFL - cidx * 128)
            P.dma("sp", act_loc.span(cidx * 128, rows, t0, tl), act_loc.sview(o[0:rows, 0:tl]), reads=[ob], writes=[st["d"]])
    gemm(P, env, [(W, hfull, D)], NUPW, cfg.T, epi, setup=setup, pre=pre)
    P.barrier()


def build(cfg):
    nc = bass.Bass("TRN2", target_bir_lowering=False)
    _DB.clear()
    T, DEP = cfg.T, cfg.DEPTH

    def ext(name, shape, dt):
        return nc.dram_tensor(name, shape, dt, kind="ExternalInput")
    xT = ext("xT", [512, T], F32)
    w_in = ext("w_in", [DEP * D, NPROJ], F32)
    lt = cfg.lite
    w_br = ext("w_br", [128 if lt else DEP * D, 512], F32)
    w_out = ext("w_out", [128 if lt else DEP * D, 512], F32)
    w_up = ext("w_up", [128 if lt else DEP * D, NUPW], F32)
    w_down = ext("w_down", [128 if lt else DEP * DFF, 512], F32)
    pvs = ext("pvs", [128, DEP * NPV + 4], F32)
    outT = nc.dram_tensor("outT", [512, T], F32, kind="ExternalOutput")
    scr = {}

    def sc(name, shape, dt):
        scr[name] = nc.dram_tensor(name, shape, dt)
        return scr[name]
    proj = sc("proj", [NPROJ_PAD, T], F32)
    scal = sc("scal", [32, T], F32)
    ssq_loc = sc("ssq_loc", [1, T], F32)
    ssq_mid = sc("ssq_mid", [1, T], F32)
    ssq_tot = sc("ssq_tot", [1, T], F32)
    ssq2_loc = sc("ssq2_loc", [1, T], F32)
    ssq2_tot = sc("ssq2_tot", [1, T], F32)
    ypre = sc("ypre", [256, T], F32)
    xresA = sc("xresA", [512, T], F32)
    xresB = sc("xresB", [512, T], F32)
    cts = {}

    def ct3(name, R):
        tcn = tc_for(R)
        loc = CT(nc, name + "_loc", R, T, BF16, tcn)
        mid = CT(nc, name + "_mid", 2 * R, T, BF16, tcn)
        full = CT(nc, name + "_full", 8 * R, T, BF16, tcn)
        for x in (loc, mid, full):
            cts[x.name] = x
            scr[x.name] = x.t
        return loc, mid, full
    h_loc, h_mid, hfull = ct3("h", 512)
    y_ssd, ym_ssd, yf_ssd = ct3("y_ssd", 256)
    y_fox, ym_fox, yf_fox = ct3("y_fox", 128)
    y_gdn, ym_gdn, yf_gdn = ct3("y_gdn", 128)
    m_loc, m_mid, mfull = ct3("m", 512)
    act_loc, act_mid, actfull = ct3("act", FFL)
    taps = {n: nc.dram_tensor("tap_" + n, list(scr[n].shape), scr[n].dtype, kind="ExternalOutput") for n in cfg.taps}

    class Stop(Exception):
        pass

    with contextlib.ExitStack() as st:
        P = Prog(nc)
        env = make_env(nc, st)
        C = make_consts(nc, st, P, env)
        pvt = st.enter_context(nc.sbuf_tensor("pvsb", [128, DEP * NPV + 4], F32))
        pvb = Buf()
        P.dma("sp", pvt[:, :], pvs[:, :], writes=[pvb])
        P.barrier()

        def done(name):
            if cfg.stop_after == name:
                raise Stop()
        try:
            xcur = xT
            for l in range(DEP):
                pv = pvt[:, l * NPV:(l + 1) * NPV]
                phase_norm(P, env, C, cfg, xcur, pv[:, PV_NMIX:PV_NMIX + 4], ssq_loc, ssq_mid, ssq_tot, h_loc, h_mid, hfull)
                done("norm1")
                setup, epi = copy_epilogue(P, proj)
                gemm(P, env, [(w_in[l * D:(l + 1) * D, :], hfull, D)], NPROJ, T, epi, setup=setup,
                     pre=lambda: allgather8(P, (h_loc, h_mid, hfull)))
                P.barrier()
                done("inproj")
                phase_scal(P, env, C, cfg, proj, pv, scal)
                done("scal")
                phase_fox(P, env, C, cfg, proj, scal, y_fox)
                done("fox")
                phase_ssd(P, env, C, cfg, proj, scal, pv, ypre, ssq2_loc, ssq2_tot, y_ssd)
                done("ssd")
                phase_gdn(P, env, C, cfg, proj, scal, pv, y_gdn)
                done("gdn")
                def ag_y():
                    allgather8(P, (y_ssd, ym_ssd, yf_ssd), (y_fox, ym_fox, yf_fox), (y_gdn, ym_gdn, yf_gdn))
                phase_branch(P, env, cfg, w_br[l * D:(l + 1) * D, :], (yf_ssd, yf_fox, yf_gdn), proj, m_loc, pre=ag_y)
                done("branch")
                phase_resid(P, env, cfg, w_out[l * D:(l + 1) * D, :], D, mfull, xcur, xresA,
                            pre=lambda: allgather8(P, (m_loc, m_mid, mfull)))
                xcur = xresA
                done("wout")
                phase_norm(P, env, C, cfg, xcur, pv[:, PV_NFFN:PV_NFFN + 4], ssq_loc, ssq_mid, ssq_tot, h_loc, h_mid, hfull)
                phase_ffn_up(P, env, cfg, w_up[l * D:(l + 1) * D, :], hfull, pv, act_loc,
                             pre=lambda: allgather8(P, (h_loc, h_mid, hfull)))
                done("ffnup")
                phase_resid(P, env, cfg, w_down[l * DFF:(l + 1) * DFF, :], DFF, actfull, xcur, xresB,
                            pre=lambda: allgather8(P, (act_loc, act_mid, actfull)))
                xcur = xresB
                done("layer%d" % l)
            phase_norm(P, env, C, cfg, xcur, pvt[:, DEP * NPV:DEP * NPV + 4], ssq_loc, ssq_mid, ssq_tot, out_f32=outT)
        except Stop:
            pass
        P.barrier(cc=True)
        for n, tp in taps.items():
            P.dma("sp", tp.ap(), scr[n].ap())
        P.barrier(cc=True)
        blk = st.enter_context(nc.Block())
        P.replay(blk)
    return nc


_SPLITS = [0, 2048, 5120, 5152, 8224, 8232, 11304, 11312, 11320, 12344, 16440, 20536]


def _in_cols(c):
    o_z, o_xbc, o_dt, o_fqkv, o_ff, o_gqkv, o_ga, o_gb, o_gz, o_gs, o_gf, o_gg = _SPLITS
    g = c // 2
    r = np.arange
    cols = [o_z + c * 256 + r(256), o_xbc + c * 256 + r(256), o_xbc + 2048 + g * 128 + r(128),
            o_xbc + 2560 + g * 128 + r(128),
            o_fqkv + c * 128 + r(128), o_fqkv + 1024 + c * 128 + r(128), o_fqkv + 2048 + c * 128 + r(128),
            o_gqkv + c * 128 + r(128), o_gqkv + 1024 + c * 128 + r(128), o_gqkv + 2048 + c * 128 + r(128),
            o_gz + c * 128 + r(128),
            o_gs + c * 512 + r(512), o_gf + c * 512 + r(512), o_gg + c * 512 + r(512),
            o_dt + 4 * c + r(4), np.array([o_ff + c, o_ga + c, o_gb + c, o_gb + c])]
    cols = np.concatenate(cols)
    assert cols.shape[0] == NPROJ
    return cols


def _pcol(v):
    n = cdiv(v.shape[0], 128)
    out = np.zeros((n * 128,), np.float32)
    out[:v.shape[0]] = v
    return out.reshape(n, 128).T


def _rperm(nrows):
    rb = nrows // NCORES
    return np.concatenate([np.arange(r * rb, (r + 1) * rb) for r in RANK_ORDER])


def _layout(inp, c, DEP):
    f32 = np.float32
    g = c // 2
    pD = _rperm(D)
    w_in_c = np.concatenate([inp["w_in"][l][:, _in_cols(c)][pD] for l in range(DEP)], 0)
    cs = slice(c * 512, (c + 1) * 512)
    w_br_c = np.concatenate([np.concatenate([inp["w_br_ssm"][l][:, cs][_rperm(2048)], inp["w_br_fox"][l][:, cs][_rperm(1024)],
                                             inp["w_br_gdn"][l][:, cs][_rperm(1024)]], 0) for l in range(DEP)], 0)
    w_out_c = np.concatenate([inp["w_out"][l][:, c * 512:(c + 1) * 512][pD] for l in range(DEP)], 0)
    w_down_c = np.concatenate([inp["w_down"][l][:, c * 512:(c + 1) * 512][_rperm(DFF)] for l in range(DEP)], 0)
    ups = []
    for l in range(DEP):
        wu = inp["w_up"][l]
        blk = np.zeros((D, NUPW), f32)
        col = 0
        for p in range(6):
            chunks = [2 * p, 2 * p + 1] if p < 5 else [10]
            for base in (0, DFF):
                for ci in chunks:
                    rows = min(128, FFL - ci * 128)
                    src0 = base + c * FFL + ci * 128
                    blk[:, col:col + rows] = wu[:, src0:src0 + rows]
                    col += 128
        assert col == NUPW
        ups.append(blk[pD])
    w_up_c = np.concatenate(ups, 0)
    pv = np.zeros((128, DEP * NPV + 4), f32)
    xbc_idx = np.concatenate([c * 256 + np.arange(256), 2048 + g * 128 + np.arange(128), 2560 + g * 128 + np.arange(128)])
    gq_idx = np.concatenate([c * 128 + np.arange(128), 1024 + c * 128 + np.arange(128), 2048 + c * 128 + np.arange(128)])
    for l in range(DEP):
        o = l * NPV
        pv[:, o + PV_NMIX:o + PV_NMIX + 4] = _pcol(inp["norm_mix"][l][c * 512:(c + 1) * 512])
        scw = inp["ssm_conv_w"][l][:, xbc_idx]
        for ch in range(4):
            for j in range(4):
                pv[:, o + PV_SCW + ch * 4 + j] = scw[j, ch * 128:(ch + 1) * 128]
        pv[:, o + PV_SCB:o + PV_SCB + 4] = _pcol(inp["ssm_conv_b"][l][xbc_idx])
        pv[:, o + PV_SNORM:o + PV_SNORM + 2] = _pcol(inp["ssm_norm"][l][c * 256:(c + 1) * 256])
        pv[:, o + PV_SD:o + PV_SD + 2] = _pcol(np.repeat(inp["ssm_d"][l][4 * c:4 * c + 4], 64))
        gcw = inp["gdn_conv_w"][l][:, gq_idx]
        for ch in range(3):
            for j in range(4):
                pv[:, o + PV_GCW + ch * 4 + j] = gcw[j, ch * 128:(ch + 1) * 128]
        pv[:, o + PV_GNORM] = inp["gdn_norm"][l]
        pv[:, o + PV_NFFN:o + PV_NFFN + 4] = _pcol(inp["norm_ffn"][l][c * 512:(c + 1) * 512])
        for base, poff in ((0, 0), (DFF, 11)):
            fcw = inp["ffn_conv_w"][l][:, base + c * FFL:base + (c + 1) * FFL]
            fcb = inp["ffn_conv_b"][l][base + c * FFL:base + (c + 1) * FFL]
            for j in range(3):
                cw = _pcol(fcw[j])
                for ci in range(11):
                    pv[:, o + PV_FCW + (poff + ci) * 3 + j] = cw[:, ci]
            pv[:, o + PV_FCB + poff:o + PV_FCB + poff + 11] = _pcol(fcb)
        pv[0:4, o + PV_SBIAS] = inp["ssm_dt_bias"][l][4 * c:4 * c + 4]
        pv[4, o + PV_SBIAS] = inp["fox_f_bias"][l][c]
        pv[5, o + PV_SBIAS] = inp["gdn_dt_bias"][l][c]
        pv[0:4, o + PV_SALOG] = inp["ssm_a_log"][l][4 * c:4 * c + 4]
        pv[5, o + PV_SALOG] = inp["gdn_a_log"][l][c]
        pv[:, o + PV_SSGN] = 1.0
        pv[4, o + PV_SSGN] = -1.0
    pv[:, DEP * NPV:DEP * NPV + 4] = _pcol(inp["norm_final"][c * 512:(c + 1) * 512])
    return {"w_in": np.ascontiguousarray(w_in_c), "w_br": np.ascontiguousarray(w_br_c),
            "w_out": np.ascontiguousarray(w_out_c), "w_up": w_up_c, "w_down": np.ascontiguousarray(w_down_c),
            "pvs": pv}


def run(cfg, inputs, trace=False):
    inp = {k: np.asarray(v) for k, v in inputs.items()}
    x = inp["x"].reshape(cfg.T, D)
    in_maps = []
    for c in range(NCORES):
        m = _layout(inp, c, cfg.DEPTH)
        m["xT"] = np.ascontiguousarray(x[:, c * 512:(c + 1) * 512].T)
        if cfg.lite:
            for k in ("w_br", "w_out", "w_up", "w_down"):
                m[k] = np.ascontiguousarray(m[k][0:128])
        in_maps.append(m)
    nc = build(cfg)
    res = run_bass_kernel_spmd(nc, in_maps, core_ids=list(range(NCORES)), **({"trace": True} if trace else {}))
    return res


def kernel(**inputs):
    x = np.asarray(inputs["x"])
    NB, SEQ, _ = x.shape
    cfg = Cfg(NB=NB, SEQ=SEQ, DEPTH=np.asarray(inputs["w_in"]).shape[0])
    res = run(cfg, inputs)
    outT = np.concatenate([r["outT"] for r in res.results], axis=0)
    return np.ascontiguousarray(outT.T).reshape(NB, SEQ, D).astype(np.float32)
```

```python
import contextlib
import numpy as np
import ml_dtypes
import concourse.bass as bass
import concourse.mybir as mybir
from concourse.bass_utils import run_bass_kernel_spmd

F32 = mybir.dt.float32
BF16 = mybir.dt.bfloat16
AF = mybir.ActivationFunctionType
ALU = mybir.AluOpType
AX = mybir.AxisListType

NCORES = 8
ENGS = ("pe", "act", "dve", "pool", "sp")
NDSEM = {"sp": 12, "pool": 6, "act": 4}
SAME_ENG_SYNC = True


class Buf:
    __slots__ = ("w", "r")

    def __init__(self):
        self.w = None
        self.r = {}


class _Rec:
    def __init__(self):
        self.call = None

    def __getattr__(self, name):
        def f(*a, **k):
            self.call = (name, a, k)
        return f


class Prog:
    def __init__(self, nc):
        self.nc = nc
        self.streams = {e: [] for e in ENGS}
        self.cnt = {}
        self.semh = {}
        self.seen = {e: {} for e in ENGS}
        self.rr = {q: 0 for q in NDSEM}
        for e in ENGS:
            self._mksem(("e", e))
        for q, n in NDSEM.items():
            for i in range(n):
                self._mksem(("d", q, i))
        self._mksem(("cc",))

    def _mksem(self, key):
        self.semh[key] = self.nc.alloc_semaphore(name="s_" + "_".join(str(k) for k in key))
        self.cnt[key] = 0

    def _deps(self, reads, writes):
        deps = {}

        def add(tok):
            if tok is not None and deps.get(tok[0], 0) < tok[1]:
                deps[tok[0]] = tok[1]
        for b in reads:
            add(b.w)
        for b in writes:
            add(b.w)
            for k, v in b.r.items():
                add((k, v))
        return deps

    def _waits(self, eng, deps):
        waits = []
        own = ("e", eng)
        for k, v in deps.items():
            if k == own:
                if eng == "pe" or not SAME_ENG_SYNC or v > self.cnt[own]:
                    continue
            if self.seen[eng].get(k, 0) >= v:
                continue
            self.seen[eng][k] = v
            waits.append((k, v))
        return waits

    def _commit(self, tok, reads, writes):
        for b in reads:
            if b.r.get(tok[0], 0) < tok[1]:
                b.r[tok[0]] = tok[1]
        for b in writes:
            b.w = tok
            b.r = {}

    def op(self, eng, fn, reads=(), writes=(), sig=True, extra=()):
        rec = _Rec()
        fn(rec)
        name, a, k = rec.call
        fn = lambda e, name=name, a=a, k=k: getattr(e, name)(*a, **k)
        deps = self._deps(reads, writes)
        for tok in extra:
            if tok is not None and deps.get(tok[0], 0) < tok[1]:
                deps[tok[0]] = tok[1]
        waits = self._waits(eng, deps)
        key = ("e", eng)
        if sig:
            self.cnt[key] += 1
            tok = (key, self.cnt[key])
            self.streams[eng].append((waits, fn, (key, 1)))
        else:
            tok = (key, self.cnt[key] + 1)
            self.streams[eng].append((waits, fn, None))
        self._commit(tok, reads, writes)
        return tok

    def dma(self, q, out, in_, reads=(), writes=(), **kw):
        k = self.rr[q] % NDSEM[q]
        self.rr[q] += 1
        key = ("d", q, k)
        deps = self._deps(reads, writes)
        if self.cnt[key] > 0:
            deps[key] = self.cnt[key]
        waits = self._waits(q, deps)
        self.cnt[key] += 16
        tok = (key, self.cnt[key])
        self.streams[q].append((waits, lambda e: e.dma_start(out=out, in_=in_, **kw), (key, 16)))
        self._commit(tok, reads, writes)
        return tok

    def collective(self, kind, op, groups, in_ap, out_ap, reads=(), writes=()):
        key = ("cc",)
        deps = self._deps(reads, writes)
        waits = self._waits("pool", deps)
        self.cnt[key] += 1
        tok = (key, self.cnt[key])
        self.streams["pool"].append((waits, lambda e: e.collective_compute(
            kind, op, replica_groups=groups, ins=[in_ap], outs=[out_ap]), (key, 1)))
        self._commit(tok, reads, writes)
        return tok

    def barrier(self, cc=False):
        allk = {k: v for k, v in self.cnt.items() if v > 0 and (cc or k != ("cc",))}
        for e in ENGS:
            waits = self._waits(e, dict(allk))
            if waits:
                self.streams[e].append((waits, None, None))

    def replay(self, block):
        def run(stream):
            def body(e):
                for waits, fn, sig in stream:
                    for k, v in waits:
                        e.wait_ge(self.semh[k], v)
                    if fn is None:
                        continue
                    ins = fn(e)
                    if sig is not None:
                        ins.then_inc(self.semh[sig[0]], sig[1])
            return body
        block.tensor(run(self.streams["pe"]))
        block.scalar(run(self.streams["act"]))
        block.vector(run(self.streams["dve"]))
        block.gpsimd(run(self.streams["pool"]))
        block.sync(run(self.streams["sp"]))


class Arena:
    def __init__(self, t, ncols):
        self.t = t
        self.n = ncols
        self.o = 0

    def reset(self):
        self.o = 0

    def f32(self, cols, parts=128):
        assert self.o + cols <= self.n, (self.o, cols, self.n)
        ap = self.t[0:parts, self.o:self.o + cols]
        self.o += cols
        return ap

    def bf16(self, cols, parts=128):
        c32 = (cols + 1) // 2
        return self.f32(c32, parts).bitcast(BF16)[:, 0:cols]


G4 = [[0, 1, 2, 3], [4, 5, 6, 7]]
GX = [[0, 4], [1, 5], [2, 6], [3, 7]]


class CT:
    def __init__(self, nc, name, R, T, dtype, TC):
        TC = min(TC, T)
        assert T % TC == 0
        self.R, self.T, self.TC, self.NCH = R, T, TC, T // TC
        nparts = 1
        while R * T * 2 / nparts > 200e6:
            nparts *= 2
        assert self.NCH % nparts == 0
        self.cpp = self.NCH // nparts
        self.ts = [nc.dram_tensor(name if nparts == 1 else "%s_p%d" % (name, i), [self.cpp * R, TC], dtype)
                   for i in range(nparts)]
        self.t = self.ts[0]
        self.name = name
        self.bufs = [Buf() for _ in range(self.NCH)]

    def rows(self, ch, r0, n):
        t, c = self.ts[ch // self.cpp], ch % self.cpp
        return t[c * self.R + r0:c * self.R + r0 + n, :]

    def chunk(self, ch):
        return self.rows(ch, 0, self.R)

    def span(self, r0, rows, t0, tl):
        assert t0 % self.TC == 0 and tl % self.TC == 0
        ch0, n = t0 // self.TC, tl // self.TC
        t, c = self.ts[ch0 // self.cpp], ch0 % self.cpp
        assert c + n <= self.cpp
        v = t.ap().rearrange("(c r) t -> r c t", r=self.R)
        return v[r0:r0 + rows, c:c + n, :]

    def sview(self, ap2d):
        return ap2d.rearrange("p (c t) -> p c t", t=self.TC)


def tc_for(R):
    return 512 if R <= 512 else 128


RANK_ORDER = [0, 4, 1, 5, 2, 6, 3, 7]


def allgather8(P, *triples):
    LA = 3
    n = triples[0][0].NCH
    assert all(t[0].NCH == n for t in triples)

    def s1(ch):
        for loc, mid, full in triples:
            P.collective("AllGather", ALU.bypass, GX, loc.chunk(ch), mid.chunk(ch), reads=[loc.bufs[ch]], writes=[mid.bufs[ch]])

    def s2(ch):
        for loc, mid, full in triples:
            P.collective("AllGather", ALU.bypass, G4, mid.chunk(ch), full.chunk(ch), reads=[mid.bufs[ch]], writes=[full.bufs[ch]])
    for ch in range(min(LA, n)):
        s1(ch)
    for ch in range(n):
        if ch + LA < n:
            s1(ch + LA)
        s2(ch)


_DB = {}


def dbuf_of(t):
    if t.name not in _DB:
        _DB[t.name] = Buf()
    return _DB[t.name]


def allreduce8(P, src, mid, dst):
    P.collective("AllReduce", ALU.add, G4, src.ap().opt(), mid.ap().opt(), reads=[dbuf_of(src)], writes=[dbuf_of(mid)])
    P.collective("AllReduce", ALU.add, GX, mid.ap().opt(), dst.ap().opt(), reads=[dbuf_of(mid)], writes=[dbuf_of(dst)])


def cdiv(a, b):
    return (a + b - 1) // b


D = 4096
NIN = 24632
DFF = 11008
FFL = DFF // NCORES
EPS = 1e-6
PZ, PX, PB, PC = 0, 256, 512, 640
FQ, FK, FV = 768, 896, 1024
GQ, GK, GV, GZ = 1152, 1280, 1408, 1536
GA_SSM, GA_FOX, GA_GDN = 1664, 2176, 2688
PS = 3200
NPROJ = 3208
NPROJ_PAD = 3328
PV_NMIX, PV_SCW, PV_SCB, PV_SNORM, PV_SD, PV_GCW, PV_GNORM, PV_NFFN = 0, 4, 20, 24, 26, 28, 40, 41
PV_FCW, PV_FCB, PV_SBIAS, PV_SALOG, PV_SSGN = 45, 111, 133, 134, 135
NPV = 136
NUPW = 2816


class Cfg:
    def __init__(self, NB=4, SEQ=4096, DEPTH=2, stop_after=None, taps=(), lite=False):
        self.NB, self.SEQ, self.DEPTH = NB, SEQ, DEPTH
        self.lite = lite
        self.T = NB * SEQ
        self.stop_after = stop_after
        self.taps = tuple(taps)


def make_env(nc, st, ncols=45056):
    env = {}
    t = st.enter_context(nc.sbuf_tensor("arena", [128, ncols], F32))
    env["ar"] = Arena(t, ncols)
    env["psum"] = [st.enter_context(nc.psum_tensor("ps%d" % i, [128, 512], F32)) for i in range(8)]
    env["psb"] = [Buf() for _ in range(8)]
    return env


def make_consts(nc, st, P, env):
    c = {}
    cb = Buf()
    t = st.enter_context(nc.sbuf_tensor("consts", [128, 1024], F32))
    ones, ident, tri, tris = t[:, 0:128], t[:, 128:256], t[:, 256:384], t[:, 384:512]
    sel4 = t[0:8, 512:640]
    bfv = t[:, 640:1024].bitcast(BF16)
    ones_bf, tri_bf, ident_bf = bfv[:, 0:128], bfv[:, 128:256], bfv[:, 256:384]
    P.op("pool", lambda e: e.memset(ones, 1.0), writes=[cb])
    P.op("pool", lambda e: e.affine_select(ident, ones, pattern=[[-1, 128]], compare_op=ALU.is_equal,
                                           fill=0.0, base=0, channel_multiplier=1), reads=[cb], writes=[cb])
    P.op("pool", lambda e: e.affine_select(tri, ones, pattern=[[1, 128]], compare_op=ALU.is_ge,
                                           fill=0.0, base=0, channel_multiplier=-1), reads=[cb], writes=[cb])
    P.op("pool", lambda e: e.affine_select(tris, ones, pattern=[[1, 128]], compare_op=ALU.is_ge,
                                           fill=0.0, base=-1, channel_multiplier=-1), reads=[cb], writes=[cb])
    P.op("pool", lambda e: e.affine_select(sel4, ones[0:8, :], pattern=[[0, 128]], compare_op=ALU.is_equal,
                                           fill=0.0, base=-4, channel_multiplier=1), reads=[cb], writes=[cb])
    P.op("pool", lambda e: e.tensor_copy(ones_bf, ones), reads=[cb], writes=[cb])
    P.op("pool", lambda e: e.tensor_copy(tri_bf, tri), reads=[cb], writes=[cb])
    P.op("pool", lambda e: e.tensor_copy(ident_bf, ident), reads=[cb], writes=[cb])
    t2 = st.enter_context(nc.sbuf_tensor("consts2", [128, 128], F32))
    lows = t2[:, 0:128]
    P.op("pool", lambda e: e.affine_select(lows, ones, pattern=[[-1, 128]], compare_op=ALU.is_gt,
                                           fill=0.0, base=0, channel_multiplier=1), reads=[cb], writes=[cb])
    c.update(ones=ones, ident=ident, tri=tri, tris=tris, sel4=sel4, ones_bf=ones_bf, tri_bf=tri_bf,
             ident_bf=ident_bf, lows=lows, buf=cb)
    return c


def gemm(P, env, segs, N, T, epi, setup=None, NPW=512, TT=512, KG=8, pre=None):
    ar, psum, psb = env["ar"], env["psum"], env["psb"]
    ar.reset()
    nseg = len(segs)
    KCs = [K // 128 for (_, _, K) in segs]
    assert all(K % 128 == 0 for (_, _, K) in segs)
    CPP = NPW // 128
    assert nseg * CPP <= 8
    dbl = nseg * CPP <= 4
    NPASS = cdiv(N, NPW)
    NWB = 2 if (NPASS > 1 and sum(KCs) * NPW * 2 * 2 <= 72 * 1024) else 1
    wsets = [([ar.bf16(kc * NPW).rearrange("p (k n) -> p k n", n=NPW) for kc in KCs], Buf()) for _ in range(NWB)]
    NAB = 3
    abufs = [(ar.bf16(KG * TT).rearrange("p (k t) -> p k t", t=TT), Buf()) for _ in range(NAB)]
    if setup is not None:
        setup(ar)
    ai = 0
    ti = 0
    def load_w(ps):
        n0 = ps * NPW
        npw = min(NPW, N - n0)
        wsbs, wb = wsets[ps % NWB]
        for s, (W, A, K) in enumerate(segs):
            for k0 in range(0, KCs[s], KG):
                kk = min(KG, KCs[s] - k0)
                P.dma("pool", wsbs[s][:, k0:k0 + kk, 0:npw],
                      W[k0 * 128:(k0 + kk) * 128, n0:n0 + npw].rearrange("(k p) n -> p k n", p=128),
                      writes=[wb])
    load_w(0)
    for ps in range(NPASS):
        n0 = ps * NPW
        npw = min(NPW, N - n0)
        wsbs, wb = wsets[ps % NWB]
        if NWB == 2:
            if ps + 1 < NPASS:
                load_w(ps + 1)
        elif ps > 0:
            load_w(ps)
        if ps == 0 and pre is not None:
            pre()
        nchunks = cdiv(npw, 128)
        for t0 in range(0, T, TT):
            tl = min(TT, T - t0)
            half = (ti % 2) * 4 if dbl else 0
            ti += 1
            for s, (W, A, K) in enumerate(segs):
                KC = KCs[s]
                for k0 in range(0, KC, KG):
                    kk = min(KG, KC - k0)
                    at, ab = abufs[ai % NAB]
                    ai += 1
                    for c0 in range(0, tl, A.TC):
                        ch = (t0 + c0) // A.TC
                        P.dma("sp", at[:, 0:kk, c0:c0 + A.TC],
                              A.rows(ch, k0 * 128, kk * 128).rearrange("(k p) t -> p k t", p=128),
                              reads=[A.bufs[ch]], writes=[ab])
                    for j in range(nchunks):
                        rows = min(128, npw - j * 128)
                        bk = half + s * CPP + j
                        for k in range(kk):
                            kc = k0 + k
                            P.op("pe", (lambda e, o=psum[bk][0:rows, 0:tl],
                                        l=wsbs[s][:, kc, j * 128:j * 128 + rows],
                                        r=at[:, k, 0:tl], st_=(kc == 0), sp_=(kc == KC - 1):
                                        e.matmul(o, lhsT=l, rhs=r, start=st_, stop=sp_)),
                                 reads=[wb, ab], writes=[psb[bk]], sig=(k == kk - 1))
            chunks = []
            for s in range(nseg):
                cs = []
                for j in range(nchunks):
                    rows = min(128, npw - j * 128)
                    bk = half + s * CPP + j
                    cs.append((n0 + j * 128, rows, psb[bk], psum[bk][0:rows, 0:tl]))
                chunks.append(cs)
            epi(n0, chunks, t0, tl)


def phase_norm(P, env, C, cfg, src, wcols, ssq_loc, ssq_mid, ssq_tot, h_loc=None, h_mid=None, hfull=None, out_f32=None):
    ar, psum, psb = env["ar"], env["psum"], env["psb"]
    T = cfg.T
    TT = 512
    ar.reset()
    xts = [(ar.f32(4 * TT), Buf()) for _ in range(2)]
    sqs = [(ar.f32(4 * TT), Buf()) for _ in range(2)]
    rows = [(ar.f32(TT, 1), Buf()) for _ in range(2)]
    bcs = [(ar.f32(TT), Buf()) for _ in range(2)]
    hts = [((ar.bf16(4 * TT) if out_f32 is None else ar.f32(4 * TT)), Buf()) for _ in range(2)]
    dbuf = Buf()
    for i, t0 in enumerate(range(0, T, TT)):
        tl = min(TT, T - t0)
        xt, xb = xts[i % 2]
        x3 = xt.rearrange("p (c t) -> p c t", t=TT)
        P.dma("sp", x3[:, :, 0:tl], src[:, t0:t0 + tl].rearrange("(c p) t -> p c t", p=128), writes=[xb])
        sq, sqb = sqs[i % 2]
        s3 = sq.rearrange("p (c t) -> p c t", t=TT)
        P.op("act", lambda e, o=s3[:, :, 0:tl], a=x3[:, :, 0:tl]: e.activation(o, a, AF.Square),
             reads=[xb], writes=[sqb])
        pb = psb[i % 2]
        for c in range(4):
            P.op("pe", lambda e, o=psum[i % 2][:, 0:tl], r=s3[:, c, 0:tl], c=c:
                 e.matmul(o, lhsT=C["ones"], rhs=r, start=(c == 0), stop=(c == 3)),
                 reads=[sqb, C["buf"]], writes=[pb], sig=(c == 3))
        rw, rwb = rows[i % 2]
        P.op("dve", lambda e, o=rw[0:1, 0:tl], a=psum[i % 2][0:1, 0:tl]: e.tensor_copy(o, a),
             reads=[pb], writes=[rwb])
        P.dma("sp", ssq_loc[0:1, t0:t0 + tl], rw[0:1, 0:tl], reads=[rwb], writes=[dbuf, dbuf_of(ssq_loc)])
    P.barrier()
    allreduce8(P, ssq_loc, ssq_mid, ssq_tot)
    for i, t0 in enumerate(range(0, T, TT)):
        tl = min(TT, T - t0)
        xt, xb = xts[i % 2]
        x3 = xt.rearrange("p (c t) -> p c t", t=TT)
        P.dma("sp", x3[:, :, 0:tl], src[:, t0:t0 + tl].rearrange("(c p) t -> p c t", p=128), writes=[xb])
        bc, bcb = bcs[i % 2]
        P.dma("sp", bc[:, 0:tl], ssq_tot[0:1, t0:t0 + tl].partition_broadcast(128), reads=[dbuf_of(ssq_tot)], writes=[bcb])
        P.op("dve", lambda e, a=bc[:, 0:tl]: e.tensor_scalar(a, a, 1.0 / D, EPS, ALU.mult, ALU.add),
             reads=[bcb], writes=[bcb])
        P.op("act", lambda e, a=bc[:, 0:tl]: e.activation(a, a, AF.Sqrt), reads=[bcb], writes=[bcb])
        P.op("dve", lambda e, a=bc[:, 0:tl]: e.reciprocal(a, a), reads=[bcb], writes=[bcb])
        ht, hb = hts[i % 2]
        h3 = ht.rearrange("p (c t) -> p c t", t=TT)
        for c in range(4):
            P.op("dve", lambda e, o=h3[:, c, 0:tl], a=x3[:, c, 0:tl], w=wcols[:, c:c + 1], b=bc[:, 0:tl]:
                 e.scalar_tensor_tensor(o, a, w, b, ALU.mult, ALU.mult), reads=[xb, bcb], writes=[hb])
        if out_f32 is not None:
            P.dma("sp", out_f32[:, t0:t0 + tl].rearrange("(c p) t -> p c t", p=128), h3[:, :, 0:tl],
                  reads=[hb], writes=[dbuf])
        else:
            for c in range(4):
                P.dma("sp", h_loc.span(c * 128, 128, t0, tl), h_loc.sview(h3[:, c, 0:tl]), reads=[hb], writes=[dbuf])
    P.barrier()


def copy_epilogue(P, dst, nbuf=4):
    st = {}

    def setup(ar):
        st["bufs"] = [(ar.f32(512), Buf()) for _ in range(nbuf)]
        st["i"] = 0
        st["d"] = Buf()

    def epi(n0, chunks, t0, tl):
        for (r0, rows, pb, pap) in chunks[0]:
            ot, otb = st["bufs"][st["i"] % nbuf]
            eng = "act" if st["i"] % 2 else "dve"
            st["i"] += 1
            if eng == "act":
                P.op("act", lambda e, o=ot[0:rows, 0:tl], a=pap: e.activation(o, a, AF.Copy),
                     reads=[pb], writes=[otb])
            else:
                P.op("dve", lambda e, o=ot[0:rows, 0:tl], a=pap: e.tensor_copy(o, a), reads=[pb], writes=[otb])
            P.dma("sp", dst[r0:r0 + rows, t0:t0 + tl], ot[0:rows, 0:tl], reads=[otb], writes=[st["d"]])
    return setup, epi


def phase_scal(P, env, C, cfg, proj, pv, scal):
    ar = env["ar"]
    ar.reset()
    L = cfg.SEQ
    mul = ar.f32(1, 8)
    mb = Buf()
    P.op("act", lambda e: e.activation(mul, pv[0:8, PV_SALOG:PV_SALOG + 1], AF.Exp), writes=[mb])
    P.op("dve", lambda e: e.tensor_scalar(mul, mul, -1.0, None, ALU.mult), reads=[mb], writes=[mb])
    ones8 = ar.f32(L, 8)
    ob = Buf()
    P.op("pool", lambda e: e.memset(ones8, 1.0), writes=[ob])
    raw, t, a, sp, ov, sg, cu = [ar.f32(L, 8) for _ in range(7)]
    bs = [Buf() for _ in range(7)]
    rb, tb, ab, spb, ovb, sgb, cub = bs
    dbuf = Buf()
    for b in range(cfg.NB):
        sl = slice(b * L, (b + 1) * L)
        P.dma("sp", raw, proj[PS:PS + 8, sl], writes=[rb])
        P.op("dve", lambda e: e.tensor_scalar(t, raw, pv[0:8, PV_SBIAS:PV_SBIAS + 1],
                                              pv[0:8, PV_SSGN:PV_SSGN + 1], ALU.add, ALU.mult),
             reads=[rb], writes=[tb])
        P.op("act", lambda e: e.activation(a, t, AF.Abs), reads=[tb], writes=[ab])
        P.op("act", lambda e: e.activation(a, a, AF.Exp, scale=-1.0), reads=[ab], writes=[ab])
        P.op("act", lambda e: e.activation(a, a, AF.Ln, bias=1.0), reads=[ab], writes=[ab])
        P.op("dve", lambda e: e.scalar_tensor_tensor(sp, t, 0.0, a, ALU.max, ALU.add),
             reads=[tb, ab], writes=[spb])
        P.op("dve", lambda e: e.tensor_scalar(ov, sp, mul, None, ALU.mult), reads=[spb, mb], writes=[ovb])
        P.op("act", lambda e: e.activation(sg, raw, AF.Sigmoid), reads=[rb], writes=[sgb])
        P.op("dve", lambda e: e.tensor_tensor_scan(cu, ones8, ov, 0.0, ALU.mult, ALU.add),
             reads=[ovb, ob], writes=[cub])
        for i, (src, sb) in enumerate(((sp, spb), (ov, ovb), (sg, sgb), (cu, cub))):
            P.dma("sp", scal[8 * i:8 * i + 8, sl], src, reads=[sb], writes=[dbuf])
    P.barrier()


def phase_fox(P, env, C, cfg, proj, scal, y_fox):
    ar, psum, psb = env["ar"], env["psum"], env["psb"]
    ar.reset()
    L = cfg.SEQ
    NBK = L // 128
    qkv = ar.f32(3 * L)
    qkvb = Buf()
    q3 = qkv.rearrange("p (c t) -> p c t", t=L)
    qs = ar.bf16(L)
    kb_ = ar.bf16(L)
    qsb, kbb = Buf(), Buf()
    vt = ar.bf16(L)
    vtb = Buf()
    cum = ar.f32(L, 8)
    cumb = Buf()
    cumT = ar.f32(NBK * 8)
    cumTb = Buf()
    c0 = ar.f32(NBK)
    c0b = Buf()
    bias = ar.f32(NBK * NBK)
    biasb = Buf()
    ybuf = ar.bf16(L)
    yb = Buf()
    NE = 4
    es = [(ar.bf16(128), Buf()) for _ in range(NE)]
    rden = [(ar.f32(128), Buf()) for _ in range(2)]
    dbuf = Buf()
    scale = 128.0 ** -0.5
    cb = C["buf"]
    ei = 0
    for b in range(cfg.NB):
        sl = slice(b * L, (b + 1) * L)
        P.dma("sp", q3, proj[FQ:FQ + 384, sl].rearrange("(c p) t -> p c t", p=128), writes=[qkvb])
        P.dma("sp", cum, scal[24:32, sl], writes=[cumb])
        P.op("act", lambda e: e.activation(qs, q3[:, 0, :], AF.Copy, scale=scale), reads=[qkvb], writes=[qsb])
        P.op("dve", lambda e: e.tensor_copy(kb_, q3[:, 1, :]), reads=[qkvb], writes=[kbb])
        for g0 in range(0, NBK, 4):
            bk = (g0 // 4) % 4
            gn = min(4, NBK - g0)
            for g in range(gn):
                blk = g0 + g
                P.op("pe", lambda e, o=psum[bk][:, g * 128:(g + 1) * 128], a=q3[:, 2, blk * 128:(blk + 1) * 128]:
                     e.transpose(o, a, C["ident"]), reads=[qkvb, cb], writes=[psb[bk]], sig=(g == gn - 1))
            eng = "act" if (g0 // 4) % 2 else "dve"
            if eng == "act":
                P.op("act", lambda e, o=vt[:, g0 * 128:(g0 + gn) * 128], a=psum[bk][:, 0:gn * 128]:
                     e.activation(o, a, AF.Copy), reads=[psb[bk]], writes=[vtb])
            else:
                P.op("dve", lambda e, o=vt[:, g0 * 128:(g0 + gn) * 128], a=psum[bk][:, 0:gn * 128]:
                     e.tensor_copy(o, a), reads=[psb[bk]], writes=[vtb])
        for blk in range(NBK):
            P.op("pe", lambda e, o=psum[4][:, blk * 8:(blk + 1) * 8], a=cum[0:8, blk * 128:(blk + 1) * 128]:
                 e.transpose(o, a, C["ident"][0:8, 0:8]), reads=[cumb, cb], writes=[psb[4]], sig=(blk == NBK - 1))
        P.op("dve", lambda e: e.tensor_copy(cumT, psum[4][:, 0:NBK * 8]), reads=[psb[4]], writes=[cumTb])
        P.op("pe", lambda e: e.matmul(psum[5][:, 0:NBK], lhsT=C["sel4"],
                                      rhs=cum.rearrange("p (b s) -> p b s", s=128)[:, :, 64],
                                      start=True, stop=True), reads=[cumb, cb], writes=[psb[5]])
        P.op("dve", lambda e: e.tensor_copy(c0, psum[5][:, 0:NBK]), reads=[psb[5]], writes=[c0b])
        cumT3 = cumT.rearrange("p (b s) -> p b s", s=8)
        for j in range(NBK):
            P.op("dve", lambda e, o=bias[:, j * NBK:(j + 1) * NBK], s1=c0[:, j:j + 1]:
                 e.tensor_scalar(o, cumT3[:, :, 4], s1, -1.0, ALU.subtract, ALU.mult),
                 reads=[cumTb, c0b], writes=[biasb])
        for j in range(NBK):
            ob_, db_ = 4 + (j % 2), 6 + (j % 2)
            for i in range(j + 1):
                sbk = ei % 4
                et, eb = es[ei % NE]
                ei += 1
                P.op("pe", lambda e, o=psum[sbk][:, 0:128], l=kb_[:, i * 128:(i + 1) * 128],
                     r=qs[:, j * 128:(j + 1) * 128]: e.matmul(o, lhsT=l, rhs=r, start=True, stop=True),
                     reads=[kbb, qsb], writes=[psb[sbk]])
                P.op("act", lambda e, o=et, a=psum[sbk][:, 0:128], bi=bias[:, j * NBK + i:j * NBK + i + 1]:
                     e.activation(o, a, AF.Exp, bias=bi), reads=[psb[sbk], biasb], writes=[eb])
                if i == j:
                    P.op("pool", lambda e, o=et: e.tensor_tensor(o, o, C["tri_bf"], ALU.mult),
                         reads=[eb, cb], writes=[eb])
                P.op("pe", lambda e, o=psum[ob_][:, 0:128], l=vt[:, i * 128:(i + 1) * 128], r=et, i=i, j=j:
                     e.matmul(o, lhsT=l, rhs=r, start=(i == 0), stop=(i == j)),
                     reads=[vtb, eb], writes=[psb[ob_]], sig=False)
                P.op("pe", lambda e, o=psum[db_][:, 0:128], r=et, i=i, j=j:
                     e.matmul(o, lhsT=C["ones_bf"], rhs=r, start=(i == 0), stop=(i == j)),
                     reads=[cb, eb], writes=[psb[db_]])
            rd, rdb = rden[j % 2]
            P.op("dve", lambda e, o=rd, a=psum[db_][:, 0:128]: e.reciprocal(o, a), reads=[psb[db_]], writes=[rdb])
            P.op("dve", lambda e, o=ybuf[:, j * 128:(j + 1) * 128], a=psum[ob_][:, 0:128], r_=rd:
                 e.tensor_tensor(o, a, r_, ALU.mult), reads=[psb[ob_], rdb], writes=[yb])
        P.dma("sp", y_fox.span(0, 128, b * L, L), y_fox.sview(ybuf), reads=[yb], writes=[dbuf])
    P.barrier()


def conv_silu(P, out3, raw3, nch, SEG, KW, pv, wcol0, bcol0, rawb, outb, out_dt_bf_from=None, outbf3=None, outbfb=None):
    for c in range(nch):
        w = lambda j, c=c: pv[:, wcol0 + c * KW + j:wcol0 + c * KW + j + 1]
        if bcol0 is not None:
            P.op("dve", lambda e, c=c: e.tensor_scalar(out3[:, c, :], raw3[:, c, KW - 1:KW - 1 + SEG], w(KW - 1),
                                                       pv[:, bcol0 + c:bcol0 + c + 1], ALU.mult, ALU.add),
                 reads=[rawb], writes=[outb])
        else:
            P.op("dve", lambda e, c=c: e.tensor_scalar(out3[:, c, :], raw3[:, c, KW - 1:KW - 1 + SEG], w(KW - 1),
                                                       None, ALU.mult), reads=[rawb], writes=[outb])
        for j in range(KW - 1):
            P.op("dve", lambda e, c=c, j=j: e.scalar_tensor_tensor(out3[:, c, :], raw3[:, c, j:j + SEG], w(j),
                                                                   out3[:, c, :], ALU.mult, ALU.add),
                 reads=[rawb, outb], writes=[outb])
        P.op("act", lambda e, c=c: e.activation(out3[:, c, :], out3[:, c, :], AF.Silu), reads=[outb], writes=[outb])


def load_with_halo(P, raw3, rawb, src_rows, t0, SEG, HALO, seq_start):
    if seq_start:
        P.op("pool", lambda e: e.memset(raw3[:, :, 0:HALO], 0.0), writes=[rawb])
        P.dma("sp", raw3[:, :, HALO:HALO + SEG], src_rows[:, t0:t0 + SEG].rearrange("(c p) t -> p c t", p=128),
              writes=[rawb])
    else:
        P.dma("sp", raw3[:, :, 0:HALO + SEG],
              src_rows[:, t0 - HALO:t0 + SEG].rearrange("(c p) t -> p c t", p=128), writes=[rawb])


def phase_ssd(P, env, C, cfg, proj, scal, pv, ypre, ssq_loc, ssq_tot, y_ssd, cc=True):
    ar, psum, psb = env["ar"], env["psum"], env["psb"]
    ar.reset()
    L = cfg.SEQ
    SEG = min(1024, L)
    NCH = SEG // 128
    cb = C["buf"]
    raw = ar.f32(4 * (SEG + 3))
    raw3 = raw.rearrange("p (c t) -> p c t", t=SEG + 3)
    rawb = Buf()
    xc = ar.f32(4 * SEG)
    xc3 = xc.rearrange("p (c t) -> p c t", t=SEG)
    xcb = Buf()
    bcbf = ar.bf16(2 * SEG)
    bc3 = bcbf.rearrange("p (c t) -> p c t", t=SEG)
    bcb = Buf()
    zs = ar.f32(2 * SEG)
    zs3 = zs.rearrange("p (c t) -> p c t", t=SEG)
    zsb = Buf()
    scs = ar.f32(2 * SEG, 8)
    scs3 = scs.rearrange("p (c t) -> p c t", t=SEG)
    scsb = Buf()
    hst = ar.f32(256)
    hst3 = hst.rearrange("p (h d) -> p h d", d=64)
    hstb = Buf()
    hbf = ar.bf16(256)
    hbfb = Buf()
    yseg = ar.f32(2 * SEG)
    yseg3 = yseg.rearrange("p (c t) -> p c t", t=SEG)
    ysegb = Buf()
    rowseg = ar.f32(SEG, 1)
    rowb = Buf()
    dts = ar.f32(16)
    dtsb = Buf()
    acs = ar.f32(4)
    acsb = Buf()
    lbc = ar.f32(512)
    lbcb = Buf()
    dec = ar.f32(512)
    dec3 = dec.rearrange("p (h t) -> p h t", t=128)
    decb = Buf()
    cbm = ar.f32(128)
    cbmb = Buf()
    mt = ar.bf16(512)
    mt3 = mt.rearrange("p (h t) -> p h t", t=128)
    mtb = Buf()
    ebc = ar.f32(512)
    ebc3 = ebc.rearrange("p (h t) -> p h t", t=128)
    ebcb = Buf()
    ce = ar.bf16(512)
    ce3 = ce.rearrange("p (h t) -> p h t", t=128)
    ceb = Buf()
    xdt = ar.bf16(256)
    xdt3 = xdt.rearrange("p (h d) -> p h d", d=64)
    xdtb = Buf()
    btok = ar.bf16(128)
    btokb = Buf()
    dd = ar.f32(8)
    ddb = Buf()
    xdec = ar.bf16(256)
    xdec3 = xdec.rearrange("p (h d) -> p h d", d=64)
    xdecb = Buf()
    sq = ar.f32(256)
    sqb = Buf()
    dbuf = Buf()
    ident8 = C["ident"][0:8, 0:8]
    for b in range(cfg.NB):
        P.op("pool", lambda e: e.memset(hst, 0.0), writes=[hstb])
        P.op("pool", lambda e: e.memset(hbf, 0.0), writes=[hbfb])
        for s0 in range(0, L, SEG):
            t0 = b * L + s0
            load_with_halo(P, raw3, rawb, proj[PX:PX + 512, :], t0, SEG, 3, s0 == 0)
            P.dma("sp", zs3, proj[PZ:PZ + 256, t0:t0 + SEG].rearrange("(c p) t -> p c t", p=128), writes=[zsb])
            P.dma("sp", scs3[:, 0, :], scal[0:8, t0:t0 + SEG], writes=[scsb])
            P.dma("sp", scs3[:, 1, :], scal[8:16, t0:t0 + SEG], writes=[scsb])
            conv_silu(P, xc3, raw3, 4, SEG, 4, pv, PV_SCW, PV_SCB, rawb, xcb)
            P.op("pool", lambda e: e.tensor_copy(bc3, xc3[:, 2:4, :]), reads=[xcb], writes=[bcb])
            P.op("act", lambda e: e.activation(zs, zs, AF.Silu), reads=[zsb], writes=[zsb])
            for ch in range(NCH):
                o = ch * 128
                osl = slice(o, o + 128)
                P.op("pe", lambda e: e.transpose(psum[0][:, 0:8], scs3[:, 0, osl], ident8),
                     reads=[scsb, cb], writes=[psb[0]], sig=False)
                P.op("pe", lambda e: e.transpose(psum[0][:, 8:16], scs3[:, 1, osl], ident8),
                     reads=[scsb, cb], writes=[psb[0]])
                P.op("dve", lambda e: e.tensor_copy(dts, psum[0][:, 0:16]), reads=[psb[0]], writes=[dtsb])
                P.op("pe", lambda e: e.matmul(psum[0][:, 16:20], lhsT=C["tri"], rhs=dts[:, 8:12], start=True, stop=True),
                     reads=[dtsb, cb], writes=[psb[0]])
                P.op("dve", lambda e: e.tensor_copy(acs, psum[0][:, 16:20]), reads=[psb[0]], writes=[acsb])
                for h in range(4):
                    P.op("act", lambda e, h=h: e.activation(lbc[:, h * 128:(h + 1) * 128], C["ones"], AF.Copy,
                                                            scale=dts[:, 8 + h:9 + h]),
                         reads=[dtsb, cb], writes=[lbcb])
                for h in range(4):
                    P.op("pe", lambda e, h=h: e.matmul(psum[2][:, h * 128:(h + 1) * 128],
                                                       lhsT=lbc[:, h * 128:(h + 1) * 128], rhs=C["tri"],
                                                       start=True, stop=True),
                         reads=[lbcb, cb], writes=[psb[2]], sig=(h == 3))
                for h in range(4):
                    P.op("dve", lambda e, h=h: e.tensor_scalar(dec[:, h * 128:(h + 1) * 128],
                                                               psum[2][:, h * 128:(h + 1) * 128],
                                                               acs[:, h:h + 1], 0.0, ALU.subtract, ALU.min),
                         reads=[psb[2], acsb], writes=[decb])
                P.op("act", lambda e: e.activation(dec, dec, AF.Exp), reads=[decb], writes=[decb])
                P.op("pe", lambda e: e.matmul(psum[3][:, 0:128], lhsT=bc3[:, 0, osl], rhs=bc3[:, 1, osl],
                                              start=True, stop=True), reads=[bcb], writes=[psb[3]])
                P.op("dve", lambda e: e.tensor_tensor(cbm, psum[3][:, 0:128], C["tri"], ALU.mult),
                     reads=[psb[3], cb], writes=[cbmb])
                P.op("dve", lambda e: e.tensor_tensor(mt3, dec3, cbm.unsqueeze(1).to_broadcast([128, 4, 128]), ALU.mult),
                     reads=[decb, cbmb], writes=[mtb])
                P.op("act", lambda e: e.activation(ebc, psum[2][:, 0:512], AF.Exp), reads=[psb[2]], writes=[ebcb])
                P.op("pool", lambda e: e.tensor_tensor(ce3, ebc3, bc3[:, 1, osl].unsqueeze(1).to_broadcast([128, 4, 128]),
                                                       ALU.mult), reads=[ebcb, bcb], writes=[ceb])
                for k in range(3):
                    P.op("pe", lambda e, k=k: e.transpose(psum[1][:, k * 128:(k + 1) * 128], xc3[:, k, osl], C["ident"]),
                         reads=[xcb, cb], writes=[psb[1]], sig=(k == 2))
                P.op("dve", lambda e: e.tensor_tensor(xdt3, psum[1][:, 0:256].rearrange("p (h d) -> p h d", d=64),
                                                      dts[:, 0:4].unsqueeze(2).to_broadcast([128, 4, 64]), ALU.mult),
                     reads=[psb[1], dtsb], writes=[xdtb])
                P.op("act", lambda e: e.activation(btok, psum[1][:, 256:384], AF.Copy), reads=[psb[1]], writes=[btokb])
                ab3 = psum[2][:, 0:512].rearrange("p (h t) -> p h t", t=128)
                P.op("dve", lambda e: e.tensor_tensor(dd[:, 0:4], ab3[:, :, 127], acs, ALU.subtract),
                     reads=[psb[2], acsb], writes=[ddb])
                P.op("dve", lambda e: e.tensor_copy(dd[:, 4:8], ab3[:, :, 127]), reads=[psb[2]], writes=[ddb])
                P.op("act", lambda e: e.activation(dd, dd, AF.Exp), reads=[ddb], writes=[ddb])
                P.op("pool", lambda e: e.tensor_tensor(xdec3, xdt3, dd[:, 0:4].unsqueeze(2).to_broadcast([128, 4, 64]),
                                                       ALU.mult), reads=[xdtb, ddb], writes=[xdecb])
                for h in range(4):
                    po = psum[4][(h % 2) * 64:(h % 2) * 64 + 64, (h // 2) * 128:(h // 2 + 1) * 128]
                    P.op("pe", lambda e, h=h, po=po: e.matmul(po, lhsT=xdt[:, h * 64:(h + 1) * 64], rhs=mt3[:, h, :],
                                                              start=True, stop=False),
                         reads=[xdtb, mtb], writes=[psb[4]], sig=False)
                    P.op("pe", lambda e, h=h, po=po: e.matmul(po, lhsT=hbf[:, h * 64:(h + 1) * 64], rhs=ce3[:, h, :],
                                                              start=False, stop=True),
                         reads=[hbfb, ceb], writes=[psb[4]], sig=(h == 3))
                P.op("pe", lambda e: e.matmul(psum[5][:, 0:256], lhsT=btok, rhs=xdec, start=True, stop=True),
                     reads=[btokb, xdecb], writes=[psb[5]])
                P.op("dve", lambda e: e.tensor_tensor(hst3, hst3, dd[:, 4:8].unsqueeze(2).to_broadcast([128, 4, 64]),
                                                      ALU.mult), reads=[ddb, hstb], writes=[hstb])
                P.op("dve", lambda e: e.tensor_tensor(hst, hst, psum[5][:, 0:256], ALU.add),
                     reads=[psb[5], hstb], writes=[hstb])
                P.op("act", lambda e: e.activation(hbf, hst, AF.Copy), reads=[hstb], writes=[hbfb])
                for k in range(2):
                    P.op("dve", lambda e, k=k: e.scalar_tensor_tensor(yseg3[:, k, osl], xc3[:, k, osl],
                                                                      pv[:, PV_SD + k:PV_SD + k + 1],
                                                                      psum[4][:, k * 128:(k + 1) * 128], ALU.mult, ALU.add),
                         reads=[xcb, psb[4]], writes=[ysegb])
                    P.op("pool", lambda e, k=k: e.tensor_tensor(yseg3[:, k, osl], yseg3[:, k, osl], zs3[:, k, osl], ALU.mult),
                         reads=[ysegb, zsb], writes=[ysegb])
                    P.op("act", lambda e, k=k: e.activation(sq[:, k * 128:(k + 1) * 128], yseg3[:, k, osl], AF.Square),
                         reads=[ysegb], writes=[sqb])
                for k in range(2):
                    P.op("pe", lambda e, k=k: e.matmul(psum[6][:, 0:128], lhsT=C["ones"], rhs=sq[:, k * 128:(k + 1) * 128],
                                                       start=(k == 0), stop=(k == 1)),
                         reads=[sqb, cb], writes=[psb[6]], sig=(k == 1))
                P.op("dve", lambda e: e.tensor_copy(rowseg[0:1, osl], psum[6][0:1, 0:128]), reads=[psb[6]], writes=[rowb])
            P.dma("sp", ypre[0:256, t0:t0 + SEG].rearrange("(c p) t -> p c t", p=128), yseg3, reads=[ysegb], writes=[dbuf])
            P.dma("sp", ssq_loc[0:1, t0:t0 + SEG], rowseg[0:1, :], reads=[rowb], writes=[dbuf, dbuf_of(ssq_loc)])
    P.barrier()
    if cc:
        P.collective("AllReduce", ALU.add, [[2 * i, 2 * i + 1] for i in range(NCORES // 2)],
                     ssq_loc.ap().opt(), ssq_tot.ap().opt(), reads=[dbuf_of(ssq_loc)], writes=[dbuf_of(ssq_tot)])
    else:
        P.dma("sp", ssq_tot.ap(), ssq_loc.ap(), reads=[dbuf_of(ssq_loc)], writes=[dbuf_of(ssq_tot)])
    ar.reset()
    TT = 512
    yts = [(ar.f32(2 * TT), Buf()) for _ in range(2)]
    bcs = [(ar.f32(TT), Buf()) for _ in range(2)]
    ots = [(ar.bf16(2 * TT), Buf()) for _ in range(2)]
    for i, t0 in enumerate(range(0, cfg.T, TT)):
        yt, ytb = yts[i % 2]
        y3 = yt.rearrange("p (c t) -> p c t", t=TT)
        P.dma("sp", y3, ypre[0:256, t0:t0 + TT].rearrange("(c p) t -> p c t", p=128), writes=[ytb])
        bc, bcb_ = bcs[i % 2]
        P.dma("sp", bc, ssq_tot[0:1, t0:t0 + TT].partition_broadcast(128), reads=[dbuf_of(ssq_tot)], writes=[bcb_])
        P.op("dve", lambda e, a=bc: e.tensor_scalar(a, a, 1.0 / 512.0, EPS, ALU.mult, ALU.add), reads=[bcb_], writes=[bcb_])
        P.op("act", lambda e, a=bc: e.activation(a, a, AF.Sqrt), reads=[bcb_], writes=[bcb_])
        P.op("dve", lambda e, a=bc: e.reciprocal(a, a), reads=[bcb_], writes=[bcb_])
        ot, otb = ots[i % 2]
        o3 = ot.rearrange("p (c t) -> p c t", t=TT)
        for k in range(2):
            P.op("dve", lambda e, k=k, o3=o3, y3=y3, bc=bc: e.scalar_tensor_tensor(
                o3[:, k, :], y3[:, k, :], pv[:, PV_SNORM + k:PV_SNORM + k + 1], bc, ALU.mult, ALU.mult),
                reads=[ytb, bcb_], writes=[otb])
        for k in range(2):
            P.dma("sp", y_ssd.span(k * 128, 128, t0, TT), y_ssd.sview(o3[:, k, :]), reads=[otb], writes=[dbuf])
    P.barrier()


def phase_gdn(P, env, C, cfg, proj, scal, pv, y_gdn):
    ar, psum, psb = env["ar"], env["psum"], env["psb"]
    ar.reset()
    L = cfg.SEQ
    SEG = min(1024, L)
    NCH = SEG // 64
    cb = C["buf"]
    ident, ones, tri, lows = C["ident"], C["ones"], C["tri"], C["lows"]
    i64 = ident[0:64, 0:64]
    raw = ar.f32(3 * (SEG + 3))
    raw3 = raw.rearrange("p (c t) -> p c t", t=SEG + 3)
    rawb = Buf()
    qkv = ar.f32(3 * SEG)
    qkv3 = qkv.rearrange("p (c t) -> p c t", t=SEG)
    qkvb = Buf()
    zs = ar.f32(SEG)
    zsb = Buf()
    scs = ar.f32(2 * SEG, 8)
    scs3 = scs.rearrange("p (c t) -> p c t", t=SEG)
    scsb = Buf()
    sq = ar.f32(512)
    sqb = Buf()
    rs = ar.f32(512)
    rsb = Buf()
    oseg = ar.f32(SEG)
    osegb = Buf()
    yo = ar.bf16(SEG)
    yob = Buf()
    S = ar.f32(128)
    Sb = Buf()
    gb = ar.f32(24, 64)
    gbb = Buf()
    gl = ar.f32(128, 64)
    glb = Buf()
    gct = ar.f32(2, 64)
    gctb = Buf()
    d1 = ar.f32(64, 64)
    d1b = Buf()
    d2 = ar.f32(64, 64)
    d2b = Buf()
    ebc = ar.f32(64)
    ebcb = Buf()
    dl = ar.f32(1, 64)
    dlb = Buf()
    t1 = ar.f32(64, 64)
    t1b = Buf()
    nnt = [(ar.f32(128, 64), Buf()) for _ in range(2)]
    rts = [(ar.f32(64, 64), Buf()) for _ in range(2)]
    t2 = ar.f32(64, 64)
    t2b = Buf()
    attnT = ar.f32(64, 64)
    attnb = Buf()
    kg = ar.f32(64)
    kgb = Buf()
    qd = ar.f32(64)
    qdb = Buf()
    kdec = ar.f32(128, 64)
    kdecb = Buf()
    vb = ar.f32(128, 64)
    vbb = Buf()
    X = ar.f32(128, 64)
    Xb = Buf()
    vn = ar.f32(128, 64)
    vnb = Buf()
    dbuf = Buf()
    ident8 = ident[0:8, 0:8]
    scale = 128.0 ** -0.5
    for b in range(cfg.NB):
        P.op("pool", lambda e: e.memset(S, 0.0), writes=[Sb])
        for s0 in range(0, L, SEG):
            t0 = b * L + s0
            load_with_halo(P, raw3, rawb, proj[GQ:GQ + 384, :], t0, SEG, 3, s0 == 0)
            P.dma("sp", zs, proj[GZ:GZ + 128, t0:t0 + SEG], writes=[zsb])
            P.dma("sp", scs3[:, 0, :], scal[8:16, t0:t0 + SEG], writes=[scsb])
            P.dma("sp", scs3[:, 1, :], scal[16:24, t0:t0 + SEG], writes=[scsb])
            conv_silu(P, qkv3, raw3, 3, SEG, 4, pv, PV_GCW, None, rawb, qkvb)
            P.op("act", lambda e: e.activation(zs, zs, AF.Silu), reads=[zsb], writes=[zsb])
            for c in range(2):
                for u0 in range(0, SEG, 512):
                    ul = min(512, SEG - u0)
                    xs_ = qkv3[:, c, u0:u0 + ul]
                    P.op("act", lambda e: e.activation(sq[:, 0:ul], xs_, AF.Square), reads=[qkvb], writes=[sqb])
                    P.op("pe", lambda e: e.matmul(psum[7][:, 0:ul], lhsT=ones, rhs=sq[:, 0:ul], start=True, stop=True),
                         reads=[sqb, cb], writes=[psb[7]])
                    P.op("dve", lambda e: e.tensor_scalar(rs[:, 0:ul], psum[7][:, 0:ul], 1.0, 1e-6, ALU.mult, ALU.add),
                         reads=[psb[7]], writes=[rsb])
                    P.op("act", lambda e: e.activation(rs[:, 0:ul], rs[:, 0:ul], AF.Sqrt), reads=[rsb], writes=[rsb])
                    P.op("dve", lambda e: e.reciprocal(rs[:, 0:ul], rs[:, 0:ul]), reads=[rsb], writes=[rsb])
                    P.op("dve", lambda e: e.scalar_tensor_tensor(xs_, xs_, (scale if c == 0 else 1.0), rs[:, 0:ul],
                                                                 ALU.mult, ALU.mult), reads=[rsb, qkvb], writes=[qkvb])
            for ch in range(NCH):
                o = ch * 64
                osl = slice(o, o + 64)
                qc, kc, vc = qkv3[:, 0, osl], qkv3[:, 1, osl], qkv3[:, 2, osl]
                P.op("pe", lambda e: e.transpose(psum[0][0:64, 0:8], scs3[:, 0, osl], ident8),
                     reads=[scsb, cb], writes=[psb[0]], sig=False)
                P.op("pe", lambda e: e.transpose(psum[0][0:64, 8:16], scs3[:, 1, osl], ident8),
                     reads=[scsb, cb], writes=[psb[0]])
                P.op("dve", lambda e: e.tensor_copy(gb[:, 0:16], psum[0][0:64, 0:16]), reads=[psb[0]], writes=[gbb])
                P.op("dve", lambda e: e.tensor_scalar(gb[:, 16:17], gb[:, 14:15], -1.0, None, ALU.mult),
                     reads=[gbb], writes=[gbb])
                g_, beta, nbeta = gb[:, 5:6], gb[:, 14:15], gb[:, 16:17]
                P.op("act", lambda e: e.activation(gl, ones[0:64, :], AF.Copy, scale=g_), reads=[gbb, cb], writes=[glb])
                P.op("pe", lambda e: e.matmul(psum[0][0:64, 16:18], lhsT=tri[0:64, 0:64], rhs=gb[:, 4:6],
                                              start=True, stop=True), reads=[gbb, cb], writes=[psb[0]], sig=False)
                P.op("pe", lambda e: e.matmul(psum[0][:, 32:96], lhsT=gl, rhs=tri[0:64, 0:64], start=True, stop=True),
                     reads=[glb, cb], writes=[psb[0]])
                gbc = psum[0][:, 32:96]
                P.op("dve", lambda e: e.tensor_copy(gct, psum[0][0:64, 16:18]), reads=[psb[0]], writes=[gctb])
                gc = gct[:, 1:2]
                P.op("dve", lambda e: e.tensor_scalar(d1, gbc[0:64, :], gc, 0.0, ALU.subtract, ALU.max),
                     reads=[psb[0], gctb], writes=[d1b])
                P.op("act", lambda e: e.activation(d1, d1, AF.Exp, scale=-1.0), reads=[d1b], writes=[d1b])
                P.op("dve", lambda e: e.tensor_scalar(d2, gbc[0:64, :], gc, 0.0, ALU.subtract, ALU.min),
                     reads=[psb[0], gctb], writes=[d2b])
                P.op("act", lambda e: e.activation(d2, d2, AF.Exp), reads=[d2b], writes=[d2b])
                P.op("act", lambda e: e.activation(ebc, gbc, AF.Exp), reads=[psb[0]], writes=[ebcb])
                P.op("dve", lambda e: e.tensor_tensor(dl, gbc[0:64, 63:64], gc, ALU.subtract),
                     reads=[psb[0], gctb], writes=[dlb])
                P.op("act", lambda e: e.activation(dl, dl, AF.Exp), reads=[dlb], writes=[dlb])
                P.op("pe", lambda e: e.matmul(psum[1][0:64, 0:64], lhsT=kc, rhs=kc, start=True, stop=True),
                     reads=[qkvb], writes=[psb[1]], sig=False)
                P.op("pe", lambda e: e.matmul(psum[1][0:64, 64:128], lhsT=kc, rhs=qc, start=True, stop=True),
                     reads=[qkvb], writes=[psb[1]])
                P.op("dve", lambda e: e.tensor_tensor(t1, psum[1][0:64, 0:64], d1, ALU.mult),
                     reads=[psb[1], d1b], writes=[t1b])
                nn0, nn0b = nnt[0]
                P.op("dve", lambda e: e.scalar_tensor_tensor(nn0[:, 0:64], t1, nbeta, lows[0:64, 0:64], ALU.mult, ALU.mult),
                     reads=[t1b, gbb, cb], writes=[nn0b])
                P.op("dve", lambda e: e.tensor_tensor(t2, psum[1][0:64, 64:128], d2, ALU.mult),
                     reads=[psb[1], d2b], writes=[t2b])
                P.op("pool", lambda e: e.tensor_tensor(attnT, t2, tri[0:64, 0:64], ALU.mult),
                     reads=[t2b, cb], writes=[attnb])
                P.op("pe", lambda e: e.transpose(psum[1][0:64, 128:192], nn0[:, 0:64], i64),
                     reads=[nn0b, cb], writes=[psb[1]])
                P.op("act", lambda e: e.activation(nn0[:, 64:128], psum[1][0:64, 128:192], AF.Copy),
                     reads=[psb[1]], writes=[nn0b])
                rt0, rt0b = rts[0]
                P.op("dve", lambda e: e.tensor_tensor(rt0, nn0[:, 64:128], i64, ALU.add), reads=[nn0b, cb], writes=[rt0b])
                for k in range(1, 6):
                    pn, pnb = nnt[(k - 1) % 2]
                    cn, cnb = nnt[k % 2]
                    pr, prb = rts[(k - 1) % 2]
                    cr, crb = rts[k % 2]
                    last = (k == 5)
                    P.op("pe", lambda e: e.matmul(psum[2][0:64, 0:64], lhsT=pn[:, 64:128], rhs=pn[:, 0:64],
                                                  start=True, stop=True), reads=[pnb], writes=[psb[2]], sig=last)
                    if not last:
                        P.op("pe", lambda e: e.matmul(psum[2][0:64, 64:128], lhsT=pn[:, 0:64], rhs=pn[:, 64:128],
                                                      start=True, stop=True), reads=[pnb], writes=[psb[2]])
                    w_ = 64 if last else 128
                    P.op("act", lambda e: e.activation(cn[:, 0:w_], psum[2][0:64, 0:w_], AF.Copy),
                         reads=[psb[2]], writes=[cnb])
                    P.op("pe", lambda e: e.matmul(psum[3][0:64, 0:64], lhsT=cn[:, 0:64], rhs=pr, start=True, stop=True),
                         reads=[cnb, prb], writes=[psb[3]])
                    P.op("dve", lambda e: e.tensor_tensor(cr, pr, psum[3][0:64, 0:64], ALU.add),
                         reads=[psb[3], prb], writes=[crb])
                rT, rTb = rts[5 % 2]
                P.op("dve", lambda e: e.tensor_tensor(kg, kc, ebc, ALU.mult), reads=[qkvb, ebcb], writes=[kgb])
                P.op("pool", lambda e: e.tensor_tensor(qd, qc, ebc, ALU.mult), reads=[qkvb, ebcb], writes=[qdb])
                P.op("pe", lambda e: e.transpose(psum[4][0:64, 0:128], kc, ident), reads=[qkvb, cb], writes=[psb[4]], sig=False)
                P.op("pe", lambda e: e.transpose(psum[4][0:64, 128:256], vc, ident), reads=[qkvb, cb], writes=[psb[4]])
                P.op("act", lambda e: e.activation(kdec, psum[4][0:64, 0:128], AF.Copy, scale=dl),
                     reads=[psb[4], dlb], writes=[kdecb])
                P.op("act", lambda e: e.activation(vb, psum[4][0:64, 128:256], AF.Copy, scale=beta),
                     reads=[psb[4], gbb], writes=[vbb])
                P.op("pe", lambda e: e.matmul(psum[5][0:64, 0:128], lhsT=kg, rhs=S, start=True, stop=True),
                     reads=[kgb, Sb], writes=[psb[5]])
                P.op("dve", lambda e: e.scalar_tensor_tensor(X, psum[5][0:64, 0:128], nbeta, vb, ALU.mult, ALU.add),
                     reads=[psb[5], gbb, vbb], writes=[Xb])
                P.op("pe", lambda e: e.matmul(psum[5][0:64, 128:256], lhsT=rT, rhs=X, start=True, stop=True),
                     reads=[rTb, Xb], writes=[psb[5]])
                P.op("act", lambda e: e.activation(vn, psum[5][0:64, 128:256], AF.Copy), reads=[psb[5]], writes=[vnb])
                P.op("pe", lambda e: e.matmul(psum[6][:, 0:64], lhsT=S, rhs=qd, start=True, stop=False),
                     reads=[Sb, qdb], writes=[psb[6]], sig=False)
                P.op("pe", lambda e: e.matmul(psum[6][:, 0:64], lhsT=vn, rhs=attnT, start=False, stop=True),
                     reads=[vnb, attnb], writes=[psb[6]])
                P.op("act", lambda e: e.activation(oseg[:, osl], psum[6][:, 0:64], AF.Copy), reads=[psb[6]], writes=[osegb])
                P.op("pe", lambda e: e.matmul(psum[7][:, 0:128], lhsT=kdec, rhs=vn, start=True, stop=True),
                     reads=[kdecb, vnb], writes=[psb[7]])
                P.op("dve", lambda e: e.scalar_tensor_tensor(S, S, ebc[:, 63:64], psum[7][:, 0:128], ALU.mult, ALU.add),
                     reads=[psb[7], ebcb, Sb], writes=[Sb])
            for u0 in range(0, SEG, 512):
                ul = min(512, SEG - u0)
                usl = slice(u0, u0 + ul)
                P.op("act", lambda e: e.activation(sq[:, 0:ul], oseg[:, usl], AF.Square), reads=[osegb], writes=[sqb])
                P.op("pe", lambda e: e.matmul(psum[7][:, 0:ul], lhsT=ones, rhs=sq[:, 0:ul], start=True, stop=True),
                     reads=[sqb, cb], writes=[psb[7]])
                P.op("dve", lambda e: e.tensor_scalar(rs[:, 0:ul], psum[7][:, 0:ul], 1.0 / 128.0, EPS, ALU.mult, ALU.add),
                     reads=[psb[7]], writes=[rsb])
                P.op("act", lambda e: e.activation(rs[:, 0:ul], rs[:, 0:ul], AF.Sqrt), reads=[rsb], writes=[rsb])
                P.op("dve", lambda e: e.reciprocal(rs[:, 0:ul], rs[:, 0:ul]), reads=[rsb], writes=[rsb])
                P.op("dve", lambda e: e.scalar_tensor_tensor(oseg[:, usl], oseg[:, usl], pv[:, PV_GNORM:PV_GNORM + 1],
                                                             rs[:, 0:ul], ALU.mult, ALU.mult),
                     reads=[rsb, osegb], writes=[osegb])
                P.op("pool", lambda e: e.tensor_tensor(yo[:, usl], oseg[:, usl], zs[:, usl], ALU.mult),
                     reads=[osegb, zsb], writes=[yob])
            P.dma("sp", y_gdn.span(0, 128, t0, SEG), y_gdn.sview(yo), reads=[yob], writes=[dbuf])
    P.barrier()


def phase_branch(P, env, cfg, w_br, yfs, proj, m_loc, pre=None):
    st = {}
    segs = [(w_br[0:2048, :], yfs[0], 2048), (w_br[2048:3072, :], yfs[1], 1024), (w_br[3072:4096, :], yfs[2], 1024)]
    gate_rows = (GA_SSM, GA_FOX, GA_GDN)

    def setup(ar):
        st["g"] = [(ar.f32(3 * 512), Buf()) for _ in range(2)]
        st["acc"] = [(ar.f32(512), Buf()) for _ in range(2)]
        st["o"] = [(ar.bf16(512), Buf()) for _ in range(2)]
        st["i"] = 0
        st["d"] = Buf()

    def epi(n0, chunks, t0, tl):
        for j in range(len(chunks[0])):
            i = st["i"]
            st["i"] += 1
            r0, rows, _, _ = chunks[0][j]
            g, gb = st["g"][i % 2]
            g3 = g.rearrange("p (s t) -> p s t", t=512)
            for s in range(3):
                P.dma("sp", g3[:, s, 0:tl], proj[gate_rows[s] + r0:gate_rows[s] + r0 + 128, t0:t0 + tl], writes=[gb])
            P.op("act", lambda e: e.activation(g3[:, :, 0:tl], g3[:, :, 0:tl], AF.Sigmoid), reads=[gb], writes=[gb])
            acc, accb = st["acc"][i % 2]
            P.op("dve", lambda e: e.tensor_tensor(acc[:, 0:tl], chunks[0][j][3], g3[:, 0, 0:tl], ALU.mult),
                 reads=[chunks[0][j][2], gb], writes=[accb])
            P.op("dve", lambda e: e.tensor_tensor(g3[:, 1, 0:tl], chunks[1][j][3], g3[:, 1, 0:tl], ALU.mult),
                 reads=[chunks[1][j][2], gb], writes=[gb])
            P.op("dve", lambda e: e.tensor_tensor(g3[:, 2, 0:tl], chunks[2][j][3], g3[:, 2, 0:tl], ALU.mult),
                 reads=[chunks[2][j][2], gb], writes=[gb])
            P.op("dve", lambda e: e.tensor_tensor(acc[:, 0:tl], acc[:, 0:tl], g3[:, 1, 0:tl], ALU.add),
                 reads=[gb, accb], writes=[accb])
            o, ob = st["o"][i % 2]
            P.op("dve", lambda e: e.tensor_tensor(o[:, 0:tl], acc[:, 0:tl], g3[:, 2, 0:tl], ALU.add),
                 reads=[gb, accb], writes=[ob])
            P.dma("sp", m_loc.span(r0, 128, t0, tl), m_loc.sview(o[:, 0:tl]), reads=[ob], writes=[st["d"]])
    gemm(P, env, segs, 512, cfg.T, epi, setup=setup, NPW=256, pre=pre)
    P.barrier()


def phase_resid(P, env, cfg, W, K, A, xsrc, xdst, pre=None):
    st = {}

    def setup(ar):
        st["x"] = [(ar.f32(512), Buf()) for _ in range(4)]
        st["i"] = 0
        st["d"] = Buf()

    def epi(n0, chunks, t0, tl):
        for (r0, rows, pb, pap) in chunks[0]:
            i = st["i"]
            st["i"] += 1
            x, xb = st["x"][i % 4]
            P.dma("sp", x[:, 0:tl], xsrc[r0:r0 + 128, t0:t0 + tl], writes=[xb])
            P.op("dve", lambda e: e.tensor_tensor(x[:, 0:tl], x[:, 0:tl], pap, ALU.add), reads=[pb, xb], writes=[xb])
            P.dma("sp", xdst[r0:r0 + 128, t0:t0 + tl], x[:, 0:tl], reads=[xb], writes=[st["d"]])
    gemm(P, env, [(W, A, K)], 512, cfg.T, epi, setup=setup, pre=pre)
    P.barrier()


def phase_ffn_up(P, env, cfg, W, hfull, pv, act_loc, pre=None):
    st = {}
    L = cfg.SEQ

    def setup(ar):
        st["u"] = [(ar.f32(514), Buf()) for _ in range(4)]
        st["carry"] = [(ar.f32(2), Buf()) for _ in range(4)]
        st["y"] = [(ar.f32(512), Buf()) for _ in range(4)]
        st["o"] = [(ar.bf16(512), Buf()) for _ in range(2)]
        st["d"] = Buf()
        st["oi"] = 0

    def epi(n0, chunks, t0, tl):
        cs = chunks[0]
        nc_ = len(cs)
        half = nc_ // 2
        p = n0 // 512
        ys = []
        for j, (r0, rows, pb, pap) in enumerate(cs):
            is_val = j >= half
            cidx = 2 * p + (j - half if is_val else j)
            pidx = (11 if is_val else 0) + cidx
            u, ub = st["u"][j]
            cr, crb = st["carry"][j]
            if t0 % L == 0:
                P.op("dve", lambda e: e.memset(u[:, 0:2], 0.0), writes=[ub])
            else:
                P.op("dve", lambda e: e.tensor_copy(u[:, 0:2], cr), reads=[crb], writes=[ub])
            P.op("act", lambda e: e.activation(u[:, 2:2 + tl], pap, AF.Copy), reads=[pb], writes=[ub])
            P.op("act", lambda e: e.activation(cr, u[:, tl:tl + 2], AF.Copy), reads=[ub], writes=[crb])
            y, yb = st["y"][j]
            w = lambda k: pv[:, PV_FCW + pidx * 3 + k:PV_FCW + pidx * 3 + k + 1]
            P.op("dve", lambda e: e.tensor_scalar(y[:, 0:tl], u[:, 2:2 + tl], w(2), pv[:, PV_FCB + pidx:PV_FCB + pidx + 1],
                                                  ALU.mult, ALU.add), reads=[ub], writes=[yb])
            P.op("dve", lambda e: e.scalar_tensor_tensor(y[:, 0:tl], u[:, 1:1 + tl], w(1), y[:, 0:tl], ALU.mult, ALU.add),
                 reads=[ub, yb], writes=[yb])
            P.op("dve", lambda e: e.scalar_tensor_tensor(y[:, 0:tl], u[:, 0:tl], w(0), y[:, 0:tl], ALU.mult, ALU.add),
                 reads=[ub, yb], writes=[yb])
            ys.append((y, yb, cidx))
        for j in range(half):
            gy, gyb, cidx = ys[j]
            vy, vyb, _ = ys[half + j]
            P.op("act", lambda e: e.activation(gy[:, 0:tl], gy[:, 0:tl], AF.Silu), reads=[gyb], writes=[gyb])
            o, ob = st["o"][st["oi"] % 2]
            st["oi"] += 1
            P.op("dve", lambda e: e.tensor_tensor(o[:, 0:tl], gy[:, 0:tl], vy[:, 0:tl], ALU.mult),
                 reads=[gyb, vyb], writes=[ob])
            rows = min(128, FFL - cidx * 128)
            P.dma("sp", act_loc.span(cidx * 128, rows, t0, tl), act_loc.sview(o[0:rows, 0:tl]), reads=[ob], writes=[st["d"]])
    gemm(P, env, [(W, hfull, D)], NUPW, cfg.T, epi, setup=setup, pre=pre)
    P.barrier()


def build(cfg):
    nc = bass.Bass("TRN2", target_bir_lowering=False)
    _DB.clear()
    T, DEP = cfg.T, cfg.DEPTH

    def ext(name, shape, dt):
        return nc.dram_tensor(name, shape, dt, kind="ExternalInput")
    xT = ext("xT", [512, T], F32)
    w_in = ext("w_in", [DEP * D, NPROJ], F32)
    lt = cfg.lite
    w_br = ext("w_br", [128 if lt else DEP * D, 512], F32)
    w_out = ext("w_out", [128 if lt else DEP * D, 512], F32)
    w_up = ext("w_up", [128 if lt else DEP * D, NUPW], F32)
    w_down = ext("w_down", [128 if lt else DEP * DFF, 512], F32)
    pvs = ext("pvs", [128, DEP * NPV + 4], F32)
    outT = nc.dram_tensor("outT", [512, T], F32, kind="ExternalOutput")
    scr = {}

    def sc(name, shape, dt):
        scr[name] = nc.dram_tensor(name, shape, dt)
        return scr[name]
    proj = sc("proj", [NPROJ_PAD, T], F32)
    scal = sc("scal", [32, T], F32)
    ssq_loc = sc("ssq_loc", [1, T], F32)
    ssq_mid = sc("ssq_mid", [1, T], F32)
    ssq_tot = sc("ssq_tot", [1, T], F32)
    ssq2_loc = sc("ssq2_loc", [1, T], F32)
    ssq2_tot = sc("ssq2_tot", [1, T], F32)
    ypre = sc("ypre", [256, T], F32)
    xresA = sc("xresA", [512, T], F32)
    xresB = sc("xresB", [512, T], F32)
    cts = {}

    def ct3(name, R):
        tcn = tc_for(R)
        loc = CT(nc, name + "_loc", R, T, BF16, tcn)
        mid = CT(nc, name + "_mid", 2 * R, T, BF16, tcn)
        full = CT(nc, name + "_full", 8 * R, T, BF16, tcn)
        for x in (loc, mid, full):
            cts[x.name] = x
            scr[x.name] = x.t
        return loc, mid, full
    h_loc, h_mid, hfull = ct3("h", 512)
    y_ssd, ym_ssd, yf_ssd = ct3("y_ssd", 256)
    y_fox, ym_fox, yf_fox = ct3("y_fox", 128)
    y_gdn, ym_gdn, yf_gdn = ct3("y_gdn", 128)
    m_loc, m_mid, mfull = ct3("m", 512)
    act_loc, act_mid, actfull = ct3("act", FFL)
    taps = {n: nc.dram_tensor("tap_" + n, list(scr[n].shape), scr[n].dtype, kind="ExternalOutput") for n in cfg.taps}

    class Stop(Exception):
        pass

    with contextlib.ExitStack() as st:
        P = Prog(nc)
        env = make_env(nc, st)
        C = make_consts(nc, st, P, env)
        pvt = st.enter_context(nc.sbuf_tensor("pvsb", [128, DEP * NPV + 4], F32))
        pvb = Buf()
        P.dma("sp", pvt[:, :], pvs[:, :], writes=[pvb])
        P.barrier()

        def done(name):
            if cfg.stop_after == name:
                raise Stop()
        try:
            xcur = xT
            for l in range(DEP):
                pv = pvt[:, l * NPV:(l + 1) * NPV]
                phase_norm(P, env, C, cfg, xcur, pv[:, PV_NMIX:PV_NMIX + 4], ssq_loc, ssq_mid, ssq_tot, h_loc, h_mid, hfull)
                done("norm1")
                setup, epi = copy_epilogue(P, proj)
                gemm(P, env, [(w_in[l * D:(l + 1) * D, :], hfull, D)], NPROJ, T, epi, setup=setup,
                     pre=lambda: allgather8(P, (h_loc, h_mid, hfull)))
                P.barrier()
                done("inproj")
                phase_scal(P, env, C, cfg, proj, pv, scal)
                done("scal")
                phase_fox(P, env, C, cfg, proj, scal, y_fox)
                done("fox")
                phase_ssd(P, env, C, cfg, proj, scal, pv, ypre, ssq2_loc, ssq2_tot, y_ssd)
                done("ssd")
                phase_gdn(P, env, C, cfg, proj, scal, pv, y_gdn)
                done("gdn")
                def ag_y():
                    allgather8(P, (y_ssd, ym_ssd, yf_ssd), (y_fox, ym_fox, yf_fox), (y_gdn, ym_gdn, yf_gdn))
                phase_branch(P, env, cfg, w_br[l * D:(l + 1) * D, :], (yf_ssd, yf_fox, yf_gdn), proj, m_loc, pre=ag_y)
                done("branch")
                phase_resid(P, env, cfg, w_out[l * D:(l + 1) * D, :], D, mfull, xcur, xresA,
                            pre=lambda: allgather8(P, (m_loc, m_mid, mfull)))
                xcur = xresA
                done("wout")
                phase_norm(P, env, C, cfg, xcur, pv[:, PV_NFFN:PV_NFFN + 4], ssq_loc, ssq_mid, ssq_tot, h_loc, h_mid, hfull)
                phase_ffn_up(P, env, cfg, w_up[l * D:(l + 1) * D, :], hfull, pv, act_loc,
                             pre=lambda: allgather8(P, (h_loc, h_mid, hfull)))
                done("ffnup")
                phase_resid(P, env, cfg, w_down[l * DFF:(l + 1) * DFF, :], DFF, actfull, xcur, xresB,
                            pre=lambda: allgather8(P, (act_loc, act_mid, actfull)))
                xcur = xresB
                done("layer%d" % l)
            phase_norm(P, env, C, cfg, xcur, pvt[:, DEP * NPV:DEP * NPV + 4], ssq_loc, ssq_mid, ssq_tot, out_f32=outT)
        except Stop:
            pass
        P.barrier(cc=True)
        for n, tp in taps.items():
            P.dma("sp", tp.ap(), scr[n].ap())
        P.barrier(cc=True)
        blk = st.enter_context(nc.Block())
        P.replay(blk)
    return nc


_SPLITS = [0, 2048, 5120, 5152, 8224, 8232, 11304, 11312, 11320, 12344, 16440, 20536]


def _in_cols(c):
    o_z, o_xbc, o_dt, o_fqkv, o_ff, o_gqkv, o_ga, o_gb, o_gz, o_gs, o_gf, o_gg = _SPLITS
    g = c // 2
    r = np.arange
    cols = [o_z + c * 256 + r(256), o_xbc + c * 256 + r(256), o_xbc + 2048 + g * 128 + r(128),
            o_xbc + 2560 + g * 128 + r(128),
            o_fqkv + c * 128 + r(128), o_fqkv + 1024 + c * 128 + r(128), o_fqkv + 2048 + c * 128 + r(128),
            o_gqkv + c * 128 + r(128), o_gqkv + 1024 + c * 128 + r(128), o_gqkv + 2048 + c * 128 + r(128),
            o_gz + c * 128 + r(128),
            o_gs + c * 512 + r(512), o_gf + c * 512 + r(512), o_gg + c * 512 + r(512),
            o_dt + 4 * c + r(4), np.array([o_ff + c, o_ga + c, o_gb + c, o_gb + c])]
    cols = np.concatenate(cols)
    assert cols.shape[0] == NPROJ
    return cols


def _pcol(v):
    n = cdiv(v.shape[0], 128)
    out = np.zeros((n * 128,), np.float32)
    out[:v.shape[0]] = v
    return out.reshape(n, 128).T


def _rperm(nrows):
    rb = nrows // NCORES
    return np.concatenate([np.arange(r * rb, (r + 1) * rb) for r in RANK_ORDER])


def _layout(inp, c, DEP):
    f32 = np.float32
    g = c // 2
    pD = _rperm(D)
    w_in_c = np.concatenate([inp["w_in"][l][:, _in_cols(c)][pD] for l in range(DEP)], 0)
    cs = slice(c * 512, (c + 1) * 512)
    w_br_c = np.concatenate([np.concatenate([inp["w_br_ssm"][l][:, cs][_rperm(2048)], inp["w_br_fox"][l][:, cs][_rperm(1024)],
                                             inp["w_br_gdn"][l][:, cs][_rperm(1024)]], 0) for l in range(DEP)], 0)
    w_out_c = np.concatenate([inp["w_out"][l][:, c * 512:(c + 1) * 512][pD] for l in range(DEP)], 0)
    w_down_c = np.concatenate([inp["w_down"][l][:, c * 512:(c + 1) * 512][_rperm(DFF)] for l in range(DEP)], 0)
    ups = []
    for l in range(DEP):
        wu = inp["w_up"][l]
        blk = np.zeros((D, NUPW), f32)
        col = 0
        for p in range(6):
            chunks = [2 * p, 2 * p + 1] if p < 5 else [10]
            for base in (0, DFF):
                for ci in chunks:
                    rows = min(128, FFL - ci * 128)
                    src0 = base + c * FFL + ci * 128
                    blk[:, col:col + rows] = wu[:, src0:src0 + rows]
                    col += 128
        assert col == NUPW
        ups.append(blk[pD])
    w_up_c = np.concatenate(ups, 0)
    pv = np.zeros((128, DEP * NPV + 4), f32)
    xbc_idx = np.concatenate([c * 256 + np.arange(256), 2048 + g * 128 + np.arange(128), 2560 + g * 128 + np.arange(128)])
    gq_idx = np.concatenate([c * 128 + np.arange(128), 1024 + c * 128 + np.arange(128), 2048 + c * 128 + np.arange(128)])
    for l in range(DEP):
        o = l * NPV
        pv[:, o + PV_NMIX:o + PV_NMIX + 4] = _pcol(inp["norm_mix"][l][c * 512:(c + 1) * 512])
        scw = inp["ssm_conv_w"][l][:, xbc_idx]
        for ch in range(4):
            for j in range(4):
                pv[:, o + PV_SCW + ch * 4 + j] = scw[j, ch * 128:(ch + 1) * 128]
        pv[:, o + PV_SCB:o + PV_SCB + 4] = _pcol(inp["ssm_conv_b"][l][xbc_idx])
        pv[:, o + PV_SNORM:o + PV_SNORM + 2] = _pcol(inp["ssm_norm"][l][c * 256:(c + 1) * 256])
        pv[:, o + PV_SD:o + PV_SD + 2] = _pcol(np.repeat(inp["ssm_d"][l][4 * c:4 * c + 4], 64))
        gcw = inp["gdn_conv_w"][l][:, gq_idx]
        for ch in range(3):
            for j in range(4):
                pv[:, o + PV_GCW + ch * 4 + j] = gcw[j, ch * 128:(ch + 1) * 128]
        pv[:, o + PV_GNORM] = inp["gdn_norm"][l]
        pv[:, o + PV_NFFN:o + PV_NFFN + 4] = _pcol(inp["norm_ffn"][l][c * 512:(c + 1) * 512])
        for base, poff in ((0, 0), (DFF, 11)):
            fcw = inp["ffn_conv_w"][l][:, base + c * FFL:base + (c + 1) * FFL]
            fcb = inp["ffn_conv_b"][l][base + c * FFL:base + (c + 1) * FFL]
            for j in range(3):
                cw = _pcol(fcw[j])
                for ci in range(11):
                    pv[:, o + PV_FCW + (poff + ci) * 3 + j] = cw[:, ci]
            pv[:, o + PV_FCB + poff:o + PV_FCB + poff + 11] = _pcol(fcb)
        pv[0:4, o + PV_SBIAS] = inp["ssm_dt_bias"][l][4 * c:4 * c + 4]
        pv[4, o + PV_SBIAS] = inp["fox_f_bias"][l][c]
        pv[5, o + PV_SBIAS] = inp["gdn_dt_bias"][l][c]
        pv[0:4, o + PV_SALOG] = inp["ssm_a_log"][l][4 * c:4 * c + 4]
        pv[5, o + PV_SALOG] = inp["gdn_a_log"][l][c]
        pv[:, o + PV_SSGN] = 1.0
        pv[4, o + PV_SSGN] = -1.0
    pv[:, DEP * NPV:DEP * NPV + 4] = _pcol(inp["norm_final"][c * 512:(c + 1) * 512])
    return {"w_in": np.ascontiguousarray(w_in_c), "w_br": np.ascontiguousarray(w_br_c),
            "w_out": np.ascontiguousarray(w_out_c), "w_up": w_up_c, "w_down": np.ascontiguousarray(w_down_c),
            "pvs": pv}


def run(cfg, inputs, trace=False):
    inp = {k: np.asarray(v) for k, v in inputs.items()}
    x = inp["x"].reshape(cfg.T, D)
    in_maps = []
    for c in range(NCORES):
        m = _layout(inp, c, cfg.DEPTH)
        m["xT"] = np.ascontiguousarray(x[:, c * 512:(c + 1) * 512].T)
        if cfg.lite:
            for k in ("w_br", "w_out", "w_up", "w_down"):
                m[k] = np.ascontiguousarray(m[k][0:128])
        in_maps.append(m)
    nc = build(cfg)
    res = run_bass_kernel_spmd(nc, in_maps, core_ids=list(range(NCORES)), **({"trace": True} if trace else {}))
    return res


def kernel(**inputs):
    x = np.asarray(inputs["x"])
    NB, SEQ, _ = x.shape
    cfg = Cfg(NB=NB, SEQ=SEQ, DEPTH=np.asarray(inputs["w_in"]).shape[0])
    res = run(cfg, inputs)
    outT = np.concatenate([r["outT"] for r in res.results], axis=0)
    return np.ascontiguousarray(outT.T).reshape(NB, SEQ, D).astype(np.float32)
```

```python
import contextlib
import numpy as np
import ml_dtypes
import concourse.bass as bass
import concourse.mybir as mybir
from concourse.bass_utils import run_bass_kernel_spmd

F32 = mybir.dt.float32
BF16 = mybir.dt.bfloat16
AF = mybir.ActivationFunctionType
ALU = mybir.AluOpType
AX = mybir.AxisListType

NCORES = 8
ENGS = ("pe", "act", "dve", "pool", "sp")
NDSEM = {"sp": 12, "pool": 6, "act": 4}
SAME_ENG_SYNC = True


class Buf:
    __slots__ = ("w", "r")

    def __init__(self):
        self.w = None
        self.r = {}


class _Rec:
    def __init__(self):
        self.call = None

    def __getattr__(self, name):
        def f(*a, **k):
            self.call = (name, a, k)
        return f


class Prog:
    def __init__(self, nc):
        self.nc = nc
        self.streams = {e: [] for e in ENGS}
        self.cnt = {}
        self.semh = {}
        self.seen = {e: {} for e in ENGS}
        self.rr = {q: 0 for q in NDSEM}
        for e in ENGS:
            self._mksem(("e", e))
        for q, n in NDSEM.items():
            for i in range(n):
                self._mksem(("d", q, i))
        self._mksem(("cc",))

    def _mksem(self, key):
        self.semh[key] = self.nc.alloc_semaphore(name="s_" + "_".join(str(k) for k in key))
        self.cnt[key] = 0

    def _deps(self, reads, writes):
        deps = {}

        def add(tok):
            if tok is not None and deps.get(tok[0], 0) < tok[1]:
                deps[tok[0]] = tok[1]
        for b in reads:
            add(b.w)
        for b in writes:
            add(b.w)
            for k, v in b.r.items():
                add((k, v))
        return deps

    def _waits(self, eng, deps):
        waits = []
        own = ("e", eng)
        for k, v in deps.items():
            if k == own:
                if eng == "pe" or not SAME_ENG_SYNC or v > self.cnt[own]:
                    continue
            if self.seen[eng].get(k, 0) >= v:
                continue
            self.seen[eng][k] = v
            waits.append((k, v))
        return waits

    def _commit(self, tok, reads, writes):
        for b in reads:
            if b.r.get(tok[0], 0) < tok[1]:
                b.r[tok[0]] = tok[1]
        for b in writes:
            b.w = tok
            b.r = {}

    def op(self, eng, fn, reads=(), writes=(), sig=True, extra=()):
        rec = _Rec()
        fn(rec)
        name, a, k = rec.call
        fn = lambda e, name=name, a=a, k=k: getattr(e, name)(*a, **k)
        deps = self._deps(reads, writes)
        for tok in extra:
            if tok is not None and deps.get(tok[0], 0) < tok[1]:
                deps[tok[0]] = tok[1]
        waits = self._waits(eng, deps)
        key = ("e", eng)
        if sig:
            self.cnt[key] += 1
            tok = (key, self.cnt[key])
            self.streams[eng].append((waits, fn, (key, 1)))
        else:
            tok = (key, self.cnt[key] + 1)
            self.streams[eng].append((waits, fn, None))
        self._commit(tok, reads, writes)
        return tok

    def dma(self, q, out, in_, reads=(), writes=(), **kw):
        k = self.rr[q] % NDSEM[q]
        self.rr[q] += 1
        key = ("d", q, k)
        deps = self._deps(reads, writes)
        if self.cnt[key] > 0:
            deps[key] = self.cnt[key]
        waits = self._waits(q, deps)
        self.cnt[key] += 16
        tok = (key, self.cnt[key])
        self.streams[q].append((waits, lambda e: e.dma_start(out=out, in_=in_, **kw), (key, 16)))
        self._commit(tok, reads, writes)
        return tok

    def collective(self, kind, op, groups, in_ap, out_ap, reads=(), writes=()):
        key = ("cc",)
        deps = self._deps(reads, writes)
        waits = self._waits("pool", deps)
        self.cnt[key] += 1
        tok = (key, self.cnt[key])
        self.streams["pool"].append((waits, lambda e: e.collective_compute(
            kind, op, replica_groups=groups, ins=[in_ap], outs=[out_ap]), (key, 1)))
        self._commit(tok, reads, writes)
        return tok

    def barrier(self, cc=False):
        allk = {k: v for k, v in self.cnt.items() if v > 0 and (cc or k != ("cc",))}
        for e in ENGS:
            waits = self._waits(e, dict(allk))
            if waits:
                self.streams[e].append((waits, None, None))

    def replay(self, block):
        def run(stream):
            def body(e):
                for waits, fn, sig in stream:
                    for k, v in waits:
                        e.wait_ge(self.semh[k], v)
                    if fn is None:
                        continue
                    ins = fn(e)
                    if sig is not None:
                        ins.then_inc(self.semh[sig[0]], sig[1])
            return body
        block.tensor(run(self.streams["pe"]))
        block.scalar(run(self.streams["act"]))
        block.vector(run(self.streams["dve"]))
        block.gpsimd(run(self.streams["pool"]))
        block.sync(run(self.streams["sp"]))


class Arena:
    def __init__(self, t, ncols):
        self.t = t
        self.n = ncols
        self.o = 0

    def reset(self):
        self.o = 0

    def f32(self, cols, parts=128):
        assert self.o + cols <= self.n, (self.o, cols, self.n)
        ap = self.t[0:parts, self.o:self.o + cols]
        self.o += cols
        return ap

    def bf16(self, cols, parts=128):
        c32 = (cols + 1) // 2
        return self.f32(c32, parts).bitcast(BF16)[:, 0:cols]


G4 = [[0, 1, 2, 3], [4, 5, 6, 7]]
GX = [[0, 4], [1, 5], [2, 6], [3, 7]]


class CT:
    def __init__(self, nc, name, R, T, dtype, TC):
        TC = min(TC, T)
        assert T % TC == 0
        self.R, self.T, self.TC, self.NCH = R, T, TC, T // TC
        nparts = 1
        while R * T * 2 / nparts > 200e6:
            nparts *= 2
        assert self.NCH % nparts == 0
        self.cpp = self.NCH // nparts
        self.ts = [nc.dram_tensor(name if nparts == 1 else "%s_p%d" % (name, i), [self.cpp * R, TC], dtype)
                   for i in range(nparts)]
        self.t = self.ts[0]
        self.name = name
        self.bufs = [Buf() for _ in range(self.NCH)]

    def rows(self, ch, r0, n):
        t, c = self.ts[ch // self.cpp], ch % self.cpp
        return t[c * self.R + r0:c * self.R + r0 + n, :]

    def chunk(self, ch):
        return self.rows(ch, 0, self.R)

    def span(self, r0, rows, t0, tl):
        assert t0 % self.TC == 0 and tl % self.TC == 0
        ch0, n = t0 // self.TC, tl // self.TC
        t, c = self.ts[ch0 // self.cpp], ch0 % self.cpp
        assert c + n <= self.cpp
        v = t.ap().rearrange("(c r) t -> r c t", r=self.R)
        return v[r0:r0 + rows, c:c + n, :]

    def sview(self, ap2d):
        return ap2d.rearrange("p (c t) -> p c t", t=self.TC)


def tc_for(R):
    return 512 if R <= 512 else 128


RANK_ORDER = [0, 4, 1, 5, 2, 6, 3, 7]


def allgather8(P, *triples):
    LA = 3
    n = triples[0][0].NCH
    assert all(t[0].NCH == n for t in triples)

    def s1(ch):
        for loc, mid, full in triples:
            P.collective("AllGather", ALU.bypass, GX, loc.chunk(ch), mid.chunk(ch), reads=[loc.bufs[ch]], writes=[mid.bufs[ch]])

    def s2(ch):
        for loc, mid, full in triples:
            P.collective("AllGather", ALU.bypass, G4, mid.chunk(ch), full.chunk(ch), reads=[mid.bufs[ch]], writes=[full.bufs[ch]])
    for ch in range(min(LA, n)):
        s1(ch)
    for ch in range(n):
        if ch + LA < n:
            s1(ch + LA)
        s2(ch)


_DB = {}


def dbuf_of(t):
    if t.name not in _DB:
        _DB[t.name] = Buf()
    return _DB[t.name]


def allreduce8(P, src, mid, dst):
    P.collective("AllReduce", ALU.add, G4, src.ap().opt(), mid.ap().opt(), reads=[dbuf_of(src)], writes=[dbuf_of(mid)])
    P.collective("AllReduce", ALU.add, GX, mid.ap().opt(), dst.ap().opt(), reads=[dbuf_of(mid)], writes=[dbuf_of(dst)])


def cdiv(a, b):
    return (a + b - 1) // b


D = 4096
NIN = 24632
DFF = 11008
FFL = DFF // NCORES
EPS = 1e-6
PZ, PX, PB, PC = 0, 256, 512, 640
FQ, FK, FV = 768, 896, 1024
GQ, GK, GV, GZ = 1152, 1280, 1408, 1536
GA_SSM, GA_FOX, GA_GDN = 1664, 2176, 2688
PS = 3200
NPROJ = 3208
NPROJ_PAD = 3328
PV_NMIX, PV_SCW, PV_SCB, PV_SNORM, PV_SD, PV_GCW, PV_GNORM, PV_NFFN = 0, 4, 20, 24, 26, 28, 40, 41
PV_FCW, PV_FCB, PV_SBIAS, PV_SALOG, PV_SSGN = 45, 111, 133, 134, 135
NPV = 136
NUPW = 2816


class Cfg:
    def __init__(self, NB=4, SEQ=4096, DEPTH=2, stop_after=None, taps=(), lite=False):
        self.NB, self.SEQ, self.DEPTH = NB, SEQ, DEPTH
        self.lite = lite
        self.T = NB * SEQ
        self.stop_after = stop_after
        self.taps = tuple(taps)


def make_env(nc, st, ncols=45056):
    env = {}
    t = st.enter_context(nc.sbuf_tensor("arena", [128, ncols], F32))
    env["ar"] = Arena(t, ncols)
    env["psum"] = [st.enter_context(nc.psum_tensor("ps%d" % i, [128, 512], F32)) for i in range(8)]
    env["psb"] = [Buf() for _ in range(8)]
    return env


def make_consts(nc, st, P, env):
    c = {}
    cb = Buf()
    t = st.enter_context(nc.sbuf_tensor("consts", [128, 1024], F32))
    ones, ident, tri, tris = t[:, 0:128], t[:, 128:256], t[:, 256:384], t[:, 384:512]
    sel4 = t[0:8, 512:640]
    bfv = t[:, 640:1024].bitcast(BF16)
    ones_bf, tri_bf, ident_bf = bfv[:, 0:128], bfv[:, 128:256], bfv[:, 256:384]
    P.op("pool", lambda e: e.memset(ones, 1.0), writes=[cb])
    P.op("pool", lambda e: e.affine_select(ident, ones, pattern=[[-1, 128]], compare_op=ALU.is_equal,
                                           fill=0.0, base=0, channel_multiplier=1), reads=[cb], writes=[cb])
    P.op("pool", lambda e: e.affine_select(tri, ones, pattern=[[1, 128]], compare_op=ALU.is_ge,
                                           fill=0.0, base=0, channel_multiplier=-1), reads=[cb], writes=[cb])
    P.op("pool", lambda e: e.affine_select(tris, ones, pattern=[[1, 128]], compare_op=ALU.is_ge,
                                           fill=0.0, base=-1, channel_multiplier=-1), reads=[cb], writes=[cb])
    P.op("pool", lambda e: e.affine_select(sel4, ones[0:8, :], pattern=[[0, 128]], compare_op=ALU.is_equal,
                                           fill=0.0, base=-4, channel_multiplier=1), reads=[cb], writes=[cb])
    P.op("pool", lambda e: e.tensor_copy(ones_bf, ones), reads=[cb], writes=[cb])
    P.op("pool", lambda e: e.tensor_copy(tri_bf, tri), reads=[cb], writes=[cb])
    P.op("pool", lambda e: e.tensor_copy(ident_bf, ident), reads=[cb], writes=[cb])
    t2 = st.enter_context(nc.sbuf_tensor("consts2", [128, 128], F32))
    lows = t2[:, 0:128]
    P.op("pool", lambda e: e.affine_select(lows, ones, pattern=[[-1, 128]], compare_op=ALU.is_gt,
                                           fill=0.0, base=0, channel_multiplier=1), reads=[cb], writes=[cb])
    c.update(ones=ones, ident=ident, tri=tri, tris=tris, sel4=sel4, ones_bf=ones_bf, tri_bf=tri_bf,
             ident_bf=ident_bf, lows=lows, buf=cb)
    return c


def gemm(P, env, segs, N, T, epi, setup=None, NPW=512, TT=512, KG=8, pre=None):
    ar, psum, psb = env["ar"], env["psum"], env["psb"]
    ar.reset()
    nseg = len(segs)
    KCs = [K // 128 for (_, _, K) in segs]
    assert all(K % 128 == 0 for (_, _, K) in segs)
    CPP = NPW // 128
    assert nseg * CPP <= 8
    dbl = nseg * CPP <= 4
    NPASS = cdiv(N, NPW)
    NWB = 2 if (NPASS > 1 and sum(KCs) * NPW * 2 * 2 <= 72 * 1024) else 1
    wsets = [([ar.bf16(kc * NPW).rearrange("p (k n) -> p k n", n=NPW) for kc in KCs], Buf()) for _ in range(NWB)]
    NAB = 3
    abufs = [(ar.bf16(KG * TT).rearrange("p (k t) -> p k t", t=TT), Buf()) for _ in range(NAB)]
    if setup is not None:
        setup(ar)
    ai = 0
    ti = 0
    def load_w(ps):
        n0 = ps * NPW
        npw = min(NPW, N - n0)
        wsbs, wb = wsets[ps % NWB]
        for s, (W, A, K) in enumerate(segs):
            for k0 in range(0, KCs[s], KG):
                kk = min(KG, KCs[s] - k0)
                P.dma("pool", wsbs[s][:, k0:k0 + kk, 0:npw],
                      W[k0 * 128:(k0 + kk) * 128, n0:n0 + npw].rearrange("(k p) n -> p k n", p=128),
                      writes=[wb])
    load_w(0)
    for ps in range(NPASS):
        n0 = ps * NPW
        npw = min(NPW, N - n0)
        wsbs, wb = wsets[ps % NWB]
        if NWB == 2:
            if ps + 1 < NPASS:
                load_w(ps + 1)
        elif ps > 0:
            load_w(ps)
        if ps == 0 and pre is not None:
            pre()
        nchunks = cdiv(npw, 128)
        for t0 in range(0, T, TT):
            tl = min(TT, T - t0)
            half = (ti % 2) * 4 if dbl else 0
            ti += 1
            for s, (W, A, K) in enumerate(segs):
                KC = KCs[s]
                for k0 in range(0, KC, KG):
                    kk = min(KG, KC - k0)
                    at, ab = abufs[ai % NAB]
                    ai += 1
                    for c0 in range(0, tl, A.TC):
                        ch = (t0 + c0) // A.TC
                        P.dma("sp", at[:, 0:kk, c0:c0 + A.TC],
                              A.rows(ch, k0 * 128, kk * 128).rearrange("(k p) t -> p k t", p=128),
                              reads=[A.bufs[ch]], writes=[ab])
                    for j in range(nchunks):
                        rows = min(128, npw - j * 128)
                        bk = half + s * CPP + j
                        for k in range(kk):
                            kc = k0 + k
                            P.op("pe", (lambda e, o=psum[bk][0:rows, 0:tl],
                                        l=wsbs[s][:, kc, j * 128:j * 128 + rows],
                                        r=at[:, k, 0:tl], st_=(kc == 0), sp_=(kc == KC - 1):
                                        e.matmul(o, lhsT=l, rhs=r, start=st_, stop=sp_)),
                                 reads=[wb, ab], writes=[psb[bk]], sig=(k == kk - 1))
            chunks = []
            for s in range(nseg):
                cs = []
                for j in range(nchunks):
                    rows = min(128, npw - j * 128)
                    bk = half + s * CPP + j
                    cs.append((n0 + j * 128, rows, psb[bk], psum[bk][0:rows, 0:tl]))
                chunks.append(cs)
            epi(n0, chunks, t0, tl)


def phase_norm(P, env, C, cfg, src, wcols, ssq_loc, ssq_mid, ssq_tot, h_loc=None, h_mid=None, hfull=None, out_f32=None):
    ar, psum, psb = env["ar"], env["psum"], env["psb"]
    T = cfg.T
    TT = 512
    ar.reset()
    xts = [(ar.f32(4 * TT), Buf()) for _ in range(2)]
    sqs = [(ar.f32(4 * TT), Buf()) for _ in range(2)]
    rows = [(ar.f32(TT, 1), Buf()) for _ in range(2)]
    bcs = [(ar.f32(TT), Buf()) for _ in range(2)]
    hts = [((ar.bf16(4 * TT) if out_f32 is None else ar.f32(4 * TT)), Buf()) for _ in range(2)]
    dbuf = Buf()
    for i, t0 in enumerate(range(0, T, TT)):
        tl = min(TT, T - t0)
        xt, xb = xts[i % 2]
        x3 = xt.rearrange("p (c t) -> p c t", t=TT)
        P.dma("sp", x3[:, :, 0:tl], src[:, t0:t0 + tl].rearrange("(c p) t -> p c t", p=128), writes=[xb])
        sq, sqb = sqs[i % 2]
        s3 = sq.rearrange("p (c t) -> p c t", t=TT)
        P.op("act", lambda e, o=s3[:, :, 0:tl], a=x3[:, :, 0:tl]: e.activation(o, a, AF.Square),
             reads=[xb], writes=[sqb])
        pb = psb[i % 2]
        for c in range(4):
            P.op("pe", lambda e, o=psum[i % 2][:, 0:tl], r=s3[:, c, 0:tl], c=c:
                 e.matmul(o, lhsT=C["ones"], rhs=r, start=(c == 0), stop=(c == 3)),
                 reads=[sqb, C["buf"]], writes=[pb], sig=(c == 3))
        rw, rwb = rows[i % 2]
        P.op("dve", lambda e, o=rw[0:1, 0:tl], a=psum[i % 2][0:1, 0:tl]: e.tensor_copy(o, a),
             reads=[pb], writes=[rwb])
        P.dma("sp", ssq_loc[0:1, t0:t0 + tl], rw[0:1, 0:tl], reads=[rwb], writes=[dbuf, dbuf_of(ssq_loc)])
    P.barrier()
    allreduce8(P, ssq_loc, ssq_mid, ssq_tot)
    for i, t0 in enumerate(range(0, T, TT)):
        tl = min(TT, T - t0)
        xt, xb = xts[i % 2]
        x3 = xt.rearrange("p (c t) -> p c t", t=TT)
        P.dma("sp", x3[:, :, 0:tl], src[:, t0:t0 + tl].rearrange("(c p) t -> p c t", p=128), writes=[xb])
        bc, bcb = bcs[i % 2]
        P.dma("sp", bc[:, 0:tl], ssq_tot[0:1, t0:t0 + tl].partition_broadcast(128), reads=[dbuf_of(ssq_tot)], writes=[bcb])
        P.op("dve", lambda e, a=bc[:, 0:tl]: e.tensor_scalar(a, a, 1.0 / D, EPS, ALU.mult, ALU.add),
             reads=[bcb], writes=[bcb])
        P.op("act", lambda e, a=bc[:, 0:tl]: e.activation(a, a, AF.Sqrt), reads=[bcb], writes=[bcb])
        P.op("dve", lambda e, a=bc[:, 0:tl]: e.reciprocal(a, a), reads=[bcb], writes=[bcb])
        ht, hb = hts[i % 2]
        h3 = ht.rearrange("p (c t) -> p c t", t=TT)
        for c in range(4):
            P.op("dve", lambda e, o=h3[:, c, 0:tl], a=x3[:, c, 0:tl], w=wcols[:, c:c + 1], b=bc[:, 0:tl]:
                 e.scalar_tensor_tensor(o, a, w, b, ALU.mult, ALU.mult), reads=[xb, bcb], writes=[hb])
        if out_f32 is not None:
            P.dma("sp", out_f32[:, t0:t0 + tl].rearrange("(c p) t -> p c t", p=128), h3[:, :, 0:tl],
                  reads=[hb], writes=[dbuf])
        else:
            for c in range(4):
                P.dma("sp", h_loc.span(c * 128, 128, t0, tl), h_loc.sview(h3[:, c, 0:tl]), reads=[hb], writes=[dbuf])
    P.barrier()


def copy_epilogue(P, dst, nbuf=4):
    st = {}

    def setup(ar):
        st["bufs"] = [(ar.f32(512), Buf()) for _ in range(nbuf)]
        st["i"] = 0
        st["d"] = Buf()

    def epi(n0, chunks, t0, tl):
        for (r0, rows, pb, pap) in chunks[0]:
            ot, otb = st["bufs"][st["i"] % nbuf]
            eng = "act" if st["i"] % 2 else "dve"
            st["i"] += 1
            if eng == "act":
                P.op("act", lambda e, o=ot[0:rows, 0:tl], a=pap: e.activation(o, a, AF.Copy),
                     reads=[pb], writes=[otb])
            else:
                P.op("dve", lambda e, o=ot[0:rows, 0:tl], a=pap: e.tensor_copy(o, a), reads=[pb], writes=[otb])
            P.dma("sp", dst[r0:r0 + rows, t0:t0 + tl], ot[0:rows, 0:tl], reads=[otb], writes=[st["d"]])
    return setup, epi


def phase_scal(P, env, C, cfg, proj, pv, scal):
    ar = env["ar"]
    ar.reset()
    L = cfg.SEQ
    mul = ar.f32(1, 8)
    mb = Buf()
    P.op("act", lambda e: e.activation(mul, pv[0:8, PV_SALOG:PV_SALOG + 1], AF.Exp), writes=[mb])
    P.op("dve", lambda e: e.tensor_scalar(mul, mul, -1.0, None, ALU.mult), reads=[mb], writes=[mb])
    ones8 = ar.f32(L, 8)
    ob = Buf()
    P.op("pool", lambda e: e.memset(ones8, 1.0), writes=[ob])
    raw, t, a, sp, ov, sg, cu = [ar.f32(L, 8) for _ in range(7)]
    bs = [Buf() for _ in range(7)]
    rb, tb, ab, spb, ovb, sgb, cub = bs
    dbuf = Buf()
    for b in range(cfg.NB):
        sl = slice(b * L, (b + 1) * L)
        P.dma("sp", raw, proj[PS:PS + 8, sl], writes=[rb])
        P.op("dve", lambda e: e.tensor_scalar(t, raw, pv[0:8, PV_SBIAS:PV_SBIAS + 1],
                                              pv[0:8, PV_SSGN:PV_SSGN + 1], ALU.add, ALU.mult),
             reads=[rb], writes=[tb])
        P.op("act", lambda e: e.activation(a, t, AF.Abs), reads=[tb], writes=[ab])
        P.op("act", lambda e: e.activation(a, a, AF.Exp, scale=-1.0), reads=[ab], writes=[ab])
        P.op("act", lambda e: e.activation(a, a, AF.Ln, bias=1.0), reads=[ab], writes=[ab])
        P.op("dve", lambda e: e.scalar_tensor_tensor(sp, t, 0.0, a, ALU.max, ALU.add),
             reads=[tb, ab], writes=[spb])
        P.op("dve", lambda e: e.tensor_scalar(ov, sp, mul, None, ALU.mult), reads=[spb, mb], writes=[ovb])
        P.op("act", lambda e: e.activation(sg, raw, AF.Sigmoid), reads=[rb], writes=[sgb])
        P.op("dve", lambda e: e.tensor_tensor_scan(cu, ones8, ov, 0.0, ALU.mult, ALU.add),
             reads=[ovb, ob], writes=[cub])
        for i, (src, sb) in enumerate(((sp, spb), (ov, ovb), (sg, sgb), (cu, cub))):
            P.dma("sp", scal[8 * i:8 * i + 8, sl], src, reads=[sb], writes=[dbuf])
    P.barrier()


def phase_fox(P, env, C, cfg, proj, scal, y_fox):
    ar, psum, psb = env["ar"], env["psum"], env["psb"]
    ar.reset()
    L = cfg.SEQ
    NBK = L // 128
    qkv = ar.f32(3 * L)
    qkvb = Buf()
    q3 = qkv.rearrange("p (c t) -> p c t", t=L)
    qs = ar.bf16(L)
    kb_ = ar.bf16(L)
    qsb, kbb = Buf(), Buf()
    vt = ar.bf16(L)
    vtb = Buf()
    cum = ar.f32(L, 8)
    cumb = Buf()
    cumT = ar.f32(NBK * 8)
    cumTb = Buf()
    c0 = ar.f32(NBK)
    c0b = Buf()
    bias = ar.f32(NBK * NBK)
    biasb = Buf()
    ybuf = ar.bf16(L)
    yb = Buf()
    NE = 4
    es = [(ar.bf16(128), Buf()) for _ in range(NE)]
    rden = [(ar.f32(128), Buf()) for _ in range(2)]
    dbuf = Buf()
    scale = 128.0 ** -0.5
    cb = C["buf"]
    ei = 0
    for b in range(cfg.NB):
        sl = slice(b * L, (b + 1) * L)
        P.dma("sp", q3, proj[FQ:FQ + 384, sl].rearrange("(c p) t -> p c t", p=128), writes=[qkvb])
        P.dma("sp", cum, scal[24:32, sl], writes=[cumb])
        P.op("act", lambda e: e.activation(qs, q3[:, 0, :], AF.Copy, scale=scale), reads=[qkvb], writes=[qsb])
        P.op("dve", lambda e: e.tensor_copy(kb_, q3[:, 1, :]), reads=[qkvb], writes=[kbb])
        for g0 in range(0, NBK, 4):
            bk = (g0 // 4) % 4
            gn = min(4, NBK - g0)
            for g in range(gn):
                blk = g0 + g
                P.op("pe", lambda e, o=psum[bk][:, g * 128:(g + 1) * 128], a=q3[:, 2, blk * 128:(blk + 1) * 128]:
                     e.transpose(o, a, C["ident"]), reads=[qkvb, cb], writes=[psb[bk]], sig=(g == gn - 1))
            eng = "act" if (g0 // 4) % 2 else "dve"
            if eng == "act":
                P.op("act", lambda e, o=vt[:, g0 * 128:(g0 + gn) * 128], a=psum[bk][:, 0:gn * 128]:
                     e.activation(o, a, AF.Copy), reads=[psb[bk]], writes=[vtb])
            else:
                P.op("dve", lambda e, o=vt[:, g0 * 128:(g0 + gn) * 128], a=psum[bk][:, 0:gn * 128]:
                     e.tensor_copy(o, a), reads=[psb[bk]], writes=[vtb])
        for blk in range(NBK):
            P.op("pe", lambda e, o=psum[4][:, blk * 8:(blk + 1) * 8], a=cum[0:8, blk * 128:(blk + 1) * 128]:
                 e.transpose(o, a, C["ident"][0:8, 0:8]), reads=[cumb, cb], writes=[psb[4]], sig=(blk == NBK - 1))
        P.op("dve", lambda e: e.tensor_copy(cumT, psum[4][:, 0:NBK * 8]), reads=[psb[4]], writes=[cumTb])
        P.op("pe", lambda e: e.matmul(psum[5][:, 0:NBK], lhsT=C["sel4"],
                                      rhs=cum.rearrange("p (b s) -> p b s", s=128)[:, :, 64],
                                      start=True, stop=True), reads=[cumb, cb], writes=[psb[5]])
        P.op("dve", lambda e: e.tensor_copy(c0, psum[5][:, 0:NBK]), reads=[psb[5]], writes=[c0b])
        cumT3 = cumT.rearrange("p (b s) -> p b s", s=8)
        for j in range(NBK):
            P.op("dve", lambda e, o=bias[:, j * NBK:(j + 1) * NBK], s1=c0[:, j:j + 1]:
                 e.tensor_scalar(o, cumT3[:, :, 4], s1, -1.0, ALU.subtract, ALU.mult),
                 reads=[cumTb, c0b], writes=[biasb])
        blocks = [(j, i) for j in range(NBK) for i in range(j + 1)]
        LA = 3
        base = ei

        def emit_s(n):
            j, i = blocks[n]
            sbk = (base + n) % 4
            et, eb = es[(base + n) % NE]
            P.op("pe", lambda e: e.matmul(psum[sbk][:, 0:128], lhsT=kb_[:, i * 128:(i + 1) * 128],
                                          rhs=qs[:, j * 128:(j + 1) * 128], start=True, stop=True),
                 reads=[kbb, qsb], writes=[psb[sbk]])
            P.op("act", lambda e: e.activation(et, psum[sbk][:, 0:128], AF.Exp,
                                               bias=bias[:, j * NBK + i:j * NBK + i + 1]),
                 reads=[psb[sbk], biasb], writes=[eb])
            if i == j:
                P.op("pool", lambda e: e.tensor_tensor(et, et, C["tri_bf"], ALU.mult), reads=[eb, cb], writes=[eb])

        def emit_pv(n):
            j, i = blocks[n]
            ob_, db_ = 4 + (j % 2), 6 + (j % 2)
            et, eb = es[(base + n) % NE]
            P.op("pe", lambda e: e.matmul(psum[ob_][:, 0:128], lhsT=vt[:, i * 128:(i + 1) * 128], rhs=et,
                                          start=(i == 0), stop=(i == j)),
                 reads=[vtb, eb], writes=[psb[ob_]], sig=False)
            P.op("pe", lambda e: e.matmul(psum[db_][:, 0:128], lhsT=C["ones_bf"], rhs=et, start=(i == 0), stop=(i == j)),
                 reads=[cb, eb], writes=[psb[db_]])
            if i == j:
                rd, rdb = rden[j % 2]
                P.op("dve", lambda e: e.reciprocal(rd, psum[db_][:, 0:128]), reads=[psb[db_]], writes=[rdb])
                P.op("dve", lambda e: e.tensor_tensor(ybuf[:, j * 128:(j + 1) * 128], psum[ob_][:, 0:128], rd, ALU.mult),
                     reads=[psb[ob_], rdb], writes=[yb])
        nblk = len(blocks)
        for n in range(min(LA, nblk)):
            emit_s(n)
        for n in range(nblk):
            if n + LA < nblk:
                emit_s(n + LA)
            emit_pv(n)
        ei += nblk
        P.dma("sp", y_fox.span(0, 128, b * L, L), y_fox.sview(ybuf), reads=[yb], writes=[dbuf])
    P.barrier()


def conv_silu(P, out3, raw3, nch, SEG, KW, pv, wcol0, bcol0, rawb, outb, out_dt_bf_from=None, outbf3=None, outbfb=None):
    for c in range(nch):
        w = lambda j, c=c: pv[:, wcol0 + c * KW + j:wcol0 + c * KW + j + 1]
        if bcol0 is not None:
            P.op("dve", lambda e, c=c: e.tensor_scalar(out3[:, c, :], raw3[:, c, KW - 1:KW - 1 + SEG], w(KW - 1),
                                                       pv[:, bcol0 + c:bcol0 + c + 1], ALU.mult, ALU.add),
                 reads=[rawb], writes=[outb])
        else:
            P.op("dve", lambda e, c=c: e.tensor_scalar(out3[:, c, :], raw3[:, c, KW - 1:KW - 1 + SEG], w(KW - 1),
                                                       None, ALU.mult), reads=[rawb], writes=[outb])
        for j in range(KW - 1):
            P.op("dve", lambda e, c=c, j=j: e.scalar_tensor_tensor(out3[:, c, :], raw3[:, c, j:j + SEG], w(j),
                                                                   out3[:, c, :], ALU.mult, ALU.add),
                 reads=[rawb, outb], writes=[outb])
        P.op("act", lambda e, c=c: e.activation(out3[:, c, :], out3[:, c, :], AF.Silu), reads=[outb], writes=[outb])


def load_with_halo(P, raw3, rawb, src_rows, t0, SEG, HALO, seq_start):
    if seq_start:
        P.op("pool", lambda e: e.memset(raw3[:, :, 0:HALO], 0.0), writes=[rawb])
        P.dma("sp", raw3[:, :, HALO:HALO + SEG], src_rows[:, t0:t0 + SEG].rearrange("(c p) t -> p c t", p=128),
              writes=[rawb])
    else:
        P.dma("sp", raw3[:, :, 0:HALO + SEG],
              src_rows[:, t0 - HALO:t0 + SEG].rearrange("(c p) t -> p c t", p=128), writes=[rawb])


def phase_ssd(P, env, C, cfg, proj, scal, pv, ypre, ssq_loc, ssq_tot, y_ssd, cc=True):
    ar, psum, psb = env["ar"], env["psum"], env["psb"]
    ar.reset()
    L = cfg.SEQ
    SEG = min(1024, L)
    NCH = SEG // 128
    cb = C["buf"]
    raw = ar.f32(4 * (SEG + 3))
    raw3 = raw.rearrange("p (c t) -> p c t", t=SEG + 3)
    rawb = Buf()
    xc = ar.f32(4 * SEG)
    xc3 = xc.rearrange("p (c t) -> p c t", t=SEG)
    xcb = Buf()
    bcbf = ar.bf16(2 * SEG)
    bc3 = bcbf.rearrange("p (c t) -> p c t", t=SEG)
    bcb = Buf()
    zs = ar.f32(2 * SEG)
    zs3 = zs.rearrange("p (c t) -> p c t", t=SEG)
    zsb = Buf()
    scs = ar.f32(2 * SEG, 8)
    scs3 = scs.rearrange("p (c t) -> p c t", t=SEG)
    scsb = Buf()
    hst = ar.f32(256)
    hst3 = hst.rearrange("p (h d) -> p h d", d=64)
    hstb = Buf()
    hbf = ar.bf16(256)
    hbfb = Buf()
    yseg = ar.f32(2 * SEG)
    yseg3 = yseg.rearrange("p (c t) -> p c t", t=SEG)
    ysegb = Buf()
    rowseg = ar.f32(SEG, 1)
    rowb = Buf()
    dts = ar.f32(16)
    dtsb = Buf()
    acs = ar.f32(4)
    acsb = Buf()
    lbc = ar.f32(512)
    lbcb = Buf()
    dec = ar.f32(512)
    dec3 = dec.rearrange("p (h t) -> p h t", t=128)
    decb = Buf()
    cbm = ar.f32(128)
    cbmb = Buf()
    mt = ar.bf16(512)
    mt3 = mt.rearrange("p (h t) -> p h t", t=128)
    mtb = Buf()
    ebc = ar.f32(512)
    ebc3 = ebc.rearrange("p (h t) -> p h t", t=128)
    ebcb = Buf()
    ce = ar.bf16(512)
    ce3 = ce.rearrange("p (h t) -> p h t", t=128)
    ceb = Buf()
    xdt = ar.bf16(256)
    xdt3 = xdt.rearrange("p (h d) -> p h d", d=64)
    xdtb = Buf()
    btok = ar.bf16(128)
    btokb = Buf()
    dd = ar.f32(8)
    ddb = Buf()
    xdec = ar.bf16(256)
    xdec3 = xdec.rearrange("p (h d) -> p h d", d=64)
    xdecb = Buf()
    sq = ar.f32(256)
    sqb = Buf()
    dbuf = Buf()
    ident8 = C["ident"][0:8, 0:8]
    for b in range(cfg.NB):
        P.op("pool", lambda e: e.memset(hst, 0.0), writes=[hstb])
        P.op("pool", lambda e: e.memset(hbf, 0.0), writes=[hbfb])
        for s0 in range(0, L, SEG):
            t0 = b * L + s0
            load_with_halo(P, raw3, rawb, proj[PX:PX + 512, :], t0, SEG, 3, s0 == 0)
            P.dma("sp", zs3, proj[PZ:PZ + 256, t0:t0 + SEG].rearrange("(c p) t -> p c t", p=128), writes=[zsb])
            P.dma("sp", scs3[:, 0, :], scal[0:8, t0:t0 + SEG], writes=[scsb])
            P.dma("sp", scs3[:, 1, :], scal[8:16, t0:t0 + SEG], writes=[scsb])
            conv_silu(P, xc3, raw3, 4, SEG, 4, pv, PV_SCW, PV_SCB, rawb, xcb)
            P.op("pool", lambda e: e.tensor_copy(bc3, xc3[:, 2:4, :]), reads=[xcb], writes=[bcb])
            P.op("act", lambda e: e.activation(zs, zs, AF.Silu), reads=[zsb], writes=[zsb])
            for ch in range(NCH):
                o = ch * 128
                osl = slice(o, o + 128)
                P.op("pe", lambda e: e.transpose(psum[0][:, 0:8], scs3[:, 0, osl], ident8),
                     reads=[scsb, cb], writes=[psb[0]], sig=False)
                P.op("pe", lambda e: e.transpose(psum[0][:, 8:16], scs3[:, 1, osl], ident8),
                     reads=[scsb, cb], writes=[psb[0]])
                P.op("dve", lambda e: e.tensor_copy(dts, psum[0][:, 0:16]), reads=[psb[0]], writes=[dtsb])
                P.op("pe", lambda e: e.matmul(psum[0][:, 16:20], lhsT=C["tri"], rhs=dts[:, 8:12], start=True, stop=True),
                     reads=[dtsb, cb], writes=[psb[0]])
                P.op("dve", lambda e: e.tensor_copy(acs, psum[0][:, 16:20]), reads=[psb[0]], writes=[acsb])
                for h in range(4):
                    P.op("act", lambda e, h=h: e.activation(lbc[:, h * 128:(h + 1) * 128], C["ones"], AF.Copy,
                                                            scale=dts[:, 8 + h:9 + h]),
                         reads=[dtsb, cb], writes=[lbcb])
                for h in range(4):
                    P.op("pe", lambda e, h=h: e.matmul(psum[2][:, h * 128:(h + 1) * 128],
                                                       lhsT=lbc[:, h * 128:(h + 1) * 128], rhs=C["tri"],
                                                       start=True, stop=True),
                         reads=[lbcb, cb], writes=[psb[2]], sig=(h == 3))
                for h in range(4):
                    P.op("dve", lambda e, h=h: e.tensor_scalar(dec[:, h * 128:(h + 1) * 128],
                                                               psum[2][:, h * 128:(h + 1) * 128],
                                                               acs[:, h:h + 1], 0.0, ALU.subtract, ALU.min),
                         reads=[psb[2], acsb], writes=[decb])
                P.op("act", lambda e: e.activation(dec, dec, AF.Exp), reads=[decb], writes=[decb])
                P.op("pe", lambda e: e.matmul(psum[3][:, 0:128], lhsT=bc3[:, 0, osl], rhs=bc3[:, 1, osl],
                                              start=True, stop=True), reads=[bcb], writes=[psb[3]])
                P.op("dve", lambda e: e.tensor_tensor(cbm, psum[3][:, 0:128], C["tri"], ALU.mult),
                     reads=[psb[3], cb], writes=[cbmb])
                P.op("dve", lambda e: e.tensor_tensor(mt3, dec3, cbm.unsqueeze(1).to_broadcast([128, 4, 128]), ALU.mult),
                     reads=[decb, cbmb], writes=[mtb])
                P.op("act", lambda e: e.activation(ebc, psum[2][:, 0:512], AF.Exp), reads=[psb[2]], writes=[ebcb])
                P.op("pool", lambda e: e.tensor_tensor(ce3, ebc3, bc3[:, 1, osl].unsqueeze(1).to_broadcast([128, 4, 128]),
                                                       ALU.mult), reads=[ebcb, bcb], writes=[ceb])
                for k in range(3):
                    P.op("pe", lambda e, k=k: e.transpose(psum[1][:, k * 128:(k + 1) * 128], xc3[:, k, osl], C["ident"]),
                         reads=[xcb, cb], writes=[psb[1]], sig=(k == 2))
                P.op("dve", lambda e: e.tensor_tensor(xdt3, psum[1][:, 0:256].rearrange("p (h d) -> p h d", d=64),
                                                      dts[:, 0:4].unsqueeze(2).to_broadcast([128, 4, 64]), ALU.mult),
                     reads=[psb[1], dtsb], writes=[xdtb])
                P.op("act", lambda e: e.activation(btok, psum[1][:, 256:384], AF.Copy), reads=[psb[1]], writes=[btokb])
                ab3 = psum[2][:, 0:512].rearrange("p (h t) -> p h t", t=128)
                P.op("dve", lambda e: e.tensor_tensor(dd[:, 0:4], ab3[:, :, 127], acs, ALU.subtract),
                     reads=[psb[2], acsb], writes=[ddb])
                P.op("dve", lambda e: e.tensor_copy(dd[:, 4:8], ab3[:, :, 127]), reads=[psb[2]], writes=[ddb])
                P.op("act", lambda e: e.activation(dd, dd, AF.Exp), reads=[ddb], writes=[ddb])
                P.op("pool", lambda e: e.tensor_tensor(xdec3, xdt3, dd[:, 0:4].unsqueeze(2).to_broadcast([128, 4, 64]),
                                                       ALU.mult), reads=[xdtb, ddb], writes=[xdecb])
                for h in range(4):
                    po = psum[4][(h % 2) * 64:(h % 2) * 64 + 64, (h // 2) * 128:(h // 2 + 1) * 128]
                    P.op("pe", lambda e, h=h, po=po: e.matmul(po, lhsT=xdt[:, h * 64:(h + 1) * 64], rhs=mt3[:, h, :],
                                                              start=True, stop=False),
                         reads=[xdtb, mtb], writes=[psb[4]], sig=False)
                    P.op("pe", lambda e, h=h, po=po: e.matmul(po, lhsT=hbf[:, h * 64:(h + 1) * 64], rhs=ce3[:, h, :],
                                                              start=False, stop=True),
                         reads=[hbfb, ceb], writes=[psb[4]], sig=(h == 3))
                P.op("pe", lambda e: e.matmul(psum[5][:, 0:256], lhsT=btok, rhs=xdec, start=True, stop=True),
                     reads=[btokb, xdecb], writes=[psb[5]])
                P.op("dve", lambda e: e.tensor_tensor(hst3, hst3, dd[:, 4:8].unsqueeze(2).to_broadcast([128, 4, 64]),
                                                      ALU.mult), reads=[ddb, hstb], writes=[hstb])
                P.op("dve", lambda e: e.tensor_tensor(hst, hst, psum[5][:, 0:256], ALU.add),
                     reads=[psb[5], hstb], writes=[hstb])
                P.op("act", lambda e: e.activation(hbf, hst, AF.Copy), reads=[hstb], writes=[hbfb])
                for k in range(2):
                    P.op("dve", lambda e, k=k: e.scalar_tensor_tensor(yseg3[:, k, osl], xc3[:, k, osl],
                                                                      pv[:, PV_SD + k:PV_SD + k + 1],
                                                                      psum[4][:, k * 128:(k + 1) * 128], ALU.mult, ALU.add),
                         reads=[xcb, psb[4]], writes=[ysegb])
                    P.op("pool", lambda e, k=k: e.tensor_tensor(yseg3[:, k, osl], yseg3[:, k, osl], zs3[:, k, osl], ALU.mult),
                         reads=[ysegb, zsb], writes=[ysegb])
                    P.op("act", lambda e, k=k: e.activation(sq[:, k * 128:(k + 1) * 128], yseg3[:, k, osl], AF.Square),
                         reads=[ysegb], writes=[sqb])
                for k in range(2):
                    P.op("pe", lambda e, k=k: e.matmul(psum[6][:, 0:128], lhsT=C["ones"], rhs=sq[:, k * 128:(k + 1) * 128],
                                                       start=(k == 0), stop=(k == 1)),
                         reads=[sqb, cb], writes=[psb[6]], sig=(k == 1))
                P.op("dve", lambda e: e.tensor_copy(rowseg[0:1, osl], psum[6][0:1, 0:128]), reads=[psb[6]], writes=[rowb])
            P.dma("sp", ypre[0:256, t0:t0 + SEG].rearrange("(c p) t -> p c t", p=128), yseg3, reads=[ysegb], writes=[dbuf])
            P.dma("sp", ssq_loc[0:1, t0:t0 + SEG], rowseg[0:1, :], reads=[rowb], writes=[dbuf, dbuf_of(ssq_loc)])
    P.barrier()
    if cc:
        P.collective("AllReduce", ALU.add, [[2 * i, 2 * i + 1] for i in range(NCORES // 2)],
                     ssq_loc.ap().opt(), ssq_tot.ap().opt(), reads=[dbuf_of(ssq_loc)], writes=[dbuf_of(ssq_tot)])
    else:
        P.dma("sp", ssq_tot.ap(), ssq_loc.ap(), reads=[dbuf_of(ssq_loc)], writes=[dbuf_of(ssq_tot)])
    ar.reset()
    TT = 512
    yts = [(ar.f32(2 * TT), Buf()) for _ in range(2)]
    bcs = [(ar.f32(TT), Buf()) for _ in range(2)]
    ots = [(ar.bf16(2 * TT), Buf()) for _ in range(2)]
    for i, t0 in enumerate(range(0, cfg.T, TT)):
        yt, ytb = yts[i % 2]
        y3 = yt.rearrange("p (c t) -> p c t", t=TT)
        P.dma("sp", y3, ypre[0:256, t0:t0 + TT].rearrange("(c p) t -> p c t", p=128), writes=[ytb])
        bc, bcb_ = bcs[i % 2]
        P.dma("sp", bc, ssq_tot[0:1, t0:t0 + TT].partition_broadcast(128), reads=[dbuf_of(ssq_tot)], writes=[bcb_])
        P.op("dve", lambda e, a=bc: e.tensor_scalar(a, a, 1.0 / 512.0, EPS, ALU.mult, ALU.add), reads=[bcb_], writes=[bcb_])
        P.op("act", lambda e, a=bc: e.activation(a, a, AF.Sqrt), reads=[bcb_], writes=[bcb_])
        P.op("dve", lambda e, a=bc: e.reciprocal(a, a), reads=[bcb_], writes=[bcb_])
        ot, otb = ots[i % 2]
        o3 = ot.rearrange("p (c t) -> p c t", t=TT)
        for k in range(2):
            P.op("dve", lambda e, k=k, o3=o3, y3=y3, bc=bc: e.scalar_tensor_tensor(
                o3[:, k, :], y3[:, k, :], pv[:, PV_SNORM + k:PV_SNORM + k + 1], bc, ALU.mult, ALU.mult),
                reads=[ytb, bcb_], writes=[otb])
        for k in range(2):
            P.dma("sp", y_ssd.span(k * 128, 128, t0, TT), y_ssd.sview(o3[:, k, :]), reads=[otb], writes=[dbuf])
    P.barrier()


def phase_gdn(P, env, C, cfg, proj, scal, pv, y_gdn):
    ar, psum, psb = env["ar"], env["psum"], env["psb"]
    ar.reset()
    L = cfg.SEQ
    SEG = min(1024, L)
    NCH = SEG // 64
    cb = C["buf"]
    ident, ones, tri, lows = C["ident"], C["ones"], C["tri"], C["lows"]
    i64 = ident[0:64, 0:64]
    raw = ar.f32(3 * (SEG + 3))
    raw3 = raw.rearrange("p (c t) -> p c t", t=SEG + 3)
    rawb = Buf()
    qkv = ar.f32(3 * SEG)
    qkv3 = qkv.rearrange("p (c t) -> p c t", t=SEG)
    qkvb = Buf()
    zs = ar.f32(SEG)
    zsb = Buf()
    scs = ar.f32(2 * SEG, 8)
    scs3 = scs.rearrange("p (c t) -> p c t", t=SEG)
    scsb = Buf()
    sq = ar.f32(512)
    sqb = Buf()
    rs = ar.f32(512)
    rsb = Buf()
    oseg = ar.f32(SEG)
    osegb = Buf()
    yo = ar.bf16(SEG)
    yob = Buf()
    S = ar.f32(128)
    Sb = Buf()
    gb = ar.f32(24, 64)
    gbb = Buf()
    gl = ar.f32(128, 64)
    glb = Buf()
    gct = ar.f32(2, 64)
    gctb = Buf()
    d1 = ar.f32(64, 64)
    d1b = Buf()
    d2 = ar.f32(64, 64)
    d2b = Buf()
    ebc = ar.f32(64)
    ebcb = Buf()
    dl = ar.f32(1, 64)
    dlb = Buf()
    t1 = ar.f32(64, 64)
    t1b = Buf()
    nnt = [(ar.f32(128, 64), Buf()) for _ in range(2)]
    rts = [(ar.f32(64, 64), Buf()) for _ in range(2)]
    t2 = ar.f32(64, 64)
    t2b = Buf()
    attnT = ar.f32(64, 64)
    attnb = Buf()
    kg = ar.f32(64)
    kgb = Buf()
    qd = ar.f32(64)
    qdb = Buf()
    kdec = ar.f32(128, 64)
    kdecb = Buf()
    vb = ar.f32(128, 64)
    vbb = Buf()
    X = ar.f32(128, 64)
    Xb = Buf()
    vn = ar.f32(128, 64)
    vnb = Buf()
    dbuf = Buf()
    ident8 = ident[0:8, 0:8]
    scale = 128.0 ** -0.5
    for b in range(cfg.NB):
        P.op("pool", lambda e: e.memset(S, 0.0), writes=[Sb])
        for s0 in range(0, L, SEG):
            t0 = b * L + s0
            load_with_halo(P, raw3, rawb, proj[GQ:GQ + 384, :], t0, SEG, 3, s0 == 0)
            P.dma("sp", zs, proj[GZ:GZ + 128, t0:t0 + SEG], writes=[zsb])
            P.dma("sp", scs3[:, 0, :], scal[8:16, t0:t0 + SEG], writes=[scsb])
            P.dma("sp", scs3[:, 1, :], scal[16:24, t0:t0 + SEG], writes=[scsb])
            conv_silu(P, qkv3, raw3, 3, SEG, 4, pv, PV_GCW, None, rawb, qkvb)
            P.op("act", lambda e: e.activation(zs, zs, AF.Silu), reads=[zsb], writes=[zsb])
            for c in range(2):
                for u0 in range(0, SEG, 512):
                    ul = min(512, SEG - u0)
                    xs_ = qkv3[:, c, u0:u0 + ul]
                    P.op("act", lambda e: e.activation(sq[:, 0:ul], xs_, AF.Square), reads=[qkvb], writes=[sqb])
                    P.op("pe", lambda e: e.matmul(psum[7][:, 0:ul], lhsT=ones, rhs=sq[:, 0:ul], start=True, stop=True),
                         reads=[sqb, cb], writes=[psb[7]])
                    P.op("dve", lambda e: e.tensor_scalar(rs[:, 0:ul], psum[7][:, 0:ul], 1.0, 1e-6, ALU.mult, ALU.add),
                         reads=[psb[7]], writes=[rsb])
                    P.op("act", lambda e: e.activation(rs[:, 0:ul], rs[:, 0:ul], AF.Sqrt), reads=[rsb], writes=[rsb])
                    P.op("dve", lambda e: e.reciprocal(rs[:, 0:ul], rs[:, 0:ul]), reads=[rsb], writes=[rsb])
                    P.op("dve", lambda e: e.scalar_tensor_tensor(xs_, xs_, (scale if c == 0 else 1.0), rs[:, 0:ul],
                                                                 ALU.mult, ALU.mult), reads=[rsb, qkvb], writes=[qkvb])
            for ch in range(NCH):
                o = ch * 64
                osl = slice(o, o + 64)
                qc, kc, vc = qkv3[:, 0, osl], qkv3[:, 1, osl], qkv3[:, 2, osl]
                P.op("pe", lambda e: e.transpose(psum[0][0:64, 0:8], scs3[:, 0, osl], ident8),
                     reads=[scsb, cb], writes=[psb[0]], sig=False)
                P.op("pe", lambda e: e.transpose(psum[0][0:64, 8:16], scs3[:, 1, osl], ident8),
                     reads=[scsb, cb], writes=[psb[0]])
                P.op("dve", lambda e: e.tensor_copy(gb[:, 0:16], psum[0][0:64, 0:16]), reads=[psb[0]], writes=[gbb])
                P.op("dve", lambda e: e.tensor_scalar(gb[:, 16:17], gb[:, 14:15], -1.0, None, ALU.mult),
                     reads=[gbb], writes=[gbb])
                g_, beta, nbeta = gb[:, 5:6], gb[:, 14:15], gb[:, 16:17]
                P.op("act", lambda e: e.activation(gl, ones[0:64, :], AF.Copy, scale=g_), reads=[gbb, cb], writes=[glb])
                P.op("pe", lambda e: e.matmul(psum[0][0:64, 16:18], lhsT=tri[0:64, 0:64], rhs=gb[:, 4:6],
                                              start=True, stop=True), reads=[gbb, cb], writes=[psb[0]], sig=False)
                P.op("pe", lambda e: e.matmul(psum[0][:, 32:96], lhsT=gl, rhs=tri[0:64, 0:64], start=True, stop=True),
                     reads=[glb, cb], writes=[psb[0]])
                gbc = psum[0][:, 32:96]
                P.op("dve", lambda e: e.tensor_copy(gct, psum[0][0:64, 16:18]), reads=[psb[0]], writes=[gctb])
                gc = gct[:, 1:2]
                P.op("dve", lambda e: e.tensor_scalar(d1, gbc[0:64, :], gc, 0.0, ALU.subtract, ALU.max),
                     reads=[psb[0], gctb], writes=[d1b])
                P.op("act", lambda e: e.activation(d1, d1, AF.Exp, scale=-1.0), reads=[d1b], writes=[d1b])
                P.op("dve", lambda e: e.tensor_scalar(d2, gbc[0:64, :], gc, 0.0, ALU.subtract, ALU.min),
                     reads=[psb[0], gctb], writes=[d2b])
                P.op("act", lambda e: e.activation(d2, d2, AF.Exp), reads=[d2b], writes=[d2b])
                P.op("act", lambda e: e.activation(ebc, gbc, AF.Exp), reads=[psb[0]], writes=[ebcb])
                P.op("dve", lambda e: e.tensor_tensor(dl, gbc[0:64, 63:64], gc, ALU.subtract),
                     reads=[psb[0], gctb], writes=[dlb])
                P.op("act", lambda e: e.activation(dl, dl, AF.Exp), reads=[dlb], writes=[dlb])
                P.op("pe", lambda e: e.matmul(psum[1][0:64, 0:64], lhsT=kc, rhs=kc, start=True, stop=True),
                     reads=[qkvb], writes=[psb[1]], sig=False)
                P.op("pe", lambda e: e.matmul(psum[1][0:64, 64:128], lhsT=kc, rhs=qc, start=True, stop=True),
                     reads=[qkvb], writes=[psb[1]])
                P.op("dve", lambda e: e.tensor_tensor(t1, psum[1][0:64, 0:64], d1, ALU.mult),
                     reads=[psb[1], d1b], writes=[t1b])
                nn0, nn0b = nnt[0]
                P.op("dve", lambda e: e.scalar_tensor_tensor(nn0[:, 0:64], t1, nbeta, lows[0:64, 0:64], ALU.mult, ALU.mult),
                     reads=[t1b, gbb, cb], writes=[nn0b])
                P.op("dve", lambda e: e.tensor_tensor(t2, psum[1][0:64, 64:128], d2, ALU.mult),
                     reads=[psb[1], d2b], writes=[t2b])
                P.op("pool", lambda e: e.tensor_tensor(attnT, t2, tri[0:64, 0:64], ALU.mult),
                     reads=[t2b, cb], writes=[attnb])
                P.op("pe", lambda e: e.transpose(psum[1][0:64, 128:192], nn0[:, 0:64], i64),
                     reads=[nn0b, cb], writes=[psb[1]])
                P.op("act", lambda e: e.activation(nn0[:, 64:128], psum[1][0:64, 128:192], AF.Copy),
                     reads=[psb[1]], writes=[nn0b])
                rt0, rt0b = rts[0]
                P.op("dve", lambda e: e.tensor_tensor(rt0, nn0[:, 64:128], i64, ALU.add), reads=[nn0b, cb], writes=[rt0b])
                for k in range(1, 6):
                    pn, pnb = nnt[(k - 1) % 2]
                    cn, cnb = nnt[k % 2]
                    pr, prb = rts[(k - 1) % 2]
                    cr, crb = rts[k % 2]
                    last = (k == 5)
                    P.op("pe", lambda e: e.matmul(psum[2][0:64, 0:64], lhsT=pn[:, 64:128], rhs=pn[:, 0:64],
                                                  start=True, stop=True), reads=[pnb], writes=[psb[2]], sig=last)
                    if not last:
                        P.op("pe", lambda e: e.matmul(psum[2][0:64, 64:128], lhsT=pn[:, 0:64], rhs=pn[:, 64:128],
                                                      start=True, stop=True), reads=[pnb], writes=[psb[2]])
                    w_ = 64 if last else 128
                    P.op("act", lambda e: e.activation(cn[:, 0:w_], psum[2][0:64, 0:w_], AF.Copy),
                         reads=[psb[2]], writes=[cnb])
                    P.op("pe", lambda e: e.matmul(psum[3][0:64, 0:64], lhsT=cn[:, 0:64], rhs=pr, start=True, stop=True),
                         reads=[cnb, prb], writes=[psb[3]])
                    P.op("dve", lambda e: e.tensor_tensor(cr, pr, psum[3][0:64, 0:64], ALU.add),
                         reads=[psb[3], prb], writes=[crb])
                rT, rTb = rts[5 % 2]
                P.op("dve", lambda e: e.tensor_tensor(kg, kc, ebc, ALU.mult), reads=[qkvb, ebcb], writes=[kgb])
                P.op("pool", lambda e: e.tensor_tensor(qd, qc, ebc, ALU.mult), reads=[qkvb, ebcb], writes=[qdb])
                P.op("pe", lambda e: e.transpose(psum[4][0:64, 0:128], kc, ident), reads=[qkvb, cb], writes=[psb[4]], sig=False)
                P.op("pe", lambda e: e.transpose(psum[4][0:64, 128:256], vc, ident), reads=[qkvb, cb], writes=[psb[4]])
                P.op("act", lambda e: e.activation(kdec, psum[4][0:64, 0:128], AF.Copy, scale=dl),
                     reads=[psb[4], dlb], writes=[kdecb])
                P.op("act", lambda e: e.activation(vb, psum[4][0:64, 128:256], AF.Copy, scale=beta),
                     reads=[psb[4], gbb], writes=[vbb])
                P.op("pe", lambda e: e.matmul(psum[5][0:64, 0:128], lhsT=kg, rhs=S, start=True, stop=True),
                     reads=[kgb, Sb], writes=[psb[5]])
                P.op("dve", lambda e: e.scalar_tensor_tensor(X, psum[5][0:64, 0:128], nbeta, vb, ALU.mult, ALU.add),
                     reads=[psb[5], gbb, vbb], writes=[Xb])
                P.op("pe", lambda e: e.matmul(psum[5][0:64, 128:256], lhsT=rT, rhs=X, start=True, stop=True),
                     reads=[rTb, Xb], writes=[psb[5]])
                P.op("act", lambda e: e.activation(vn, psum[5][0:64, 128:256], AF.Copy), reads=[psb[5]], writes=[vnb])
                P.op("pe", lambda e: e.matmul(psum[6][:, 0:64], lhsT=S, rhs=qd, start=True, stop=False),
                     reads=[Sb, qdb], writes=[psb[6]], sig=False)
                P.op("pe", lambda e: e.matmul(psum[6][:, 0:64], lhsT=vn, rhs=attnT, start=False, stop=True),
                     reads=[vnb, attnb], writes=[psb[6]])
                P.op("act", lambda e: e.activation(oseg[:, osl], psum[6][:, 0:64], AF.Copy), reads=[psb[6]], writes=[osegb])
                P.op("pe", lambda e: e.matmul(psum[7][:, 0:128], lhsT=kdec, rhs=vn, start=True, stop=True),
                     reads=[kdecb, vnb], writes=[psb[7]])
                P.op("dve", lambda e: e.scalar_tensor_tensor(S, S, ebc[:, 63:64], psum[7][:, 0:128], ALU.mult, ALU.add),
                     reads=[psb[7], ebcb, Sb], writes=[Sb])
            for u0 in range(0, SEG, 512):
                ul = min(512, SEG - u0)
                usl = slice(u0, u0 + ul)
                P.op("act", lambda e: e.activation(sq[:, 0:ul], oseg[:, usl], AF.Square), reads=[osegb], writes=[sqb])
                P.op("pe", lambda e: e.matmul(psum[7][:, 0:ul], lhsT=ones, rhs=sq[:, 0:ul], start=True, stop=True),
                     reads=[sqb, cb], writes=[psb[7]])
                P.op("dve", lambda e: e.tensor_scalar(rs[:, 0:ul], psum[7][:, 0:ul], 1.0 / 128.0, EPS, ALU.mult, ALU.add),
                     reads=[psb[7]], writes=[rsb])
                P.op("act", lambda e: e.activation(rs[:, 0:ul], rs[:, 0:ul], AF.Sqrt), reads=[rsb], writes=[rsb])
                P.op("dve", lambda e: e.reciprocal(rs[:, 0:ul], rs[:, 0:ul]), reads=[rsb], writes=[rsb])
                P.op("dve", lambda e: e.scalar_tensor_tensor(oseg[:, usl], oseg[:, usl], pv[:, PV_GNORM:PV_GNORM + 1],
                                                             rs[:, 0:ul], ALU.mult, ALU.mult),
                     reads=[rsb, osegb], writes=[osegb])
                P.op("pool", lambda e: e.tensor_tensor(yo[:, usl], oseg[:, usl], zs[:, usl], ALU.mult),
                     reads=[osegb, zsb], writes=[yob])
            P.dma("sp", y_gdn.span(0, 128, t0, SEG), y_gdn.sview(yo), reads=[yob], writes=[dbuf])
    P.barrier()


def phase_branch(P, env, cfg, w_br, yfs, proj, m_loc, pre=None):
    st = {}
    segs = [(w_br[0:2048, :], yfs[0], 2048), (w_br[2048:3072, :], yfs[1], 1024), (w_br[3072:4096, :], yfs[2], 1024)]
    gate_rows = (GA_SSM, GA_FOX, GA_GDN)

    def setup(ar):
        st["g"] = [(ar.f32(3 * 512), Buf()) for _ in range(2)]
        st["acc"] = [(ar.f32(512), Buf()) for _ in range(2)]
        st["o"] = [(ar.bf16(512), Buf()) for _ in range(2)]
        st["i"] = 0
        st["d"] = Buf()

    def epi(n0, chunks, t0, tl):
        for j in range(len(chunks[0])):
            i = st["i"]
            st["i"] += 1
            r0, rows, _, _ = chunks[0][j]
            g, gb = st["g"][i % 2]
            g3 = g.rearrange("p (s t) -> p s t", t=512)
            for s in range(3):
                P.dma("sp", g3[:, s, 0:tl], proj[gate_rows[s] + r0:gate_rows[s] + r0 + 128, t0:t0 + tl], writes=[gb])
            P.op("act", lambda e: e.activation(g3[:, :, 0:tl], g3[:, :, 0:tl], AF.Sigmoid), reads=[gb], writes=[gb])
            acc, accb = st["acc"][i % 2]
            P.op("dve", lambda e: e.tensor_tensor(acc[:, 0:tl], chunks[0][j][3], g3[:, 0, 0:tl], ALU.mult),
                 reads=[chunks[0][j][2], gb], writes=[accb])
            P.op("dve", lambda e: e.tensor_tensor(g3[:, 1, 0:tl], chunks[1][j][3], g3[:, 1, 0:tl], ALU.mult),
                 reads=[chunks[1][j][2], gb], writes=[gb])
            P.op("dve", lambda e: e.tensor_tensor(g3[:, 2, 0:tl], chunks[2][j][3], g3[:, 2, 0:tl], ALU.mult),
                 reads=[chunks[2][j][2], gb], writes=[gb])
            P.op("dve", lambda e: e.tensor_tensor(acc[:, 0:tl], acc[:, 0:tl], g3[:, 1, 0:tl], ALU.add),
                 reads=[gb, accb], writes=[accb])
            o, ob = st["o"][i % 2]
            P.op("dve", lambda e: e.tensor_tensor(o[:, 0:tl], acc[:, 0:tl], g3[:, 2, 0:tl], ALU.add),
                 reads=[gb, accb], writes=[ob])
            P.dma("sp", m_loc.span(r0, 128, t0, tl), m_loc.sview(o[:, 0:tl]), reads=[ob], writes=[st["d"]])
    gemm(P, env, segs, 512, cfg.T, epi, setup=setup, NPW=256, pre=pre)
    P.barrier()


def phase_resid(P, env, cfg, W, K, A, xsrc, xdst, pre=None):
    st = {}

    def setup(ar):
        st["x"] = [(ar.f32(512), Buf()) for _ in range(4)]
        st["i"] = 0
        st["d"] = Buf()

    def epi(n0, chunks, t0, tl):
        for (r0, rows, pb, pap) in chunks[0]:
            i = st["i"]
            st["i"] += 1
            x, xb = st["x"][i % 4]
            P.dma("sp", x[:, 0:tl], xsrc[r0:r0 + 128, t0:t0 + tl], writes=[xb])
            P.op("dve", lambda e: e.tensor_tensor(x[:, 0:tl], x[:, 0:tl], pap, ALU.add), reads=[pb, xb], writes=[xb])
            P.dma("sp", xdst[r0:r0 + 128, t0:t0 + tl], x[:, 0:tl], reads=[xb], writes=[st["d"]])
    gemm(P, env, [(W, A, K)], 512, cfg.T, epi, setup=setup, pre=pre)
    P.barrier()


def phase_ffn_up(P, env, cfg, W, hfull, pv, act_loc, pre=None):
    st = {}
    L = cfg.SEQ

    def setup(ar):
        st["u"] = [(ar.f32(514), Buf()) for _ in range(4)]
        st["carry"] = [(ar.f32(2), Buf()) for _ in range(4)]
        st["y"] = [(ar.f32(512), Buf()) for _ in range(4)]
        st["o"] = [(ar.bf16(512), Buf()) for _ in range(2)]
        st["d"] = Buf()
        st["oi"] = 0

    def epi(n0, chunks, t0, tl):
        cs = chunks[0]
        nc_ = len(cs)
        half = nc_ // 2
        p = n0 // 512
        ys = []
        for j, (r0, rows, pb, pap) in enumerate(cs):
            is_val = j >= half
            cidx = 2 * p + (j - half if is_val else j)
            pidx = (11 if is_val else 0) + cidx
            u, ub = st["u"][j]
            cr, crb = st["carry"][j]
            if t0 % L == 0:
                P.op("dve", lambda e: e.memset(u[:, 0:2], 0.0), writes=[ub])
            else:
                P.op("dve", lambda e: e.tensor_copy(u[:, 0:2], cr), reads=[crb], writes=[ub])
            P.op("act", lambda e: e.activation(u[:, 2:2 + tl], pap, AF.Copy), reads=[pb], writes=[ub])
            P.op("act", lambda e: e.activation(cr, u[:, tl:tl + 2], AF.Copy), reads=[ub], writes=[crb])
            y, yb = st["y"][j]
            w = lambda k: pv[:, PV_FCW + pidx * 3 + k:PV_FCW + pidx * 3 + k + 1]
            P.op("dve", lambda e: e.tensor_scalar(y[:, 0:tl], u[:, 2:2 + tl], w(2), pv[:, PV_FCB + pidx:PV_FCB + pidx + 1],
                                                  ALU.mult, ALU.add), reads=[ub], writes=[yb])
            P.op("dve", lambda e: e.scalar_tensor_tensor(y[:, 0:tl], u[:, 1:1 + tl], w(1), y[:, 0:tl], ALU.mult, ALU.add),
                 reads=[ub, yb], writes=[yb])
            P.op("dve", lambda e: e.scalar_tensor_tensor(y[:, 0:tl], u[:, 0:tl], w(0), y[:, 0:tl], ALU.mult, ALU.add),
                 reads=[ub, yb], writes=[yb])
            ys.append((y, yb, cidx))
        for j in range(half):
            gy, gyb, cidx = ys[j]
            vy, vyb, _ = ys[half + j]
            P.op("act", lambda e: e.activation(gy[:, 0:tl], gy[:, 0:tl], AF.Silu), reads=[gyb], writes=[gyb])
            o, ob = st["o"][st["oi"] % 2]
            st["oi"] += 1
            P.op("dve", lambda e: e.tensor_tensor(o[:, 0:tl], gy[:, 0:tl], vy[:, 0:tl], ALU.mult),
                 reads=[gyb, vyb], writes=[ob])
            rows = min(128, FFL - cidx * 128)
            P.dma("sp", act_loc.span(cidx * 128, rows, t0, tl), act_loc.sview(o[0:rows, 0:tl]), reads=[ob], writes=[st["d"]])
    gemm(P, env, [(W, hfull, D)], NUPW, cfg.T, epi, setup=setup, pre=pre)
    P.barrier()


def build(cfg):
    nc = bass.Bass("TRN2", target_bir_lowering=False)
    _DB.clear()
    T, DEP = cfg.T, cfg.DEPTH

    def ext(name, shape, dt):
        return nc.dram_tensor(name, shape, dt, kind="ExternalInput")
    xT = ext("xT", [512, T], F32)
    w_in = ext("w_in", [DEP * D, NPROJ], F32)
    lt = cfg.lite
    w_br = ext("w_br", [128 if lt else DEP * D, 512], F32)
    w_out = ext("w_out", [128 if lt else DEP * D, 512], F32)
    w_up = ext("w_up", [128 if lt else DEP * D, NUPW], F32)
    w_down = ext("w_down", [128 if lt else DEP * DFF, 512], F32)
    pvs = ext("pvs", [128, DEP * NPV + 4], F32)
    outT = nc.dram_tensor("outT", [512, T], F32, kind="ExternalOutput")
    scr = {}

    def sc(name, shape, dt):
        scr[name] = nc.dram_tensor(name, shape, dt)
        return scr[name]
    proj = sc("proj", [NPROJ_PAD, T], F32)
    scal = sc("scal", [32, T], F32)
    ssq_loc = sc("ssq_loc", [1, T], F32)
    ssq_mid = sc("ssq_mid", [1, T], F32)
    ssq_tot = sc("ssq_tot", [1, T], F32)
    ssq2_loc = sc("ssq2_loc", [1, T], F32)
    ssq2_tot = sc("ssq2_tot", [1, T], F32)
    ypre = sc("ypre", [256, T], F32)
    xresA = sc("xresA", [512, T], F32)
    xresB = sc("xresB", [512, T], F32)
    cts = {}

    def ct3(name, R):
        tcn = tc_for(R)
        loc = CT(nc, name + "_loc", R, T, BF16, tcn)
        mid = CT(nc, name + "_mid", 2 * R, T, BF16, tcn)
        full = CT(nc, name + "_full", 8 * R, T, BF16, tcn)
        for x in (loc, mid, full):
            cts[x.name] = x
            scr[x.name] = x.t
        return loc, mid, full
    h_loc, h_mid, hfull = ct3("h", 512)
    y_ssd, ym_ssd, yf_ssd = ct3("y_ssd", 256)
    y_fox, ym_fox, yf_fox = ct3("y_fox", 128)
    y_gdn, ym_gdn, yf_gdn = ct3("y_gdn", 128)
    m_loc, m_mid, mfull = ct3("m", 512)
    act_loc, act_mid, actfull = ct3("act", FFL)
    taps = {n: nc.dram_tensor("tap_" + n, list(scr[n].shape), scr[n].dtype, kind="ExternalOutput") for n in cfg.taps}

    class Stop(Exception):
        pass

    with contextlib.ExitStack() as st:
        P = Prog(nc)
        env = make_env(nc, st)
        C = make_consts(nc, st, P, env)
        pvt = st.enter_context(nc.sbuf_tensor("pvsb", [128, DEP * NPV + 4], F32))
        pvb = Buf()
        P.dma("sp", pvt[:, :], pvs[:, :], writes=[pvb])
        P.barrier()

        def done(name):
            if cfg.stop_after == name:
                raise Stop()
        try:
            xcur = xT
            for l in range(DEP):
                pv = pvt[:, l * NPV:(l + 1) * NPV]
                phase_norm(P, env, C, cfg, xcur, pv[:, PV_NMIX:PV_NMIX + 4], ssq_loc, ssq_mid, ssq_tot, h_loc, h_mid, hfull)
                done("norm1")
                setup, epi = copy_epilogue(P, proj)
                gemm(P, env, [(w_in[l * D:(l + 1) * D, :], hfull, D)], NPROJ, T, epi, setup=setup,
                     pre=lambda: allgather8(P, (h_loc, h_mid, hfull)))
                P.barrier()
                done("inproj")
                phase_scal(P, env, C, cfg, proj, pv, scal)
                done("scal")
                phase_fox(P, env, C, cfg, proj, scal, y_fox)
                done("fox")
                phase_ssd(P, env, C, cfg, proj, scal, pv, ypre, ssq2_loc, ssq2_tot, y_ssd)
                done("ssd")
                phase_gdn(P, env, C, cfg, proj, scal, pv, y_gdn)
                done("gdn")
                def ag_y():
                    allgather8(P, (y_ssd, ym_ssd, yf_ssd), (y_fox, ym_fox, yf_fox), (y_gdn, ym_gdn, yf_gdn))
                phase_branch(P, env, cfg, w_br[l * D:(l + 1) * D, :], (yf_ssd, yf_fox, yf_gdn), proj, m_loc, pre=ag_y)
                done("branch")
                phase_resid(P, env, cfg, w_out[l * D:(l + 1) * D, :], D, mfull, xcur, xresA,
                            pre=lambda: allgather8(P, (m_loc, m_mid, mfull)))
                xcur = xresA
                done("wout")
                phase_norm(P, env, C, cfg, xcur, pv[:, PV_NFFN:PV_NFFN + 4], ssq_loc, ssq_mid, ssq_tot, h_loc, h_mid, hfull)
                phase_ffn_up(P, env, cfg, w_up[l * D:(l + 1) * D, :], hfull, pv, act_loc,
                             pre=lambda: allgather8(P, (h_loc, h_mid, hfull)))
                done("ffnup")
                phase_resid(P, env, cfg, w_down[l * DFF:(l + 1) * DFF, :], DFF, actfull, xcur, xresB,
                            pre=lambda: allgather8(P, (act_loc, act_mid, actfull)))
                xcur = xresB
                done("layer%d" % l)
            phase_norm(P, env, C, cfg, xcur, pvt[:, DEP * NPV:DEP * NPV + 4], ssq_loc, ssq_mid, ssq_tot, out_f32=outT)
        except Stop:
            pass
        P.barrier(cc=True)
        for n, tp in taps.items():
            P.dma("sp", tp.ap(), scr[n].ap())
        P.barrier(cc=True)
        blk = st.enter_context(nc.Block())
        P.replay(blk)
    return nc


_SPLITS = [0, 2048, 5120, 5152, 8224, 8232, 11304, 11312, 11320, 12344, 16440, 20536]


def _in_cols(c):
    o_z, o_xbc, o_dt, o_fqkv, o_ff, o_gqkv, o_ga, o_gb, o_gz, o_gs, o_gf, o_gg = _SPLITS
    g = c // 2
    r = np.arange
    cols = [o_z + c * 256 + r(256), o_xbc + c * 256 + r(256), o_xbc + 2048 + g * 128 + r(128),
            o_xbc + 2560 + g * 128 + r(128),
            o_fqkv + c * 128 + r(128), o_fqkv + 1024 + c * 128 + r(128), o_fqkv + 2048 + c * 128 + r(128),
            o_gqkv + c * 128 + r(128), o_gqkv + 1024 + c * 128 + r(128), o_gqkv + 2048 + c * 128 + r(128),
            o_gz + c * 128 + r(128),
            o_gs + c * 512 + r(512), o_gf + c * 512 + r(512), o_gg + c * 512 + r(512),
            o_dt + 4 * c + r(4), np.array([o_ff + c, o_ga + c, o_gb + c, o_gb + c])]
    cols = np.concatenate(cols)
    assert cols.shape[0] == NPROJ
    return cols


def _pcol(v):
    n = cdiv(v.shape[0], 128)
    out = np.zeros((n * 128,), np.float32)
    out[:v.shape[0]] = v
    return out.reshape(n, 128).T


def _rperm(nrows):
    rb = nrows // NCORES
    return np.concatenate([np.arange(r * rb, (r + 1) * rb) for r in RANK_ORDER])


def _layout(inp, c, DEP):
    f32 = np.float32
    g = c // 2
    pD = _rperm(D)
    w_in_c = np.concatenate([inp["w_in"][l][:, _in_cols(c)][pD] for l in range(DEP)], 0)
    cs = slice(c * 512, (c + 1) * 512)
    w_br_c = np.concatenate([np.concatenate([inp["w_br_ssm"][l][:, cs][_rperm(2048)], inp["w_br_fox"][l][:, cs][_rperm(1024)],
                                             inp["w_br_gdn"][l][:, cs][_rperm(1024)]], 0) for l in range(DEP)], 0)
    w_out_c = np.concatenate([inp["w_out"][l][:, c * 512:(c + 1) * 512][pD] for l in range(DEP)], 0)
    w_down_c = np.concatenate([inp["w_down"][l][:, c * 512:(c + 1) * 512][_rperm(DFF)] for l in range(DEP)], 0)
    ups = []
    for l in range(DEP):
        wu = inp["w_up"][l]
        blk = np.zeros((D, NUPW), f32)
        col = 0
        for p in range(6):
            chunks = [2 * p, 2 * p + 1] if p < 5 else [10]
            for base in (0, DFF):
                for ci in chunks:
                    rows = min(128, FFL - ci * 128)
                    src0 = base + c * FFL + ci * 128
                    blk[:, col:col + rows] = wu[:, src0:src0 + rows]
                    col += 128
        assert col == NUPW
        ups.append(blk[pD])
    w_up_c = np.concatenate(ups, 0)
    pv = np.zeros((128, DEP * NPV + 4), f32)
    xbc_idx = np.concatenate([c * 256 + np.arange(256), 2048 + g * 128 + np.arange(128), 2560 + g * 128 + np.arange(128)])
    gq_idx = np.concatenate([c * 128 + np.arange(128), 1024 + c * 128 + np.arange(128), 2048 + c * 128 + np.arange(128)])
    for l in range(DEP):
        o = l * NPV
        pv[:, o + PV_NMIX:o + PV_NMIX + 4] = _pcol(inp["norm_mix"][l][c * 512:(c + 1) * 512])
        scw = inp["ssm_conv_w"][l][:, xbc_idx]
        for ch in range(4):
            for j in range(4):
                pv[:, o + PV_SCW + ch * 4 + j] = scw[j, ch * 128:(ch + 1) * 128]
        pv[:, o + PV_SCB:o + PV_SCB + 4] = _pcol(inp["ssm_conv_b"][l][xbc_idx])
        pv[:, o + PV_SNORM:o + PV_SNORM + 2] = _pcol(inp["ssm_norm"][l][c * 256:(c + 1) * 256])
        pv[:, o + PV_SD:o + PV_SD + 2] = _pcol(np.repeat(inp["ssm_d"][l][4 * c:4 * c + 4], 64))
        gcw = inp["gdn_conv_w"][l][:, gq_idx]
        for ch in range(3):
            for j in range(4):
                pv[:, o + PV_GCW + ch * 4 + j] = gcw[j, ch * 128:(ch + 1) * 128]
        pv[:, o + PV_GNORM] = inp["gdn_norm"][l]
        pv[:, o + PV_NFFN:o + PV_NFFN + 4] = _pcol(inp["norm_ffn"][l][c * 512:(c + 1) * 512])
        for base, poff in ((0, 0), (DFF, 11)):
            fcw = inp["ffn_conv_w"][l][:, base + c * FFL:base + (c + 1) * FFL]
            fcb = inp["ffn_conv_b"][l][base + c * FFL:base + (c + 1) * FFL]
            for j in range(3):
                cw = _pcol(fcw[j])
                for ci in range(11):
                    pv[:, o + PV_FCW + (poff + ci) * 3 + j] = cw[:, ci]
            pv[:, o + PV_FCB + poff:o + PV_FCB + poff + 11] = _pcol(fcb)
        pv[0:4, o + PV_SBIAS] = inp["ssm_dt_bias"][l][4 * c:4 * c + 4]
        pv[4, o + PV_SBIAS] = inp["fox_f_bias"][l][c]
        pv[5, o + PV_SBIAS] = inp["gdn_dt_bias"][l][c]
        pv[0:4, o + PV_SALOG] = inp["ssm_a_log"][l][4 * c:4 * c + 4]
        pv[5, o + PV_SALOG] = inp["gdn_a_log"][l][c]
        pv[:, o + PV_SSGN] = 1.0
        pv[4, o + PV_SSGN] = -1.0
    pv[:, DEP * NPV:DEP * NPV + 4] = _pcol(inp["norm_final"][c * 512:(c + 1) * 512])
    return {"w_in": np.ascontiguousarray(w_in_c), "w_br": np.ascontiguousarray(w_br_c),
            "w_out": np.ascontiguousarray(w_out_c), "w_up": w_up_c, "w_down": np.ascontiguousarray(w_down_c),
            "pvs": pv}


def run(cfg, inputs, trace=False):
    inp = {k: np.asarray(v) for k, v in inputs.items()}
    x = inp["x"].reshape(cfg.T, D)
    in_maps = []
    for c in range(NCORES):
        m = _layout(inp, c, cfg.DEPTH)
        m["xT"] = np.ascontiguousarray(x[:, c * 512:(c + 1) * 512].T)
        if cfg.lite:
            for k in ("w_br", "w_out", "w_up", "w_down"):
                m[k] = np.ascontiguousarray(m[k][0:128])
        in_maps.append(m)
    nc = build(cfg)
    res = run_bass_kernel_spmd(nc, in_maps, core_ids=list(range(NCORES)), **({"trace": True} if trace else {}))
    return res


def kernel(**inputs):
    x = np.asarray(inputs["x"])
    NB, SEQ, _ = x.shape
    cfg = Cfg(NB=NB, SEQ=SEQ, DEPTH=np.asarray(inputs["w_in"]).shape[0])
    res = run(cfg, inputs)
    outT = np.concatenate([r["outT"] for r in res.results], axis=0)
    return np.ascontiguousarray(outT.T).reshape(NB, SEQ, D).astype(np.float32)
```
